# Optimizing a Trainium2 kernel written in Bass

```python
import math
import jax, jax.numpy as jnp
from jax import lax
import numpy as np

D_MODEL = 1024
BATCH = 2
SEQ = 16384
DEPTH = 1

N_MEM = 256
D_FF = 2816
D_CONV = D_MODEL
CONV_WIDTH = 31
D_SGU = D_MODEL
SGU_GROUPS = 4
CHUNK = 128
X_HEADS = 4
X_HEAD_DIM = D_MODEL // X_HEADS
D_IN = 2 * D_CONV + 2 * D_SGU + 2 * D_MODEL
EPS_RMS = 1e-6
EPS_LN = 1e-5

kernel_name = "hybrid_conformer_gmlp_memxattn_block"


def rms_norm(x, g):
    xf = x.astype(jnp.float32)
    y = xf * lax.rsqrt(jnp.mean(xf * xf, axis=-1, keepdims=True) + EPS_RMS)
    return (y * g.astype(jnp.float32)).astype(x.dtype)


def layer_norm(x, g, b):
    xf = x.astype(jnp.float32)
    mu = jnp.mean(xf, axis=-1, keepdims=True)
    xc = xf - mu
    var = jnp.mean(xc * xc, axis=-1, keepdims=True)
    y = xc * lax.rsqrt(var + EPS_LN)
    return (y * g.astype(jnp.float32) + b.astype(jnp.float32)).astype(x.dtype)


def swiglu(x, w_gu, w_down):
    gu = x @ w_gu
    g, u = jnp.split(gu, 2, axis=-1)
    return (jax.nn.silu(g) * u) @ w_down


def causal_depthwise_conv(a, w, b):
    c = a.shape[-1]
    y = lax.conv_general_dilated(
        a, w.astype(a.dtype)[:, None, :],
        window_strides=(1,), padding=[(CONV_WIDTH - 1, 0)],
        dimension_numbers=("NWC", "WIO", "NWC"),
        feature_group_count=c)
    return y + b.astype(a.dtype)


def conformer_conv_branch(a_val, a_gate, conv_w, conv_b, ln_g, ln_b, w_a_out):
    a = a_val * jax.nn.sigmoid(a_gate)
    a = causal_depthwise_conv(a, conv_w, conv_b)
    a = jax.nn.silu(layer_norm(a, ln_g, ln_b))
    return a @ w_a_out


def spatial_gating_branch(u, v, ln_g, ln_b, sgu_w, sgu_b, w_b_out):
    bsz, s, _ = u.shape
    u = jax.nn.gelu(u)
    v = layer_norm(jax.nn.gelu(v), ln_g, ln_b)
    n_chunks = s // CHUNK
    gd = D_SGU // SGU_GROUPS
    vc = v.reshape(bsz, n_chunks, CHUNK, SGU_GROUPS, gd)
    mask = jnp.tril(jnp.ones((CHUNK, CHUNK), dtype=bool))
    w_s = jnp.where(mask[None], sgu_w, 0.0).astype(v.dtype)
    mixed = jnp.einsum("gts,bcsgd->bctgd", w_s, vc)
    mixed = mixed + jnp.transpose(sgu_b)[None, None, :, :, None].astype(v.dtype)
    out = u * mixed.reshape(bsz, s, D_SGU)
    return out @ w_b_out


def memory_cross_attention(xn, memn, w_q, w_kv, w_o):
    bsz, s, _ = xn.shape
    q = (xn @ w_q).reshape(bsz, s, X_HEADS, X_HEAD_DIM)
    kv = memn @ w_kv
    k, v = jnp.split(kv, 2, axis=-1)
    k = k.reshape(bsz, N_MEM, X_HEADS, X_HEAD_DIM)
    v = v.reshape(bsz, N_MEM, X_HEADS, X_HEAD_DIM)
    scores = jnp.einsum("bshd,bmhd->bhsm", q.astype(jnp.float32), k.astype(jnp.float32))
    p = jax.nn.softmax(scores * (1.0 / math.sqrt(X_HEAD_DIM)), axis=-1).astype(v.dtype)
    o = jnp.einsum("bhsm,bmhd->bshd", p, v).reshape(bsz, s, D_MODEL)
    return o @ w_o


def setup_inputs(seed: int = 0) -> dict:
    key = jax.random.key(seed)
    ks = jax.random.split(key, 32)

    def dense(k, shape, fan_in):
        return jax.random.normal(k, shape, jnp.float32) * (fan_in ** -0.5)

    def gain(k, shape):
        return 1.0 + 0.02 * jax.random.normal(k, shape, jnp.float32)

    def small(k, shape):
        return 0.02 * jax.random.normal(k, shape, jnp.float32)

    L = DEPTH
    return {
        "x": jax.random.normal(ks[0], (BATCH, SEQ, D_MODEL), jnp.float32),
        "mem": jax.random.normal(ks[1], (BATCH, N_MEM, D_MODEL), jnp.float32),
        "ffn1_norm": gain(ks[2], (L, D_MODEL)),
        "ffn1_w_gu": dense(ks[3], (L, D_MODEL, 2 * D_FF), D_MODEL),
        "ffn1_w_down": dense(ks[4], (L, D_FF, D_MODEL), D_FF),
        "mix_norm": gain(ks[5], (L, D_MODEL)),
        "w_in": dense(ks[6], (L, D_MODEL, D_IN), D_MODEL),
        "b_in": small(ks[7], (L, D_IN)),
        "conv_w": dense(ks[8], (L, CONV_WIDTH, D_CONV), CONV_WIDTH),
        "conv_b": small(ks[9], (L, D_CONV)),
        "conv_ln_g": gain(ks[10], (L, D_CONV)),
        "conv_ln_b": small(ks[11], (L, D_CONV)),
        "w_a_out": dense(ks[12], (L, D_CONV, D_MODEL), D_CONV),
        "sgu_ln_g": gain(ks[13], (L, D_SGU)),
        "sgu_ln_b": small(ks[14], (L, D_SGU)),
        "sgu_w": dense(ks[15], (L, SGU_GROUPS, CHUNK, CHUNK), CHUNK),
        "sgu_b": gain(ks[16], (L, SGU_GROUPS, CHUNK)),
        "w_b_out": dense(ks[17], (L, D_SGU, D_MODEL), D_SGU),
        "w_out": dense(ks[18], (L, D_MODEL, D_MODEL), D_MODEL),
        "xattn_norm": gain(ks[19], (L, D_MODEL)),
        "mem_norm": gain(ks[20], (L, D_MODEL)),
        "w_q": dense(ks[21], (L, D_MODEL, D_MODEL), D_MODEL),
        "w_kv": dense(ks[22], (L, D_MODEL, 2 * D_MODEL), D_MODEL),
        "w_o": dense(ks[23], (L, D_MODEL, D_MODEL), D_MODEL),
        "ffn2_norm": gain(ks[24], (L, D_MODEL)),
        "ffn2_w_gu": dense(ks[25], (L, D_MODEL, 2 * D_FF), D_MODEL),
        "ffn2_w_down": dense(ks[26], (L, D_FF, D_MODEL), D_FF),
        "final_norm": gain(ks[27], (D_MODEL,)),
    }


def reference(x, mem, ffn1_norm, ffn1_w_gu, ffn1_w_down, mix_norm, w_in, b_in,
              conv_w, conv_b, conv_ln_g, conv_ln_b, w_a_out,
              sgu_ln_g, sgu_ln_b, sgu_w, sgu_b, w_b_out, w_out,
              xattn_norm, mem_norm, w_q, w_kv, w_o,
              ffn2_norm, ffn2_w_gu, ffn2_w_down, final_norm):
    split_at = [D_CONV, 2 * D_CONV, 2 * D_CONV + D_SGU, 2 * D_CONV + 2 * D_SGU,
                2 * D_CONV + 2 * D_SGU + D_MODEL]
    h = x
    for l in range(DEPTH):
        h = h + 0.5 * swiglu(rms_norm(h, ffn1_norm[l]), ffn1_w_gu[l], ffn1_w_down[l])

        n = rms_norm(h, mix_norm[l])
        p = n @ w_in[l] + b_in[l]
        a_val, a_gate, b_u, b_v, g_a, g_b = jnp.split(p, split_at, axis=-1)
        y_a = conformer_conv_branch(a_val, a_gate, conv_w[l], conv_b[l],
                                    conv_ln_g[l], conv_ln_b[l], w_a_out[l])
        y_b = spatial_gating_branch(b_u, b_v, sgu_ln_g[l], sgu_ln_b[l],
                                    sgu_w[l], sgu_b[l], w_b_out[l])
        merged = jax.nn.sigmoid(g_a) * y_a + jax.nn.sigmoid(g_b) * y_b
        h = h + merged @ w_out[l]

        h = h + memory_cross_attention(rms_norm(h, xattn_norm[l]), rms_norm(mem, mem_norm[l]),
                                       w_q[l], w_kv[l], w_o[l])

        h = h + 0.5 * swiglu(rms_norm(h, ffn2_norm[l]), ffn2_w_gu[l], ffn2_w_down[l])
    return rms_norm(h, final_norm)
```

```python
import contextlib
import os
import numpy as np
import concourse.bass as bass
import concourse.mybir as mybir
from concourse.bass_utils import run_bass_kernel_spmd

F32 = mybir.dt.float32
BF16 = mybir.dt.bfloat16
AF = mybir.ActivationFunctionType
ALU = mybir.AluOpType

P = 128
D = 1024
DC = 8
DFF = 2816
FC = 22
TT = 512
NMEM = 256
HALO = 32
CW = 31
NCORES = 8
SEQ = 16384
NSLOT = 6
SLOT_ELEMS = 4096
EPS_RMS = 1e-6
EPS_LN = 1e-5

V_G_FFN1, V_G_MIX, V_G_XAT, V_G_MEM, V_G_FFN2, V_G_FIN = 0, 8, 16, 24, 32, 40
V_BIN = 48
V_CONVB, V_CLNG, V_CLNB = 96, 104, 112
V_CONVW = 120
V_SLNG = 120 + CW * 8
V_SLNB = V_SLNG + 8
NV = V_SLNB + 8


class Buf:
    __slots__ = ("w", "r", "const")

    def __init__(self, const=False):
        self.w = None
        self.r = []
        self.const = const


class Chan:
    def __init__(self, sem):
        self.sem = sem
        self.count = 0


class Sched:
    ENG = ("pe", "act", "dve", "pool", "sp")

    def __init__(self):
        self.q = {e: [] for e in self.ENG}
        self.label = ""
        self.labels = {e: [] for e in self.ENG}

    def add(self, eng, fn, reads=(), writes=(), chan=None):
        raw, other = [], []
        for b in reads:
            if b.w is not None:
                raw.append(b.w)
        for b in writes:
            if b.w is not None:
                other.append(b.w)
            other.extend(b.r)
        idx = len(self.q[eng])
        if chan is not None:
            chan.count += 16
            ev = ("d", chan, chan.count)
        else:
            ev = ("e", eng, idx)
        self.q[eng].append((fn, raw, other, chan))
        self.labels[eng].append(self.label)
        for b in reads:
            if not b.const:
                b.r.append(ev)
        for b in writes:
            b.w = ev
            b.r = []
        return ev

    def emit(self, nc, block, esem):
        waits = {e: [] for e in self.ENG}
        flagged = {e: set() for e in self.ENG}
        for E in self.ENG:
            seen_e, seen_d = {}, {}
            for idx, (fn, raw, other, chan) in enumerate(self.q[E]):
                need_e, need_d = {}, {}
                for deps, is_raw in ((raw, True), (other, False)):
                    for d in deps:
                        if d[0] == "e":
                            Pn, j = d[1], d[2]
                            if Pn == E and E == "pe":
                                continue
                            if j > need_e.get(Pn, -1):
                                need_e[Pn] = j
                        else:
                            ch, v = d[1], d[2]
                            if v > need_d.get(ch, 0):
                                need_d[ch] = v
                w = []
                for Pn, j in need_e.items():
                    if seen_e.get(Pn, -1) >= j:
                        continue
                    seen_e[Pn] = j
                    flagged[Pn].add(j)
                    w.append(("e", Pn, j))
                for ch, v in need_d.items():
                    if seen_d.get(ch, 0) >= v:
                        continue
                    seen_d[ch] = v
                    w.append(("d", ch, v))
                waits[E].append(w)
        count_at = {}
        for E in self.ENG:
            c = 0
            m = {}
            for idx in range(len(self.q[E])):
                if idx in flagged[E]:
                    c += 1
                    m[idx] = c
            count_at[E] = m

        def run(E, eng):
            for idx, (fn, raw, other, chan) in enumerate(self.q[E]):
                for w in waits[E][idx]:
                    if w[0] == "e":
                        eng.wait_ge(esem[w[1]], count_at[w[1]][w[2]])
                    else:
                        eng.wait_ge(w[1].sem, w[2])
                ins = fn(eng)
                if chan is not None:
                    ins.then_inc(chan.sem, 16)
                elif idx in flagged[E]:
                    ins.then_inc(esem[E], 1)

        @block.sync
        def _(e):
            run("sp", e)

        @block.tensor
        def _(e):
            run("pe", e)

        @block.scalar
        def _(e):
            run("act", e)

        @block.vector
        def _(e):
            run("dve", e)

        @block.gpsimd
        def _(e):
            run("pool", e)


def build_program(NT, stages="all", do_halo=True):
    nc = bass.Bass("TRN2", target_bir_lowering=False)
    S = Sched()
    es = contextlib.ExitStack()

    def dram_in(name, shape, dt=F32):
        return nc.dram_tensor(name, list(shape), dt, kind="ExternalInput").ap()

    NTOK = NT * TT
    x_d = dram_in("x", [NTOK, D])
    xh_d = dram_in("xh", [HALO, D])
    mem_d = dram_in("mem", [NMEM, D])
    vecs_d = dram_in("vecs", [P, NV])
    rows_d = dram_in("rows", [1, 1536])
    bc_d = dram_in("bc", [P, 512])
    hmask_d = dram_in("hmask", [P, 1])
    ident_d = dram_in("ident", [P, P])
    tril_d = dram_in("tril", [P, P])
    sguw_d = dram_in("sgu_w", [4, P, P])
    w_d = {
        "gu1": dram_in("ffn1_w_gu", [D, 2 * DFF]),
        "dn1": dram_in("ffn1_w_down", [DFF, D]),
        "win": dram_in("w_in", [D, 6 * D]),
        "wa": dram_in("w_a_out", [D, D]),
        "wb": dram_in("w_b_out", [D, D]),
        "wout": dram_in("w_out", [D, D]),
        "wq": dram_in("w_q", [D, D]),
        "wkv": dram_in("w_kv", [D, 2 * D]),
        "wo": dram_in("w_o", [D, D]),
        "gu2": dram_in("ffn2_w_gu", [D, 2 * DFF]),
        "dn2": dram_in("ffn2_w_down", [DFF, D]),
    }
    y_d = nc.dram_tensor("y", [NTOK, D], F32, kind="ExternalOutput").ap()

    def sb(name, shape, dt):
        return es.enter_context(nc.sbuf_tensor("sb_" + name, list(shape), dt))

    def sem(name):
        return es.enter_context(nc.semaphore(name))

    esem = {e: sem("prog_" + e) for e in ("pe", "act", "dve", "pool")}

    slice_ids = {}
    slice_list = []

    def slice_id(desc):
        if desc not in slice_ids:
            slice_ids[desc] = len(slice_list)
            slice_list.append(desc)
        return slice_ids[desc]

    def ffn_slices(tag):
        out = []
        for jj in range(11):
            out.append(("gu", "gu" + tag, jj))
        for c in range(DC):
            out.append(("down", "dn" + tag, c))
        return out

    def cols_slices(key, lo, n):
        return [("cols", key, lo + i * 512, 512) for i in range(n)]

    mixer_slices = (
        [("cols", "win", 0, 512), ("cols", "win", 1024, 512),
         ("cols", "win", 512, 512), ("cols", "win", 1536, 512)]
        + cols_slices("win", 3072, 2)
        + cols_slices("win", 2048, 2)
        + [("cols", "wb", 0, 512), ("cols", "win", 5120, 512),
           ("cols", "wb", 512, 512), ("cols", "win", 5632, 512)]
        + [("cols", "wa", 0, 512), ("cols", "win", 4096, 512),
           ("cols", "wa", 512, 512), ("cols", "win", 4608, 512)]
        + cols_slices("wout", 0, 2)
    )
    xattn_slices = cols_slices("wq", 0, 2) + cols_slices("wo", 0, 2)
    kv_slices = cols_slices("wkv", 0, 4)
    halo_slices = ffn_slices("1") + mixer_slices[:4]

    tile_slices = ffn_slices("1")
    if stages in ("xpose", "norm"):
        tile_slices = []
    if stages in ("all", "mixer", "xattn"):
        tile_slices = tile_slices + mixer_slices
    if stages in ("all", "xattn"):
        tile_slices = tile_slices + xattn_slices
    if stages == "all":
        tile_slices = tile_slices + ffn_slices("2")

    plan = []
    if stages in ("all", "xattn"):
        plan += kv_slices
    for _ in range(NT):
        plan += tile_slices
    for d_ in plan:
        slice_id(d_)
    NSL = len(slice_list)

    scratch = nc.dram_tensor("wscratch", [max(NSL, 1), P, SLOT_ELEMS], BF16, kind="Internal").ap()
    scratch_buf = [Buf() for _ in range(NSL)]
    in_scratch = [False] * NSL
    n_uses = [0] * NSL
    for d_ in plan:
        n_uses[slice_ids[d_]] += 1
    slots_t = sb("wslots", [P, NSLOT, SLOT_ELEMS], BF16)
    slot_buf = [Buf() for _ in range(NSLOT)]
    slot_chan = [Chan(sem("wslot%d" % i)) for i in range(NSLOT)]
    store_chan = [Chan(sem("wstore%d" % i)) for i in range(NSLOT)]
    cast_chan = [Chan(sem("wcast%d" % i)) for i in range(NSLOT)]

    def slice_elems(desc):
        if desc[0] == "cols":
            return 8 * desc[3]
        return {"gu": 4096, "down": FC * P}[desc[0]]

    def emit_first_fetch(sid, s):
        desc = slice_list[sid]
        kind = desc[0]
        pieces = []
        if kind == "cols":
            _, key, c0, ncols = desc
            src_ = w_d[key][:, c0:c0 + ncols].rearrange("(kc p) n -> p kc n", p=P)
            dst = slots_t[:, s, 0:8 * ncols].rearrange("p (kc n) -> p kc n", n=ncols)
            pieces.append((dst, src_))
        elif kind == "gu":
            _, key, jj = desc
            dst_all = slots_t[:, s, 0:4096].rearrange("p (kc n) -> p kc n", n=512)
            for half in range(2):
                c0 = half * DFF + jj * 256
                src_ = w_d[key][:, c0:c0 + 256].rearrange("(kc p) n -> p kc n", p=P)
                pieces.append((dst_all[:, :, half * 256:(half + 1) * 256], src_))
        else:
            _, key, c = desc
            src_ = w_d[key][:, c * P:(c + 1) * P].rearrange("(kc p) n -> p kc n", p=P)
            dst = slots_t[:, s, 0:FC * P].rearrange("p (kc n) -> p kc n", n=P)
            pieces.append((dst, src_))
        for dst, src_ in pieces:
            S.add("pool", lambda e, dst=dst, src_=src_: e.dma_start(out=dst, in_=src_),
                  writes=[slot_buf[s]], chan=cast_chan[s])
        if n_uses[sid] > 1:
            ne = slice_elems(desc)
            S.add("sp", lambda e, ne=ne: e.dma_start(out=scratch[sid, :, 0:ne], in_=slots_t[:, s, 0:ne]),
                  reads=[slot_buf[s]], writes=[scratch_buf[sid]], chan=store_chan[s])
        in_scratch[sid] = True

    class WStream:
        def __init__(self):
            self.next_load = 0
            self.next_use = 0
            self.done_upto = 0
            self.done_flags = [False] * len(plan)

        def pump(self):
            while self.next_load < len(plan) and self.next_load - NSLOT < self.done_upto:
                n = self.next_load
                sid = slice_ids[plan[n]]
                s = n % NSLOT
                desc = plan[n]
                if not in_scratch[sid]:
                    emit_first_fetch(sid, s)
                else:
                    ne = slice_elems(desc)
                    dst = slots_t[:, s, 0:ne]
                    src = scratch[sid, :, 0:ne]
                    S.add("sp", lambda e, dst=dst, src=src: e.dma_start(out=dst, in_=src),
                          reads=[scratch_buf[sid]], writes=[slot_buf[s]], chan=slot_chan[s])
                self.next_load += 1

        def next(self, desc):
            n = self.next_use
            assert plan[n] == desc, (n, plan[n], desc)
            self.next_use += 1
            self.pump()
            assert self.next_load > n
            s = n % NSLOT
            return n, s

        def done(self, n):
            self.done_flags[n] = True
            while self.done_upto < len(plan) and self.done_flags[self.done_upto]:
                self.done_upto += 1
            self.pump()

    W = WStream()

    def slot_cols(s, ncols):
        return slots_t[:, s, 0:8 * ncols].rearrange("p (kc n) -> p kc n", n=ncols)

    def slot_down(s):
        return slots_t[:, s, 0:FC * P].rearrange("p (kc n) -> p kc n", n=P)

    vecs = sb("vecs", [P, NV], F32)
    vecs_b = Buf(const=True)
    ident = sb("ident", [P, P], F32)
    ident_b = Buf(const=True)
    ones_bf = sb("ones_bf", [P, P], BF16)
    ones_b = Buf(const=True)
    hmask = sb("hmask", [P, 1], F32)
    hmask_b = Buf(const=True)
    S.add("sp", lambda e: e.dma_start(out=vecs[:, :], in_=vecs_d[:, :]), writes=[vecs_b], chan=Chan(sem("c_vecs")))
    S.add("sp", lambda e: e.dma_start(out=ident[:, :], in_=ident_d[:, :]), writes=[ident_b], chan=Chan(sem("c_ident")))
    DBG = os.environ.get("KDBG", "")
    if "B" not in DBG:
        S.add("sp", lambda e: e.dma_start(out=hmask[:, :], in_=hmask_d[:, :]), writes=[hmask_b], chan=Chan(sem("c_hmask")))
    if "A" not in DBG:
        S.add("dve", lambda e: e.memset(ones_bf[:, :], 1.0), writes=[ones_b])

    def vcol(c):
        return vecs[:, c:c + 1]

    bhalf = sb("bhalf", [P, 16], F32)
    bhalf_b = Buf(const=True)
    S.add("dve", lambda e: e.tensor_scalar(bhalf[:, :], vecs[:, V_BIN + 32:V_BIN + 48], 0.5, None, ALU.mult),
          reads=[vecs_b], writes=[bhalf_b])

    hT = sb("hT", [P, DC, TT], F32)
    hT_b = [Buf() for _ in range(DC)]
    hT_main, hT_main_b = hT, hT_b
    hTh = sb("hTh", [P, DC, HALO], F32)
    hTh_b = [Buf() for _ in range(DC)]
    xnh = sb("xnh", [P, DC, HALO], BF16)
    xnh_b = [Buf() for _ in range(DC)]
    hidh = sb("hidh", [P, FC, HALO], BF16)
    hidh_b = [Buf() for _ in range(FC)]
    xn = sb("xn", [P, DC, TT], BF16)
    xn_b = [Buf() for _ in range(DC)]
    hid = sb("hid", [P, 24, TT], BF16)
    hid_b = [Buf() for _ in range(24)]
    sq = sb("sq", [P, 4, TT], BF16)
    sq_b = [Buf() for _ in range(4)]
    rstd = sb("rstd", [P, 2, TT], F32)
    rstd_b = [Buf(), Buf()]
    tmpf = sb("tmpf", [P, 3, TT], F32)
    tmpf_b = [Buf() for _ in range(3)]
    xs = sb("xs", [P, 3, D], F32)
    xs_b = [Buf() for _ in range(3)]
    xs_chan = [Chan(sem("xs%d" % i)) for i in range(3)]
    os_chan = [Chan(sem("os%d" % i)) for i in range(4)]
    ystore_b = [Buf() for _ in range(4)]

    psum = [es.enter_context(nc.psum_tensor("ps%d" % i, [P, TT], F32)) for i in range(8)]
    psum_b = [Buf() for _ in range(8)]
    ps_ctr = [0]

    def ps_next():
        i = ps_ctr[0] % 6
        ps_ctr[0] += 1
        return psum[i], psum_b[i]

    rr = {"sq": 0, "rstd": 0, "tmpf": 0, "xs": 0, "os": 0}

    def ring(name, n):
        i = rr[name] % n
        rr[name] += 1
        return i

    def mm(out, lhsT, rhs, start, stop, reads, writes):
        S.add("pe", lambda e: e.matmul(out, lhsT, rhs, start=start, stop=stop),
              reads=reads, writes=writes)

    def load_x_tile(src_d, r0, ntok, hT=None, hT_b=None):
        if hT is None:
            hT, hT_b = hT_main, hT_main_b
        S.label = "load_x"
        nch = (ntok + P - 1) // P
        for tc in range(nch):
            n = min(P, ntok - tc * P)
            i = ring("xs", 3)
            S.add("sp", lambda e, i=i, n=n, tc=tc: e.dma_start(
                out=xs[0:n, i, :], in_=src_d[r0 + tc * P:r0 + tc * P + n, :]),
                writes=[xs_b[i]], chan=xs_chan[i])
            for g in range(2):
                pt, pb = ps_next()
                for j in range(4):
                    c = 4 * g + j
                    S.add("pe", lambda e, pt=pt, j=j, i=i, n=n, c=c: e.transpose(
                        pt[:, j * P:j * P + n], xs[0:n, i, c * P:(c + 1) * P], ident[0:n, 0:n]),
                        reads=[xs_b[i], ident_b], writes=[pb])
                for j in range(4):
                    c = 4 * g + j
                    eng = "dve"
                    if eng == "act":
                        S.add("act", lambda e, pt=pt, j=j, n=n, c=c, tc=tc: e.activation(
                            hT[:, c, tc * P:tc * P + n], pt[:, j * P:j * P + n], AF.Copy),
                            reads=[pb], writes=[hT_b[c]])
                    else:
                        S.add("dve", lambda e, pt=pt, j=j, n=n, c=c, tc=tc: e.tensor_copy(
                            hT[:, c, tc * P:tc * P + n], pt[:, j * P:j * P + n]),
                            reads=[pb], writes=[hT_b[c]])

    stat_pending = []

    def stat_add(src, src_b, c, n, delay=0):
        i = ring("sq", 4)
        S.add("act", lambda e, i=i, c=c: e.activation(sq[:, i, 0:n], src[:, c, 0:n], AF.Square),
              reads=[src_b[c]], writes=[sq_b[i]])
        stat_pending.append((i, c, n))
        while len(stat_pending) > delay:
            stat_flush_one()

    def stat_flush_one():
        i, c, n = stat_pending.pop(0)
        mm(psum[6][:, 0:n], ones_bf[:, :], sq[:, i, 0:n], c == 0, c == DC - 1,
           [sq_b[i], ones_b], [psum_b[6]])

    def stat_flush():
        while stat_pending:
            stat_flush_one()

    def rmsnorm(src, src_b, gcol, n, dst, dst_b, nchunks=DC, pre=False):
        pt, pb = psum[6], psum_b[6]
        if not pre:
            for c in range(nchunks):
                stat_add(src, src_b, c, n)
        i = ring("rstd", 2)
        t = ring("tmpf", 3)
        S.add("act", lambda e: e.activation(tmpf[:, t, 0:n], pt[:, 0:n], AF.Sqrt,
                                            bias=eps_rms[:, 0:1], scale=1.0 / D),
              reads=[pb, eps_b], writes=[tmpf_b[t]])
        S.add("dve", lambda e: e.reciprocal(rstd[:, i, 0:n], tmpf[:, t, 0:n]),
              reads=[tmpf_b[t]], writes=[rstd_b[i]])
        for c in range(nchunks):
            S.add("dve", lambda e, c=c: e.scalar_tensor_tensor(
                out=dst[:, c, 0:n], in0=src[:, c, 0:n], scalar=vcol(gcol + c),
                in1=rstd[:, i, 0:n], op0=ALU.mult, op1=ALU.mult),
                reads=[src_b[c], rstd_b[i], vecs_b], writes=[dst_b[c]])

    eps_rms = sb("eps_rms", [P, 2], F32)
    eps_b = Buf(const=True)
    if "A" not in DBG:
        S.add("dve", lambda e: e.memset(eps_rms[:, 0:1], EPS_RMS), writes=[eps_b])
        S.add("dve", lambda e: e.memset(eps_rms[:, 1:2], EPS_LN), writes=[eps_b])

    def ffn(tag, gcol, n, pre=False, halo=False):
        S.label = "ffn_norm"
        if pre != "done":
            rmsnorm(hT, hT_b, gcol, n, xn, xn_b, pre=pre)
        if halo:
            rmsnorm(hTh, hTh_b, gcol, HALO, xnh, xnh_b)
        S.label = "ffn_up"
        for jj in range(11):
            sn, s = W.next(("gu", "gu" + tag, jj))
            wv = slot_cols(s, 512)
            for sub in range(2):
                j = 2 * jj + sub
                pg, pgb = ps_next()
                pu, pub = ps_next()
                for k in range(DC):
                    mm(pg[:, 0:n], wv[:, k, sub * P:(sub + 1) * P], xn[:, k, 0:n],
                       k == 0, k == DC - 1, [slot_buf[s], xn_b[k]], [pgb])
                for k in range(DC):
                    mm(pu[:, 0:n], wv[:, k, 256 + sub * P:256 + (sub + 1) * P], xn[:, k, 0:n],
                       k == 0, k == DC - 1, [slot_buf[s], xn_b[k]], [pub])
                t = ring("tmpf", 3)
                S.add("act", lambda e, t=t, pg=pg: e.activation(tmpf[:, t, 0:n], pg[:, 0:n], AF.Silu),
                      reads=[pgb], writes=[tmpf_b[t]])
                S.add("dve", lambda e, t=t, pu=pu, j=j: e.tensor_tensor(
                    out=hid[:, j, 0:n], in0=pu[:, 0:n], in1=tmpf[:, t, 0:n], op=ALU.mult),
                    reads=[pub, tmpf_b[t]], writes=[hid_b[j]])
                if halo:
                    ph, phb = ps_next()
                    for k in range(DC):
                        mm(ph[:, 0:HALO], wv[:, k, sub * P:(sub + 1) * P], xnh[:, k, :],
                           k == 0, k == DC - 1, [slot_buf[s], xnh_b[k]], [phb])
                    for k in range(DC):
                        mm(ph[:, HALO:2 * HALO], wv[:, k, 256 + sub * P:256 + (sub + 1) * P], xnh[:, k, :],
                           k == 0, k == DC - 1, [slot_buf[s], xnh_b[k]], [phb])
                    t = ring("tmpf", 3)
                    S.add("act", lambda e, t=t, ph=ph: e.activation(tmpf[:, t, 0:HALO], ph[:, 0:HALO], AF.Silu),
                          reads=[phb], writes=[tmpf_b[t]])
                    S.add("dve", lambda e, t=t, ph=ph, j=j: e.tensor_tensor(
                        out=hidh[:, j, :], in0=ph[:, HALO:2 * HALO], in1=tmpf[:, t, 0:HALO], op=ALU.mult),
                        reads=[phb, tmpf_b[t]], writes=[hidh_b[j]])
            W.done(sn)
        S.label = "ffn_down"
        for c in range(DC):
            sn, s = W.next(("down", "dn" + tag, c))
            wv = slot_down(s)
            po, pob = ps_next()
            for j in range(FC):
                mm(po[:, 0:n], wv[:, j, :], hid[:, j, 0:n], j == 0, j == FC - 1,
                   [slot_buf[s], hid_b[j]], [pob])
            if halo:
                ph, phb = ps_next()
                for j in range(FC):
                    mm(ph[:, 0:HALO], wv[:, j, :], hidh[:, j, :], j == 0, j == FC - 1,
                       [slot_buf[s], hidh_b[j]], [phb])
            W.done(sn)
            S.add("dve", lambda e, c=c, po=po: e.scalar_tensor_tensor(
                out=hT[:, c, 0:n], in0=po[:, 0:n], scalar=0.5, in1=hT[:, c, 0:n],
                op0=ALU.mult, op1=ALU.add),
                reads=[pob, hT_b[c]], writes=[hT_b[c]])
            stat_add(hT, hT_b, c, n, delay=1)
            if halo:
                S.add("dve", lambda e, c=c, ph=ph: e.scalar_tensor_tensor(
                    out=hTh[:, c, :], in0=ph[:, 0:HALO], scalar=0.5, in1=hTh[:, c, :],
                    op0=ALU.mult, op1=ALU.add),
                    reads=[phb, hTh_b[c]], writes=[hTh_b[c]])
        stat_flush()

    def store_tile(src, src_b, r0):
        S.label = "store"
        for tc in range(TT // P):
            i = ring("os", 4)
            for g in range(2):
                pt, pb = ps_next()
                for j in range(4):
                    c = 4 * g + j
                    S.add("pe", lambda e, pt=pt, j=j, c=c, tc=tc: e.transpose(
                        pt[:, j * P:(j + 1) * P], src[:, c, tc * P:(tc + 1) * P], ident[:, :]),
                        reads=[src_b[c], ident_b], writes=[pb])
                if g == 0:
                    S.add("act", lambda e, pt=pt, i=i: e.activation(osb[:, i, 0:512], pt[:, :], AF.Copy),
                          reads=[pb], writes=[os_b[i]])
                else:
                    S.add("act", lambda e, pt=pt, i=i: e.activation(osb[:, i, 512:1024], pt[:, :], AF.Copy),
                          reads=[pb], writes=[os_b[i]])
            S.add("sp", lambda e, i=i, tc=tc: e.dma_start(
                out=y_d[r0 + tc * P:r0 + (tc + 1) * P, :], in_=osb[:, i, :]),
                reads=[os_b[i]], writes=[ystore_b[i]], chan=os_chan[i])


    aT = sb("aT", [P, DC, HALO + TT], BF16)
    aT_b = [Buf() for _ in range(DC)]
    acc = sb("acc", [P, DC, TT], F32)
    acc_b = [Buf() for _ in range(DC)]
    gv = sb("gv", [P, 4, D], F32)
    gv_b = [Buf() for _ in range(4)]
    t2v = gv[:, :, :].rearrange("p a (b n) -> p (a b) n", n=TT)
    osb = gv
    os_b = gv_b
    vtok = sb("vtok", [P, 4, D], BF16)
    vtok_b = [Buf() for _ in range(4)]
    bct = sb("bct", [P, 512], F32)
    bct_b = Buf(const=True)
    brow = sb("brow", [P, 2, 1024], BF16)
    brow_b = Buf(const=True)
    rowsf = sb("rowsf", [1, 1536], F32)
    rowsf_b = Buf()
    WsT = sb("WsT", [P, 4, P], BF16)
    WsT_b = Buf(const=True)
    KT = sb("KT", [P, DC, NMEM], BF16)
    KT_b = Buf(const=True)
    Vt = sb("Vt", [P, 2, D], BF16)
    Vt_b = Buf(const=True)
    lnst = sb("lnst", [P, 2, TT], F32)
    lnst_b = [Buf() for _ in range(2)]
    bst = sb("bst", [P, 4, 12], F32)
    bst_b = Buf()
    mv = sb("mv", [P, 4, 2], F32)
    mv_b = Buf()
    sdv = sb("sdv", [P, 8], F32)
    sdv_b = Buf()
    identb = sb("identb", [P, P], BF16)
    identb_b = Buf(const=True)
    dg = sb("dg", [P, 8, P], BF16)
    dg_b = [Buf() for _ in range(8)]
    rr["dg"] = 0
    B2 = sb("B2", [P, DC, P], F32)
    B2_b = Buf(const=True)

    def setup_consts():
        S.add("sp", lambda e: e.dma_start(out=bct[:, :], in_=bc_d[:, :]), writes=[bct_b], chan=Chan(sem("c_bc")))
        S.add("sp", lambda e: e.dma_start(out=rowsf[:, :], in_=rows_d[:, :]), writes=[rowsf_b], chan=Chan(sem("c_rows")))
        S.add("pool", lambda e: e.memset(brow[:, :, :], 0.0), writes=[brow_b])
        S.add("dve", lambda e: e.tensor_copy(brow[0:1, 0, :], rowsf[0:1, 0:1024]), reads=[rowsf_b, brow_b], writes=[brow_b])
        S.add("dve", lambda e: e.tensor_tensor(out=brow[0:1, 1, :], in0=rowsf[0:1, 0:1024], in1=brow[0:1, 0, :],
                                               op=ALU.subtract), reads=[rowsf_b, brow_b], writes=[brow_b])
        wv = tmpf[:, 0, :].rearrange("p (g s) -> p g s", s=P)
        S.add("sp", lambda e: e.dma_start(out=wv, in_=sguw_d.rearrange("g t s -> t g s")),
              writes=[tmpf_b[0]], chan=Chan(sem("c_sguw")))
        S.add("sp", lambda e: e.dma_start(out=tmpf[:, 1, 0:P], in_=tril_d[:, :]),
              writes=[tmpf_b[1]], chan=Chan(sem("c_tril")))
        pt, pb = ps_next()
        for g in range(4):
            S.add("dve", lambda e, g=g: e.tensor_tensor(out=wv[:, g, :], in0=wv[:, g, :], in1=tmpf[:, 1, 0:P],
                                                        op=ALU.mult),
                  reads=[tmpf_b[0], tmpf_b[1]], writes=[tmpf_b[0]])
        for g in range(4):
            S.add("pe", lambda e, g=g: e.transpose(pt[:, g * P:(g + 1) * P], wv[:, g, :], ident[:, :]),
                  reads=[tmpf_b[0], ident_b], writes=[pb])
        S.add("dve", lambda e: e.tensor_copy(WsT[:, :, :].rearrange("p g t -> p (g t)"), pt[:, :]),
              reads=[pb], writes=[WsT_b])
        S.add("dve", lambda e: e.tensor_copy(identb[:, :], ident[:, :]), reads=[ident_b], writes=[identb_b])
        p2, p2b = ps_next()
        for g in range(4):
            mm(p2[:, g * P:(g + 1) * P], ones_bf[:, :], WsT[:, g, :], True, True, [ones_b, WsT_b], [p2b])
        for c in range(DC):
            g = c // 2
            S.add("dve", lambda e, c=c, g=g: e.scalar_tensor_tensor(
                out=B2[:, c, :], in0=p2[:, g * P:(g + 1) * P], scalar=vcol(V_SLNB + c),
                in1=bct[:, g * P:(g + 1) * P], op0=ALU.mult, op1=ALU.add),
                reads=[p2b, vecs_b, bct_b], writes=[B2_b])

    def kv_prep():
        load_x_tile(mem_d, 0, NMEM)
        rmsnorm(hT, hT_b, V_G_MEM, NMEM, xn, xn_b)
        for p in range(2):
            sn, s = W.next(("cols", "wkv", p * 512, 512))
            wv = slot_cols(s, 512)
            for cc in range(4):
                c = 4 * p + cc
                pk, pkb = ps_next()
                for k in range(DC):
                    mm(pk[:, 0:NMEM], wv[:, k, cc * P:(cc + 1) * P], xn[:, k, 0:NMEM],
                       k == 0, k == DC - 1, [slot_buf[s], xn_b[k]], [pkb])
                S.add("act", lambda e, c=c, pk=pk: e.activation(KT[:, c, :], pk[:, 0:NMEM], AF.Copy),
                      reads=[pkb], writes=[KT_b])
            W.done(sn)
        for p in range(2):
            sn, s = W.next(("cols", "wkv", 1024 + p * 512, 512))
            wv = slot_cols(s, 512)
            for mc in range(2):
                pv, pvb = ps_next()
                for k in range(DC):
                    mm(pv[:, :], xn[:, k, mc * P:(mc + 1) * P], wv[:, k, :],
                       k == 0, k == DC - 1, [slot_buf[s], xn_b[k]], [pvb])
                S.add("act", lambda e, mc=mc, p=p, pv=pv: e.activation(
                    Vt[:, mc, p * 512:(p + 1) * 512], pv[:, :], AF.Copy), reads=[pvb], writes=[Vt_b])
            W.done(sn)

    def mixer_a_proj(n, col0, halo=False):
        streams = [(xn, xn_b, n, col0)]
        if halo:
            streams.append((xnh, xnh_b, HALO, 0))
        for p in range(2):
            snv, sv = W.next(("cols", "win", p * 512, 512))
            sng, sg = W.next(("cols", "win", 1024 + p * 512, 512))
            wvv, wvg = slot_cols(sv, 512), slot_cols(sg, 512)
            for cc in range(4):
                c = 4 * p + cc
                for (sx, sx_b, sn_, sc0) in streams:
                    pv, pvb = ps_next()
                    pg, pgb = ps_next()
                    for k in range(DC):
                        mm(pv[:, 0:sn_], wvv[:, k, cc * P:(cc + 1) * P], sx[:, k, 0:sn_],
                           k == 0, k == DC - 1, [slot_buf[sv], sx_b[k]], [pvb])
                    for k in range(DC):
                        mm(pg[:, 0:sn_], wvg[:, k, cc * P:(cc + 1) * P], sx[:, k, 0:sn_],
                           k == 0, k == DC - 1, [slot_buf[sg], sx_b[k]], [pgb])
                    t = ring("tmpf", 3)
                    S.add("act", lambda e, t=t, pg=pg, c=c, sn_=sn_: e.activation(
                        tmpf[:, t, 0:sn_], pg[:, 0:sn_], AF.Sigmoid, bias=vcol(V_BIN + 8 + c)),
                        reads=[pgb, vecs_b], writes=[tmpf_b[t]])
                    S.add("dve", lambda e, t=t, pv=pv, c=c, sn_=sn_, sc0=sc0: e.scalar_tensor_tensor(
                        out=aT[:, c, sc0:sc0 + sn_], in0=pv[:, 0:sn_], scalar=vcol(V_BIN + c),
                        in1=tmpf[:, t, 0:sn_], op0=ALU.add, op1=ALU.mult),
                        reads=[pvb, tmpf_b[t], vecs_b], writes=[aT_b[c]])
                if halo:
                    S.add("dve", lambda e, c=c: e.tensor_scalar(aT[:, c, 0:HALO], aT[:, c, 0:HALO],
                                                                hmask[:, 0:1], None, ALU.mult),
                          reads=[aT_b[c], hmask_b], writes=[aT_b[c]])
            W.done(snv)
            W.done(sng)

    def mixer(halo=False):
        n = TT
        S.label = "mix_norm"
        rmsnorm(hT, hT_b, V_G_MIX, n, xn, xn_b, pre=True)
        if halo:
            rmsnorm(hTh, hTh_b, V_G_MIX, HALO, xnh, xnh_b)
        S.label = "mix_aproj"
        mixer_a_proj(n, HALO, halo=halo)
        S.label = "mix_v"
        for p in range(2):
            sn, s = W.next(("cols", "win", 3072 + p * 512, 512))
            wv = slot_cols(s, 512)
            for tc in range(4):
                pv, pvb = ps_next()
                for k in range(DC):
                    mm(pv[:, :], xn[:, k, tc * P:(tc + 1) * P], wv[:, k, :], k == 0, False,
                       [slot_buf[s], xn_b[k]], [pvb])
                mm(pv[:, :], ones_bf[:, :], brow[:, 0, p * 512:(p + 1) * 512], False, False,
                   [ones_b, brow_b], [pvb])
                mm(pv[:, :], ones_bf[:, :], brow[:, 1, p * 512:(p + 1) * 512], False, True,
                   [ones_b, brow_b], [pvb])
                S.add("act", lambda e, pv=pv, tc=tc, p=p: e.activation(
                    gv[:, tc, p * 512:(p + 1) * 512], pv[:, :], AF.Gelu_apprx_tanh),
                    reads=[pvb], writes=[gv_b[tc]])
            W.done(sn)
        for tc in range(4):
            for hh in range(2):
                S.add("dve", lambda e, tc=tc, hh=hh: e.bn_stats(
                    bst[:, tc, hh * 6:(hh + 1) * 6], gv[:, tc, hh * 512:(hh + 1) * 512]),
                    reads=[gv_b[tc]], writes=[bst_b])
            S.add("dve", lambda e, tc=tc: e.bn_aggr(
                mv[:, tc, :], bst[:, tc, :].rearrange("p (a b) -> p a b", b=6)),
                reads=[bst_b], writes=[mv_b])
        S.add("act", lambda e: e.activation(sdv[:, 0:4], mv[:, :, 1], AF.Sqrt, bias=eps_rms[:, 1:2]),
              reads=[mv_b, eps_b], writes=[sdv_b])
        S.add("dve", lambda e: e.reciprocal(sdv[:, 4:8], sdv[:, 0:4]), reads=[sdv_b], writes=[sdv_b])
        for tc in range(4):
            S.add("dve", lambda e, tc=tc: e.tensor_scalar(
                vtok[:, tc, :], gv[:, tc, :], mv[:, tc, 0:1], sdv[:, 4 + tc:5 + tc], ALU.subtract, ALU.mult),
                reads=[gv_b[tc], mv_b, sdv_b], writes=[vtok_b[tc]])
        S.label = "mix_u"
        for p in range(2):
            sn, s = W.next(("cols", "win", 2048 + p * 512, 512))
            wv = slot_cols(s, 512)
            for cc in range(4):
                c = 4 * p + cc
                pu, pub = ps_next()
                for k in range(DC):
                    mm(pu[:, :], wv[:, k, cc * P:(cc + 1) * P], xn[:, k, :],
                       k == 0, k == DC - 1, [slot_buf[s], xn_b[k]], [pub])
                S.add("act", lambda e, pu=pu, c=c: e.activation(
                    hid[:, c, :], pu[:, :], AF.Gelu_apprx_tanh, bias=vcol(V_BIN + 16 + c)),
                    reads=[pub, vecs_b], writes=[hid_b[c]])
            W.done(sn)
        S.label = "mix_conv"
        ps_s, ps_sb = psum[6], psum_b[6]
        ps_q, ps_qb = psum[7], psum_b[7]
        cstat_pending = []

        def cstat_flush_one():
            j1, j2, cj = cstat_pending.pop(0)
            mm(ps_s[:, :], ones_bf[:, :], sq[:, j1, :], cj == 0, cj == DC - 1, [sq_b[j1], ones_b], [ps_sb])
            mm(ps_q[:, :], ones_bf[:, :], sq[:, j2, :], cj == 0, cj == DC - 1, [sq_b[j2], ones_b], [ps_qb])
        def conv_builds(c_lo, c_hi, act_chunk):
            for c in range(c_lo, c_hi):
                for k in range(CW):
                    r = ring("dg", 8)
                    if k % 2 == 0 or c == act_chunk:
                        S.add("act", lambda e, r=r, k=k, c=c: e.activation(
                            dg[:, r, :], identb[:, :], AF.Copy, scale=vcol(V_CONVW + k * 8 + c)),
                            reads=[identb_b, vecs_b], writes=[dg_b[r]])
                    else:
                        S.add("dve", lambda e, r=r, k=k, c=c: e.tensor_scalar(
                            dg[:, r, :], identb[:, :], vcol(V_CONVW + k * 8 + c), None, ALU.mult),
                            reads=[identb_b, vecs_b], writes=[dg_b[r]])
                    yield r

        builds = conv_builds(0, 4, -1)
        ready = []
        for _ in range(7):
            ready.append(next(builds))
        for c in range(0, 4):
            pc, pcb = ps_next()
            for k in range(CW):
                nb = next(builds, None)
                if nb is not None:
                    ready.append(nb)
                r = ready.pop(0)
                mm(pc[:, :], dg[:, r, :], aT[:, c, 2 + k:2 + k + TT], k == 0, k == CW - 1,
                   [dg_b[r], aT_b[c]], [pcb])
            S.add("act", lambda e, pc=pc, c=c: e.activation(acc[:, c, :], pc[:, :], AF.Identity,
                                                            bias=vcol(V_CONVB + c)),
                  reads=[pcb, vecs_b], writes=[acc_b[c]])
            i1 = ring("sq", 4)
            S.add("act", lambda e, i=i1, pc=pc, c=c: e.activation(sq[:, i, :], pc[:, :], AF.Identity,
                                                                  bias=vcol(V_CONVB + c)),
                  reads=[pcb, vecs_b], writes=[sq_b[i1]])
            i2 = ring("sq", 4)
            S.add("act", lambda e, i=i2, pc=pc, c=c: e.activation(sq[:, i, :], pc[:, :], AF.Square,
                                                                  bias=vcol(V_CONVB + c)),
                  reads=[pcb, vecs_b], writes=[sq_b[i2]])
            cstat_pending.append((i1, i2, c))
            if len(cstat_pending) > 1:
                cstat_flush_one()
        S.label = "mix_sgu"
        for c in range(DC):
            pm, pmb = ps_next()
            g = c // 2
            for tc in range(4):
                mm(pm[:, tc * P:(tc + 1) * P], vtok[:, tc, c * P:(c + 1) * P], WsT[:, g, :], True, True,
                   [vtok_b[tc], WsT_b], [pmb])
            t = ring("tmpf", 3)
            S.add("dve", lambda e, t=t, pm=pm, c=c: e.scalar_tensor_tensor(
                out=tmpf[:, t, :].rearrange("p (a b) -> p a b", b=P),
                in0=pm[:, :].rearrange("p (a b) -> p a b", b=P), scalar=vcol(V_SLNG + c),
                in1=B2[:, c, :].unsqueeze(1).broadcast_to([P, 4, P]), op0=ALU.mult, op1=ALU.add),
                reads=[pmb, vecs_b, B2_b], writes=[tmpf_b[t]])
            S.add("dve", lambda e, t=t, c=c: e.tensor_tensor(
                out=hid[:, 16 + c, :], in0=tmpf[:, t, :], in1=hid[:, c, :], op=ALU.mult),
                reads=[tmpf_b[t], hid_b[c]], writes=[hid_b[16 + c]])
        S.label = "mix_conv"
        builds = conv_builds(4, DC, 4)
        ready = []
        for _ in range(7):
            ready.append(next(builds))
        for c in range(4, DC):
            pc, pcb = ps_next()
            for k in range(CW):
                nb = next(builds, None)
                if nb is not None:
                    ready.append(nb)
                r = ready.pop(0)
                mm(pc[:, :], dg[:, r, :], aT[:, c, 2 + k:2 + k + TT], k == 0, k == CW - 1,
                   [dg_b[r], aT_b[c]], [pcb])
            S.add("act", lambda e, pc=pc, c=c: e.activation(acc[:, c, :], pc[:, :], AF.Identity,
                                                            bias=vcol(V_CONVB + c)),
                  reads=[pcb, vecs_b], writes=[acc_b[c]])
            i1 = ring("sq", 4)
            S.add("act", lambda e, i=i1, pc=pc, c=c: e.activation(sq[:, i, :], pc[:, :], AF.Identity,
                                                                  bias=vcol(V_CONVB + c)),
                  reads=[pcb, vecs_b], writes=[sq_b[i1]])
            i2 = ring("sq", 4)
            S.add("act", lambda e, i=i2, pc=pc, c=c: e.activation(sq[:, i, :], pc[:, :], AF.Square,
                                                                  bias=vcol(V_CONVB + c)),
                  reads=[pcb, vecs_b], writes=[sq_b[i2]])
            cstat_pending.append((i1, i2, c))
            if len(cstat_pending) > 1:
                cstat_flush_one()
        while cstat_pending:
            cstat_flush_one()
        S.label = "mix_convln"
        S.add("dve", lambda e: e.tensor_scalar(lnst[:, 0, :], ps_s[:, :], 1.0 / D, None, ALU.mult),
              reads=[ps_sb], writes=[lnst_b[0]])
        S.add("dve", lambda e: e.tensor_tensor(out=lnst[:, 1, :], in0=lnst[:, 0, :], in1=lnst[:, 0, :],
                                               op=ALU.mult), reads=[lnst_b[0]], writes=[lnst_b[1]])
        S.add("dve", lambda e: e.scalar_tensor_tensor(
            out=lnst[:, 1, :], in0=ps_q[:, :], scalar=1.0 / D, in1=lnst[:, 1, :],
            op0=ALU.mult, op1=ALU.subtract), reads=[ps_qb, lnst_b[1]], writes=[lnst_b[1]])
        tq = ring("tmpf", 3)
        S.add("act", lambda e: e.activation(tmpf[:, tq, :], lnst[:, 1, :], AF.Sqrt, bias=eps_rms[:, 1:2]),
              reads=[lnst_b[1], eps_b], writes=[tmpf_b[tq]])
        S.add("dve", lambda e: e.reciprocal(lnst[:, 1, :], tmpf[:, tq, :]), reads=[tmpf_b[tq]], writes=[lnst_b[1]])
        S.add("dve", lambda e: e.tensor_tensor(out=lnst[:, 0, :], in0=lnst[:, 0, :], in1=lnst[:, 1, :],
                                               op=ALU.mult), reads=[lnst_b[0], lnst_b[1]], writes=[lnst_b[0]])
        for c in range(DC):
            S.add("pool", lambda e, c=c: e.tensor_tensor(out=acc[:, c, :], in0=acc[:, c, :], in1=lnst[:, 1, :],
                                                         op=ALU.mult),
                  reads=[acc_b[c], lnst_b[1]], writes=[acc_b[c]])

        def convln_chunk(c):
            S.add("dve", lambda e, c=c: e.tensor_tensor(out=acc[:, c, :], in0=acc[:, c, :], in1=lnst[:, 0, :],
                                                        op=ALU.subtract),
                  reads=[acc_b[c], lnst_b[0]], writes=[acc_b[c]])
            S.add("act", lambda e, c=c: e.activation(hid[:, 8 + c, :], acc[:, c, :], AF.Silu,
                                                     bias=vcol(V_CLNB + c), scale=vcol(V_CLNG + c)),
                  reads=[acc_b[c], vecs_b], writes=[hid_b[8 + c]])
        S.label = "mix_wb"
        for p in range(2):
            snb, sbb = W.next(("cols", "wb", p * 512, 512))
            sng, sg = W.next(("cols", "win", 5120 + p * 512, 512))
            wvb, wvg = slot_cols(sbb, 512), slot_cols(sg, 512)
            pgs = []
            for cc in range(4):
                pg, pgb = ps_next()
                for k in range(DC):
                    mm(pg[:, :], wvg[:, k, cc * P:(cc + 1) * P], xn[:, k, :],
                       k == 0, k == DC - 1, [slot_buf[sg], xn_b[k]], [pgb])
                pgs.append((pg, pgb))
            for cc in range(4):
                c = 4 * p + cc
                pg, pgb = pgs[cc]
                py, pyb = ps_next()
                for k in range(DC):
                    mm(py[:, :], wvb[:, k, cc * P:(cc + 1) * P], hid[:, 16 + k, :],
                       k == 0, k == DC - 1, [slot_buf[sbb], hid_b[16 + k]], [pyb])
                S.label = "mix_convln"
                convln_chunk(c)
                S.label = "mix_wb"
                t = ring("tmpf", 3)
                S.add("act", lambda e, t=t, pg=pg, c=c: e.activation(
                    tmpf[:, t, :], pg[:, :], AF.Tanh, bias=bhalf[:, 8 + c:9 + c], scale=0.5),
                    reads=[pgb, bhalf_b], writes=[tmpf_b[t]])
                S.add("dve", lambda e, t=t, py=py, c=c: e.scalar_tensor_tensor(
                    out=t2v[:, c, :], in0=tmpf[:, t, :], scalar=1.0, in1=py[:, :],
                    op0=ALU.add, op1=ALU.mult),
                    reads=[pyb, tmpf_b[t]], writes=[gv_b[c // 2]])
            W.done(snb)
            W.done(sng)
        S.label = "mix_wa"
        for p in range(2):
            sna, sa = W.next(("cols", "wa", p * 512, 512))
            sng, sg = W.next(("cols", "win", 4096 + p * 512, 512))
            wva, wvg = slot_cols(sa, 512), slot_cols(sg, 512)
            for cc in range(4):
                c = 4 * p + cc
                py, pyb = ps_next()
                pg, pgb = ps_next()
                for k in range(DC):
                    mm(py[:, :], wva[:, k, cc * P:(cc + 1) * P], hid[:, 8 + k, :],
                       k == 0, k == DC - 1, [slot_buf[sa], hid_b[8 + k]], [pyb])
                for k in range(DC):
                    mm(pg[:, :], wvg[:, k, cc * P:(cc + 1) * P], xn[:, k, :],
                       k == 0, k == DC - 1, [slot_buf[sg], xn_b[k]], [pgb])
                t = ring("tmpf", 3)
                S.add("act", lambda e, t=t, pg=pg, c=c: e.activation(
                    tmpf[:, t, :], pg[:, :], AF.Tanh, bias=bhalf[:, c:c + 1], scale=0.5),
                    reads=[pgb, bhalf_b], writes=[tmpf_b[t]])
                S.add("dve", lambda e, t=t, py=py: e.scalar_tensor_tensor(
                    out=tmpf[:, t, :], in0=tmpf[:, t, :], scalar=1.0, in1=py[:, :],
                    op0=ALU.add, op1=ALU.mult),
                    reads=[pyb, tmpf_b[t]], writes=[tmpf_b[t]])
                S.add("pool", lambda e, t=t, c=c: e.tensor_tensor(
                    out=hid[:, c, :], in0=tmpf[:, t, :], in1=t2v[:, c, :], op=ALU.add),
                    reads=[tmpf_b[t], gv_b[c // 2]], writes=[hid_b[c]])
            W.done(sna)
            W.done(sng)
        S.label = "mix_wout"
        proj_residual("wout", 0, scale=0.5)
        for c in range(DC):
            S.add("pool", lambda e, c=c: e.tensor_copy(aT[:, c, 0:HALO], aT[:, c, TT:TT + HALO]),
                  reads=[aT_b[c]], writes=[aT_b[c]])

    def proj_residual(key, src0, scale=1.0):
        for p in range(2):
            sn, s = W.next(("cols", key, p * 512, 512))
            wv = slot_cols(s, 512)
            for cc in range(4):
                c = 4 * p + cc
                po, pob = ps_next()
                for k in range(DC):
                    mm(po[:, :], wv[:, k, cc * P:(cc + 1) * P], hid[:, src0 + k, :],
                       k == 0, k == DC - 1, [slot_buf[s], hid_b[src0 + k]], [pob])
                S.add("dve", lambda e, c=c, po=po: e.scalar_tensor_tensor(
                    out=hT[:, c, :], in0=po[:, :], scalar=scale, in1=hT[:, c, :],
                    op0=ALU.mult, op1=ALU.add),
                    reads=[pob, hT_b[c]], writes=[hT_b[c]])
                stat_add(hT, hT_b, c, TT, delay=2)
            W.done(sn)
        stat_flush()

    def xattn():
        n = TT
        S.label = "xat_norm"
        rmsnorm(hT, hT_b, V_G_XAT, n, xn, xn_b, pre=True)
        S.label = "xat_q"
        for p in range(2):
            sn, s = W.next(("cols", "wq", p * 512, 512))
            wv = slot_cols(s, 512)
            for cc in range(4):
                c = 4 * p + cc
                pq, pqb = ps_next()
                for k in range(DC):
                    mm(pq[:, :], wv[:, k, cc * P:(cc + 1) * P], xn[:, k, :],
                       k == 0, k == DC - 1, [slot_buf[s], xn_b[k]], [pqb])
                S.add("act", lambda e, c=c, pq=pq: e.activation(hid[:, 8 + c, :], pq[:, :], AF.Copy,
                                                                scale=1.0 / 16.0),
                      reads=[pqb], writes=[hid_b[8 + c]])
            W.done(sn)
        S.label = "xat_attn"

        def head_scores(h):
            for mc in range(2):
                psc, pscb = ps_next()
                for dc in range(2):
                    mm(psc[:, :], KT[:, 2 * h + dc, mc * P:(mc + 1) * P], hid[:, 8 + 2 * h + dc, :],
                       dc == 0, dc == 1, [KT_b, hid_b[8 + 2 * h + dc]], [pscb])
                S.add("act", lambda e, psc=psc, h=h, mc=mc: e.activation(
                    hid[:, 16 + 2 * h + mc, :], psc[:, :], AF.Exp),
                    reads=[pscb], writes=[hid_b[16 + 2 * h + mc]])

        def head_rest(h):
            pss, pssb = ps_next()
            for mc in range(2):
                mm(pss[:, :], ones_bf[:, :], hid[:, 16 + 2 * h + mc, :], mc == 0, mc == 1,
                   [ones_b, hid_b[16 + 2 * h + mc]], [pssb])
            i = ring("rstd", 2)
            S.add("dve", lambda e, i=i, pss=pss: e.reciprocal(rstd[:, i, :], pss[:, :]),
                  reads=[pssb], writes=[rstd_b[i]])
            for dc in range(2):
                po, pob = ps_next()
                for mc in range(2):
                    mm(po[:, :], Vt[:, mc, (2 * h + dc) * P:(2 * h + dc + 1) * P], hid[:, 16 + 2 * h + mc, :],
                       mc == 0, mc == 1, [Vt_b, hid_b[16 + 2 * h + mc]], [pob])
                S.add("dve", lambda e, i=i, po=po, h=h, dc=dc: e.tensor_tensor(
                    out=hid[:, 2 * h + dc, :], in0=po[:, :], in1=rstd[:, i, :], op=ALU.mult),
                    reads=[pob, rstd_b[i]], writes=[hid_b[2 * h + dc]])

        head_scores(0)
        for h in range(4):
            if h + 1 < 4:
                head_scores(h + 1)
            head_rest(h)
        S.label = "xat_wo"
        proj_residual("wo", 0)

    full = stages in ("all", "mixer", "xattn")
    if full:
        setup_consts()
    if stages in ("all", "xattn"):
        kv_prep()
    fuse_halo = do_halo and full
    if fuse_halo:
        load_x_tile(xh_d, 0, HALO, hTh, hTh_b)
    def tile_head(it):
        load_x_tile(x_d, it * TT, TT)
        S.label = "ffn_norm"
        rmsnorm(hT, hT_b, V_G_FFN1, TT, xn, xn_b)

    for it in range(NT):
        if stages != "all":
            load_x_tile(x_d, it * TT, TT)
        if stages == "xpose":
            store_tile(hT, hT_b, it * TT)
            continue
        if stages == "norm":
            rmsnorm(hT, hT_b, V_G_FFN1, TT, hT, hT_b)
            store_tile(hT, hT_b, it * TT)
            continue
        if stages != "all":
            ffn("1", V_G_FFN1, TT, halo=(fuse_halo and it == 0))
        else:
            if it == 0:
                tile_head(0)
            ffn("1", V_G_FFN1, TT, pre="done", halo=(fuse_halo and it == 0))
        if stages == "ffn1":
            store_tile(hT, hT_b, it * TT)
            continue
        mixer(halo=(fuse_halo and it == 0))
        if stages == "mixer":
            store_tile(hT, hT_b, it * TT)
            continue
        xattn()
        if stages == "xattn":
            store_tile(hT, hT_b, it * TT)
            continue
        ffn("2", V_G_FFN2, TT, pre=True)
        S.label = "final_norm"
        rmsnorm(hT, hT_b, V_G_FIN, TT, acc, acc_b, pre=True)
        if it + 1 < NT:
            tile_head(it + 1)
        store_tile(acc, acc_b, it * TT)

    S.add("sp", lambda e: e.nop(), reads=ystore_b)

    with nc.Block() as block:
        S.emit(nc, block, esem)
    es.close()
    nc._pe_labels = S.labels["pe"]
    return nc


def _pack_inputs(inputs, NT):
    f = lambda a: np.ascontiguousarray(np.asarray(a, dtype=np.float32))
    x = f(inputs["x"])
    mem = f(inputs["mem"])

    def fm(v):
        return f(v).reshape(-1, P).T

    vecs = np.zeros((P, NV), np.float32)
    vecs[:, V_G_FFN1:V_G_FFN1 + 8] = fm(inputs["ffn1_norm"][0])
    vecs[:, V_G_MIX:V_G_MIX + 8] = fm(inputs["mix_norm"][0])
    vecs[:, V_G_XAT:V_G_XAT + 8] = fm(inputs["xattn_norm"][0])
    vecs[:, V_G_MEM:V_G_MEM + 8] = fm(inputs["mem_norm"][0])
    vecs[:, V_G_FFN2:V_G_FFN2 + 8] = fm(inputs["ffn2_norm"][0])
    vecs[:, V_G_FIN:V_G_FIN + 8] = fm(inputs["final_norm"])
    vecs[:, V_BIN:V_BIN + 48] = fm(inputs["b_in"][0])
    vecs[:, V_CONVB:V_CONVB + 8] = fm(inputs["conv_b"][0])
    vecs[:, V_CLNG:V_CLNG + 8] = fm(inputs["conv_ln_g"][0])
    vecs[:, V_CLNB:V_CLNB + 8] = fm(inputs["conv_ln_b"][0])
    cw = f(inputs["conv_w"][0])
    vecs[:, V_CONVW:V_CONVW + CW * 8] = cw.reshape(CW, 8, P).transpose(2, 0, 1).reshape(P, CW * 8)
    vecs[:, V_SLNG:V_SLNG + 8] = fm(inputs["sgu_ln_g"][0])
    vecs[:, V_SLNB:V_SLNB + 8] = fm(inputs["sgu_ln_b"][0])
    rows = np.concatenate([f(inputs["b_in"][0])[3072:4096], f(inputs["sgu_b"][0]).reshape(-1)])[None, :]
    bc = np.broadcast_to(f(inputs["sgu_b"][0]).reshape(1, -1), (P, 512))
    shared = {
        "vecs": vecs, "rows": np.ascontiguousarray(rows), "bc": np.ascontiguousarray(bc),
        "ident": np.eye(P, dtype=np.float32),
        "tril": np.tril(np.ones((P, P), np.float32)),
        "sgu_w": f(inputs["sgu_w"][0]),
        "ffn1_w_gu": f(inputs["ffn1_w_gu"][0]), "ffn1_w_down": f(inputs["ffn1_w_down"][0]),
        "w_in": f(inputs["w_in"][0]), "w_a_out": f(inputs["w_a_out"][0]),
        "w_b_out": f(inputs["w_b_out"][0]), "w_out": f(inputs["w_out"][0]),
        "w_q": f(inputs["w_q"][0]), "w_kv": f(inputs["w_kv"][0]), "w_o": f(inputs["w_o"][0]),
        "ffn2_w_gu": f(inputs["ffn2_w_gu"][0]), "ffn2_w_down": f(inputs["ffn2_w_down"][0]),
    }
    per = NT * TT
    in_maps = []
    for i in range(NCORES):
        b, q = i // 4, i % 4
        t0 = q * per
        m = dict(shared)
        m["x"] = np.ascontiguousarray(x[b, t0:t0 + per])
        if q == 0:
            m["xh"] = np.zeros((HALO, D), np.float32)
            m["hmask"] = np.zeros((P, 1), np.float32)
        else:
            m["xh"] = np.ascontiguousarray(x[b, t0 - HALO:t0])
            m["hmask"] = np.ones((P, 1), np.float32)
        m["mem"] = np.ascontiguousarray(mem[b])
        in_maps.append(m)
    return in_maps


_PROG_CACHE = {}


def _run(inputs, NT=8, stages="all", trace=False):
    key = (NT, stages)
    if key not in _PROG_CACHE:
        _PROG_CACHE[key] = build_program(NT, stages)
    nc = _PROG_CACHE[key]
    in_maps = _pack_inputs(inputs, NT)
    res = run_bass_kernel_spmd(nc, in_maps, core_ids=list(range(NCORES)), trace=trace)
    per = NT * TT
    out = np.zeros((2, 4 * per, D), np.float32)
    for i in range(NCORES):
        b, q = i // 4, i % 4
        out[b, q * per:(q + 1) * per] = res.results[i]["y"]
    return out, res


def kernel(**inputs):
    out, _ = _run(inputs, NT=SEQ // 4 // TT, stages="all")
    return out
```

```python
import contextlib
import os
import numpy as np
import concourse.bass as bass
import concourse.mybir as mybir
from concourse.bass_utils import run_bass_kernel_spmd

F32 = mybir.dt.float32
BF16 = mybir.dt.bfloat16
AF = mybir.ActivationFunctionType
ALU = mybir.AluOpType

P = 128
D = 1024
DC = 8
DFF = 2816
FC = 22
TT = 512
NMEM = 256
HALO = 32
CW = 31
NCORES = 8
SEQ = 16384
NSLOT = 6
SLOT_ELEMS = 4096
EPS_RMS = 1e-6
EPS_LN = 1e-5

V_G_FFN1, V_G_MIX, V_G_XAT, V_G_MEM, V_G_FFN2, V_G_FIN = 0, 8, 16, 24, 32, 40
V_BIN = 48
V_CONVB, V_CLNG, V_CLNB = 96, 104, 112
V_CONVW = 120
V_SLNG = 120 + CW * 8
V_SLNB = V_SLNG + 8
NV = V_SLNB + 8


class Buf:
    __slots__ = ("w", "r", "const")

    def __init__(self, const=False):
        self.w = None
        self.r = []
        self.const = const


class Chan:
    def __init__(self, sem):
        self.sem = sem
        self.count = 0


class Sched:
    ENG = ("pe", "act", "dve", "pool", "sp")

    def __init__(self):
        self.q = {e: [] for e in self.ENG}
        self.label = ""
        self.labels = {e: [] for e in self.ENG}

    def add(self, eng, fn, reads=(), writes=(), chan=None):
        raw, other = [], []
        for b in reads:
            if b.w is not None:
                raw.append(b.w)
        for b in writes:
            if b.w is not None:
                other.append(b.w)
            other.extend(b.r)
        idx = len(self.q[eng])
        if chan is not None:
            chan.count += 16
            ev = ("d", chan, chan.count)
        else:
            ev = ("e", eng, idx)
        self.q[eng].append((fn, raw, other, chan))
        self.labels[eng].append(self.label)
        for b in reads:
            if not b.const:
                b.r.append(ev)
        for b in writes:
            b.w = ev
            b.r = []
        return ev

    def emit(self, nc, block, esem):
        waits = {e: [] for e in self.ENG}
        flagged = {e: set() for e in self.ENG}
        for E in self.ENG:
            seen_e, seen_d = {}, {}
            for idx, (fn, raw, other, chan) in enumerate(self.q[E]):
                need_e, need_d = {}, {}
                for deps, is_raw in ((raw, True), (other, False)):
                    for d in deps:
                        if d[0] == "e":
                            Pn, j = d[1], d[2]
                            if Pn == E and E == "pe":
                                continue
                            if j > need_e.get(Pn, -1):
                                need_e[Pn] = j
                        else:
                            ch, v = d[1], d[2]
                            if v > need_d.get(ch, 0):
                                need_d[ch] = v
                w = []
                for Pn, j in need_e.items():
                    if seen_e.get(Pn, -1) >= j:
                        continue
                    seen_e[Pn] = j
                    flagged[Pn].add(j)
                    w.append(("e", Pn, j))
                for ch, v in need_d.items():
                    if seen_d.get(ch, 0) >= v:
                        continue
                    seen_d[ch] = v
                    w.append(("d", ch, v))
                waits[E].append(w)
        count_at = {}
        for E in self.ENG:
            c = 0
            m = {}
            for idx in range(len(self.q[E])):
                if idx in flagged[E]:
                    c += 1
                    m[idx] = c
            count_at[E] = m

        def run(E, eng):
            for idx, (fn, raw, other, chan) in enumerate(self.q[E]):
                for w in waits[E][idx]:
                    if w[0] == "e":
                        eng.wait_ge(esem[w[1]], count_at[w[1]][w[2]])
                    else:
                        eng.wait_ge(w[1].sem, w[2])
                ins = fn(eng)
                if chan is not None:
                    ins.then_inc(chan.sem, 16)
                elif idx in flagged[E]:
                    ins.then_inc(esem[E], 1)

        @block.sync
        def _(e):
            run("sp", e)

        @block.tensor
        def _(e):
            run("pe", e)

        @block.scalar
        def _(e):
            run("act", e)

        @block.vector
        def _(e):
            run("dve", e)

        @block.gpsimd
        def _(e):
            run("pool", e)


def build_program(NT, stages="all", do_halo=True):
    nc = bass.Bass("TRN2", target_bir_lowering=False)
    S = Sched()
    es = contextlib.ExitStack()

    def dram_in(name, shape, dt=F32):
        return nc.dram_tensor(name, list(shape), dt, kind="ExternalInput").ap()

    NTOK = NT * TT
    x_d = dram_in("x", [NTOK, D])
    xh_d = dram_in("xh", [HALO, D])
    mem_d = dram_in("mem", [NMEM, D])
    vecs_d = dram_in("vecs", [P, NV])
    rows_d = dram_in("rows", [1, 1536])
    bc_d = dram_in("bc", [P, 512])
    hmask_d = dram_in("hmask", [P, 1])
    ident_d = dram_in("ident", [P, P])
    tril_d = dram_in("tril", [P, P])
    sguw_d = dram_in("sgu_w", [4, P, P])
    w_d = {
        "gu1": dram_in("ffn1_w_gu", [D, 2 * DFF]),
        "dn1": dram_in("ffn1_w_down", [DFF, D]),
        "win": dram_in("w_in", [D, 6 * D]),
        "wa": dram_in("w_a_out", [D, D]),
        "wb": dram_in("w_b_out", [D, D]),
        "wout": dram_in("w_out", [D, D]),
        "wq": dram_in("w_q", [D, D]),
        "wkv": dram_in("w_kv", [D, 2 * D]),
        "wo": dram_in("w_o", [D, D]),
        "gu2": dram_in("ffn2_w_gu", [D, 2 * DFF]),
        "dn2": dram_in("ffn2_w_down", [DFF, D]),
    }
    y_d = nc.dram_tensor("y", [NTOK, D], F32, kind="ExternalOutput").ap()

    def sb(name, shape, dt):
        return es.enter_context(nc.sbuf_tensor("sb_" + name, list(shape), dt))

    def sem(name):
        return es.enter_context(nc.semaphore(name))

    esem = {e: sem("prog_" + e) for e in ("pe", "act", "dve", "pool")}

    slice_ids = {}
    slice_list = []

    def slice_id(desc):
        if desc not in slice_ids:
            slice_ids[desc] = len(slice_list)
            slice_list.append(desc)
        return slice_ids[desc]

    def ffn_slices(tag):
        out = []
        for jj in range(11):
            out.append(("gu", "gu" + tag, jj))
        for c in range(DC):
            out.append(("down", "dn" + tag, c))
        return out

    def cols_slices(key, lo, n):
        return [("cols", key, lo + i * 512, 512) for i in range(n)]

    mixer_slices = (
        [("cols", "win", 0, 512), ("cols", "win", 1024, 512),
         ("cols", "win", 512, 512), ("cols", "win", 1536, 512)]
        + cols_slices("win", 3072, 2)
        + cols_slices("win", 2048, 2)
        + [("cols", "wb", 0, 512), ("cols", "win", 5120, 512),
           ("cols", "wb", 512, 512), ("cols", "win", 5632, 512)]
        + [("cols", "wa", 0, 512), ("cols", "win", 4096, 512),
           ("cols", "wa", 512, 512), ("cols", "win", 4608, 512)]
        + cols_slices("wout", 0, 2)
    )
    xattn_slices = cols_slices("wq", 0, 2) + cols_slices("wo", 0, 2)
    kv_slices = cols_slices("wkv", 0, 4)
    halo_slices = ffn_slices("1") + mixer_slices[:4]

    tile_slices = ffn_slices("1")
    if stages in ("xpose", "norm"):
        tile_slices = []
    if stages in ("all", "mixer", "xattn"):
        tile_slices = tile_slices + mixer_slices
    if stages in ("all", "xattn"):
        tile_slices = tile_slices + xattn_slices
    if stages == "all":
        tile_slices = tile_slices + ffn_slices("2")

    plan = []
    if stages in ("all", "xattn"):
        plan += kv_slices
    for _ in range(NT):
        plan += tile_slices
    for d_ in plan:
        slice_id(d_)
    NSL = len(slice_list)

    scratch = nc.dram_tensor("wscratch", [max(NSL, 1), P, SLOT_ELEMS], BF16, kind="Internal").ap()
    scratch_buf = [Buf() for _ in range(NSL)]
    in_scratch = [False] * NSL
    n_uses = [0] * NSL
    for d_ in plan:
        n_uses[slice_ids[d_]] += 1
    slots_t = sb("wslots", [P, NSLOT, SLOT_ELEMS], BF16)
    slot_buf = [Buf() for _ in range(NSLOT)]
    slot_chan = [Chan(sem("wslot%d" % i)) for i in range(NSLOT)]
    store_chan = [Chan(sem("wstore%d" % i)) for i in range(NSLOT)]
    cast_chan = [Chan(sem("wcast%d" % i)) for i in range(NSLOT)]

    def slice_elems(desc):
        if desc[0] == "cols":
            return 8 * desc[3]
        return {"gu": 4096, "down": FC * P}[desc[0]]

    def emit_first_fetch(sid, s):
        desc = slice_list[sid]
        kind = desc[0]
        pieces = []
        if kind == "cols":
            _, key, c0, ncols = desc
            src_ = w_d[key][:, c0:c0 + ncols].rearrange("(kc p) n -> p kc n", p=P)
            dst = slots_t[:, s, 0:8 * ncols].rearrange("p (kc n) -> p kc n", n=ncols)
            pieces.append((dst, src_))
        elif kind == "gu":
            _, key, jj = desc
            dst_all = slots_t[:, s, 0:4096].rearrange("p (kc n) -> p kc n", n=512)
            for half in range(2):
                c0 = half * DFF + jj * 256
                src_ = w_d[key][:, c0:c0 + 256].rearrange("(kc p) n -> p kc n", p=P)
                pieces.append((dst_all[:, :, half * 256:(half + 1) * 256], src_))
        else:
            _, key, c = desc
            src_ = w_d[key][:, c * P:(c + 1) * P].rearrange("(kc p) n -> p kc n", p=P)
            dst = slots_t[:, s, 0:FC * P].rearrange("p (kc n) -> p kc n", n=P)
            pieces.append((dst, src_))
        for dst, src_ in pieces:
            S.add("pool", lambda e, dst=dst, src_=src_: e.dma_start(out=dst, in_=src_),
                  writes=[slot_buf[s]], chan=cast_chan[s])
        if n_uses[sid] > 1:
            ne = slice_elems(desc)
            S.add("sp", lambda e, ne=ne: e.dma_start(out=scratch[sid, :, 0:ne], in_=slots_t[:, s, 0:ne]),
                  reads=[slot_buf[s]], writes=[scratch_buf[sid]], chan=store_chan[s])
        in_scratch[sid] = True

    class WStream:
        def __init__(self):
            self.next_load = 0
            self.next_use = 0
            self.done_upto = 0
            self.done_flags = [False] * len(plan)

        def pump(self):
            while self.next_load < len(plan) and self.next_load - NSLOT < self.done_upto:
                n = self.next_load
                sid = slice_ids[plan[n]]
                s = n % NSLOT
                desc = plan[n]
                if not in_scratch[sid]:
                    emit_first_fetch(sid, s)
                else:
                    ne = slice_elems(desc)
                    dst = slots_t[:, s, 0:ne]
                    src = scratch[sid, :, 0:ne]
                    S.add("sp", lambda e, dst=dst, src=src: e.dma_start(out=dst, in_=src),
                          reads=[scratch_buf[sid]], writes=[slot_buf[s]], chan=slot_chan[s])
                self.next_load += 1

        def next(self, desc):
            n = self.next_use
            assert plan[n] == desc, (n, plan[n], desc)
            self.next_use += 1
            self.pump()
            assert self.next_load > n
            s = n % NSLOT
            return n, s

        def done(self, n):
            self.done_flags[n] = True
            while self.done_upto < len(plan) and self.done_flags[self.done_upto]:
                self.done_upto += 1
            self.pump()

    W = WStream()

    def slot_cols(s, ncols):
        return slots_t[:, s, 0:8 * ncols].rearrange("p (kc n) -> p kc n", n=ncols)

    def slot_down(s):
        return slots_t[:, s, 0:FC * P].rearrange("p (kc n) -> p kc n", n=P)

    vecs = sb("vecs", [P, NV], F32)
    vecs_b = Buf(const=True)
    ident = sb("ident", [P, P], F32)
    ident_b = Buf(const=True)
    ones_bf = sb("ones_bf", [P, P], BF16)
    ones_b = Buf(const=True)
    hmask = sb("hmask", [P, 1], F32)
    hmask_b = Buf(const=True)
    S.add("sp", lambda e: e.dma_start(out=vecs[:, :], in_=vecs_d[:, :]), writes=[vecs_b], chan=Chan(sem("c_vecs")))
    S.add("sp", lambda e: e.dma_start(out=ident[:, :], in_=ident_d[:, :]), writes=[ident_b], chan=Chan(sem("c_ident")))
    DBG = os.environ.get("KDBG", "")
    if "B" not in DBG:
        S.add("sp", lambda e: e.dma_start(out=hmask[:, :], in_=hmask_d[:, :]), writes=[hmask_b], chan=Chan(sem("c_hmask")))
    if "A" not in DBG:
        S.add("dve", lambda e: e.memset(ones_bf[:, :], 1.0), writes=[ones_b])

    def vcol(c):
        return vecs[:, c:c + 1]

    bhalf = sb("bhalf", [P, 16], F32)
    bhalf_b = Buf(const=True)
    S.add("dve", lambda e: e.tensor_scalar(bhalf[:, :], vecs[:, V_BIN + 32:V_BIN + 48], 0.5, None, ALU.mult),
          reads=[vecs_b], writes=[bhalf_b])

    hT = sb("hT", [P, DC, TT], F32)
    hT_b = [Buf() for _ in range(DC)]
    hT_main, hT_main_b = hT, hT_b
    hTh = sb("hTh", [P, DC, HALO], F32)
    hTh_b = [Buf() for _ in range(DC)]
    xnh = sb("xnh", [P, DC, HALO], BF16)
    xnh_b = [Buf() for _ in range(DC)]
    hidh = sb("hidh", [P, FC, HALO], BF16)
    hidh_b = [Buf() for _ in range(FC)]
    xn = sb("xn", [P, DC, TT], BF16)
    xn_b = [Buf() for _ in range(DC)]
    hid = sb("hid", [P, 24, TT], BF16)
    hid_b = [Buf() for _ in range(24)]
    sq = sb("sq", [P, 4, TT], BF16)
    sq_b = [Buf() for _ in range(4)]
    rstd = sb("rstd", [P, 2, TT], F32)
    rstd_b = [Buf(), Buf()]
    tmpf = sb("tmpf", [P, 3, TT], F32)
    tmpf_b = [Buf() for _ in range(3)]
    xs = sb("xs", [P, 3, D], F32)
    xs_b = [Buf() for _ in range(3)]
    xs_chan = [Chan(sem("xs%d" % i)) for i in range(3)]
    os_chan = [Chan(sem("os%d" % i)) for i in range(4)]
    ystore_b = [Buf() for _ in range(4)]

    psum = [es.enter_context(nc.psum_tensor("ps%d" % i, [P, TT], F32)) for i in range(8)]
    psum_b = [Buf() for _ in range(8)]
    ps_ctr = [0]

    def ps_next():
        i = ps_ctr[0] % 6
        ps_ctr[0] += 1
        return psum[i], psum_b[i]

    rr = {"sq": 0, "rstd": 0, "tmpf": 0, "xs": 0, "os": 0}

    def ring(name, n):
        i = rr[name] % n
        rr[name] += 1
        return i

    def mm(out, lhsT, rhs, start, stop, reads, writes):
        S.add("pe", lambda e: e.matmul(out, lhsT, rhs, start=start, stop=stop),
              reads=reads, writes=writes)

    def load_x_tile(src_d, r0, ntok, hT=None, hT_b=None):
        if hT is None:
            hT, hT_b = hT_main, hT_main_b
        S.label = "load_x"
        nch = (ntok + P - 1) // P
        for tc in range(nch):
            n = min(P, ntok - tc * P)
            i = ring("xs", 3)
            S.add("sp", lambda e, i=i, n=n, tc=tc: e.dma_start(
                out=xs[0:n, i, :], in_=src_d[r0 + tc * P:r0 + tc * P + n, :]),
                writes=[xs_b[i]], chan=xs_chan[i])
            for g in range(2):
                pt, pb = ps_next()
                for j in range(4):
                    c = 4 * g + j
                    S.add("pe", lambda e, pt=pt, j=j, i=i, n=n, c=c: e.transpose(
                        pt[:, j * P:j * P + n], xs[0:n, i, c * P:(c + 1) * P], ident[0:n, 0:n]),
                        reads=[xs_b[i], ident_b], writes=[pb])
                for j in range(4):
                    c = 4 * g + j
                    eng = "dve"
                    if eng == "act":
                        S.add("act", lambda e, pt=pt, j=j, n=n, c=c, tc=tc: e.activation(
                            hT[:, c, tc * P:tc * P + n], pt[:, j * P:j * P + n], AF.Copy),
                            reads=[pb], writes=[hT_b[c]])
                    else:
                        S.add("dve", lambda e, pt=pt, j=j, n=n, c=c, tc=tc: e.tensor_copy(
                            hT[:, c, tc * P:tc * P + n], pt[:, j * P:j * P + n]),
                            reads=[pb], writes=[hT_b[c]])

    stat_pending = []

    def stat_add(src, src_b, c, n, delay=0):
        i = ring("sq", 4)
        S.add("act", lambda e, i=i, c=c: e.activation(sq[:, i, 0:n], src[:, c, 0:n], AF.Square),
              reads=[src_b[c]], writes=[sq_b[i]])
        stat_pending.append((i, c, n))
        while len(stat_pending) > delay:
            stat_flush_one()

    def stat_flush_one():
        i, c, n = stat_pending.pop(0)
        mm(psum[6][:, 0:n], ones_bf[:, :], sq[:, i, 0:n], c == 0, c == DC - 1,
           [sq_b[i], ones_b], [psum_b[6]])

    def stat_flush():
        while stat_pending:
            stat_flush_one()

    WARM_MM = 18

    def rmsnorm(src, src_b, gcol, n, dst, dst_b, nchunks=DC, pre=False, warm=True):
        pt, pb = psum[6], psum_b[6]
        if not pre:
            for c in range(nchunks):
                stat_add(src, src_b, c, n)
        i = ring("rstd", 2)
        t = ring("tmpf", 3)
        S.add("act", lambda e: e.activation(tmpf[:, t, 0:n], pt[:, 0:n], AF.Sqrt,
                                            bias=eps_rms[:, 0:1], scale=1.0 / D),
              reads=[pb, eps_b], writes=[tmpf_b[t]])
        S.add("dve", lambda e: e.reciprocal(rstd[:, i, 0:n], tmpf[:, t, 0:n]),
              reads=[tmpf_b[t]], writes=[rstd_b[i]])
        for c in range(nchunks):
            S.add("dve", lambda e, c=c: e.scalar_tensor_tensor(
                out=dst[:, c, 0:n], in0=src[:, c, 0:n], scalar=vcol(gcol + c),
                in1=rstd[:, i, 0:n], op0=ALU.mult, op1=ALU.mult),
                reads=[src_b[c], rstd_b[i], vecs_b], writes=[dst_b[c]])
        if warm and n == TT:
            lab = S.label
            S.label = "warm"
            for _ in range(WARM_MM):
                mm(psum[7][:, :], ones_bf[:, :], brow[:, 0, 0:TT], True, True, [ones_b, brow_b], [psum_b[7]])
            S.label = lab

    eps_rms = sb("eps_rms", [P, 2], F32)
    eps_b = Buf(const=True)
    if "A" not in DBG:
        S.add("dve", lambda e: e.memset(eps_rms[:, 0:1], EPS_RMS), writes=[eps_b])
        S.add("dve", lambda e: e.memset(eps_rms[:, 1:2], EPS_LN), writes=[eps_b])

    def ffn(tag, gcol, n, pre=False, halo=False):
        S.label = "ffn_norm"
        if pre != "done":
            rmsnorm(hT, hT_b, gcol, n, xn, xn_b, pre=pre)
        if halo:
            rmsnorm(hTh, hTh_b, gcol, HALO, xnh, xnh_b)
        S.label = "ffn_up"
        for jj in range(11):
            sn, s = W.next(("gu", "gu" + tag, jj))
            wv = slot_cols(s, 512)
            for sub in range(2):
                j = 2 * jj + sub
                pg, pgb = ps_next()
                pu, pub = ps_next()
                for k in range(DC):
                    mm(pg[:, 0:n], wv[:, k, sub * P:(sub + 1) * P], xn[:, k, 0:n],
                       k == 0, k == DC - 1, [slot_buf[s], xn_b[k]], [pgb])
                for k in range(DC):
                    mm(pu[:, 0:n], wv[:, k, 256 + sub * P:256 + (sub + 1) * P], xn[:, k, 0:n],
                       k == 0, k == DC - 1, [slot_buf[s], xn_b[k]], [pub])
                t = ring("tmpf", 3)
                S.add("act", lambda e, t=t, pg=pg: e.activation(tmpf[:, t, 0:n], pg[:, 0:n], AF.Silu),
                      reads=[pgb], writes=[tmpf_b[t]])
                S.add("dve", lambda e, t=t, pu=pu, j=j: e.tensor_tensor(
                    out=hid[:, j, 0:n], in0=pu[:, 0:n], in1=tmpf[:, t, 0:n], op=ALU.mult),
                    reads=[pub, tmpf_b[t]], writes=[hid_b[j]])
                if halo:
                    ph, phb = ps_next()
                    for k in range(DC):
                        mm(ph[:, 0:HALO], wv[:, k, sub * P:(sub + 1) * P], xnh[:, k, :],
                           k == 0, k == DC - 1, [slot_buf[s], xnh_b[k]], [phb])
                    for k in range(DC):
                        mm(ph[:, HALO:2 * HALO], wv[:, k, 256 + sub * P:256 + (sub + 1) * P], xnh[:, k, :],
                           k == 0, k == DC - 1, [slot_buf[s], xnh_b[k]], [phb])
                    t = ring("tmpf", 3)
                    S.add("act", lambda e, t=t, ph=ph: e.activation(tmpf[:, t, 0:HALO], ph[:, 0:HALO], AF.Silu),
                          reads=[phb], writes=[tmpf_b[t]])
                    S.add("dve", lambda e, t=t, ph=ph, j=j: e.tensor_tensor(
                        out=hidh[:, j, :], in0=ph[:, HALO:2 * HALO], in1=tmpf[:, t, 0:HALO], op=ALU.mult),
                        reads=[phb, tmpf_b[t]], writes=[hidh_b[j]])
            W.done(sn)
        S.label = "ffn_down"
        for c in range(DC):
            sn, s = W.next(("down", "dn" + tag, c))
            wv = slot_down(s)
            po, pob = ps_next()
            for j in range(FC):
                mm(po[:, 0:n], wv[:, j, :], hid[:, j, 0:n], j == 0, j == FC - 1,
                   [slot_buf[s], hid_b[j]], [pob])
            if halo:
                ph, phb = ps_next()
                for j in range(FC):
                    mm(ph[:, 0:HALO], wv[:, j, :], hidh[:, j, :], j == 0, j == FC - 1,
                       [slot_buf[s], hidh_b[j]], [phb])
            W.done(sn)
            S.add("dve", lambda e, c=c, po=po: e.scalar_tensor_tensor(
                out=hT[:, c, 0:n], in0=po[:, 0:n], scalar=0.5, in1=hT[:, c, 0:n],
                op0=ALU.mult, op1=ALU.add),
                reads=[pob, hT_b[c]], writes=[hT_b[c]])
            stat_add(hT, hT_b, c, n, delay=1)
            if halo:
                S.add("dve", lambda e, c=c, ph=ph: e.scalar_tensor_tensor(
                    out=hTh[:, c, :], in0=ph[:, 0:HALO], scalar=0.5, in1=hTh[:, c, :],
                    op0=ALU.mult, op1=ALU.add),
                    reads=[phb, hTh_b[c]], writes=[hTh_b[c]])
        stat_flush()

    def store_tile(src, src_b, r0):
        S.label = "store"
        for tc in range(TT // P):
            i = ring("os", 4)
            for g in range(2):
                pt, pb = ps_next()
                for j in range(4):
                    c = 4 * g + j
                    S.add("pe", lambda e, pt=pt, j=j, c=c, tc=tc: e.transpose(
                        pt[:, j * P:(j + 1) * P], src[:, c, tc * P:(tc + 1) * P], ident[:, :]),
                        reads=[src_b[c], ident_b], writes=[pb])
                if g == 0:
                    S.add("act", lambda e, pt=pt, i=i: e.activation(osb[:, i, 0:512], pt[:, :], AF.Copy),
                          reads=[pb], writes=[os_b[i]])
                else:
                    S.add("act", lambda e, pt=pt, i=i: e.activation(osb[:, i, 512:1024], pt[:, :], AF.Copy),
                          reads=[pb], writes=[os_b[i]])
            S.add("sp", lambda e, i=i, tc=tc: e.dma_start(
                out=y_d[r0 + tc * P:r0 + (tc + 1) * P, :], in_=osb[:, i, :]),
                reads=[os_b[i]], writes=[ystore_b[i]], chan=os_chan[i])


    aT = sb("aT", [P, DC, HALO + TT], BF16)
    aT_b = [Buf() for _ in range(DC)]
    acc = sb("acc", [P, DC, TT], F32)
    acc_b = [Buf() for _ in range(DC)]
    gv = sb("gv", [P, 4, D], F32)
    gv_b = [Buf() for _ in range(4)]
    t2v = gv[:, :, :].rearrange("p a (b n) -> p (a b) n", n=TT)
    osb = gv
    os_b = gv_b
    vtok = sb("vtok", [P, 4, D], BF16)
    vtok_b = [Buf() for _ in range(4)]
    bct = sb("bct", [P, 512], F32)
    bct_b = Buf(const=True)
    brow = sb("brow", [P, 2, 1024], BF16)
    brow_b = Buf(const=True)
    rowsf = sb("rowsf", [1, 1536], F32)
    rowsf_b = Buf()
    WsT = sb("WsT", [P, 4, P], BF16)
    WsT_b = Buf(const=True)
    KT = sb("KT", [P, DC, NMEM], BF16)
    KT_b = Buf(const=True)
    Vt = sb("Vt", [P, 2, D], BF16)
    Vt_b = Buf(const=True)
    lnst = sb("lnst", [P, 2, TT], F32)
    lnst_b = [Buf() for _ in range(2)]
    bst = sb("bst", [P, 4, 12], F32)
    bst_b = Buf()
    mv = sb("mv", [P, 4, 2], F32)
    mv_b = Buf()
    sdv = sb("sdv", [P, 8], F32)
    sdv_b = Buf()
    identb = sb("identb", [P, P], BF16)
    identb_b = Buf(const=True)
    dg = sb("dg", [P, 6, P], BF16)
    dg_b = [Buf() for _ in range(6)]
    rr["dg"] = 0
    B2 = sb("B2", [P, DC, P], F32)
    B2_b = Buf(const=True)

    def setup_consts():
        S.add("sp", lambda e: e.dma_start(out=bct[:, :], in_=bc_d[:, :]), writes=[bct_b], chan=Chan(sem("c_bc")))
        S.add("sp", lambda e: e.dma_start(out=rowsf[:, :], in_=rows_d[:, :]), writes=[rowsf_b], chan=Chan(sem("c_rows")))
        S.add("pool", lambda e: e.memset(brow[:, :, :], 0.0), writes=[brow_b])
        S.add("dve", lambda e: e.tensor_copy(brow[0:1, 0, :], rowsf[0:1, 0:1024]), reads=[rowsf_b, brow_b], writes=[brow_b])
        S.add("dve", lambda e: e.tensor_tensor(out=brow[0:1, 1, :], in0=rowsf[0:1, 0:1024], in1=brow[0:1, 0, :],
                                               op=ALU.subtract), reads=[rowsf_b, brow_b], writes=[brow_b])
        wv = tmpf[:, 0, :].rearrange("p (g s) -> p g s", s=P)
        S.add("sp", lambda e: e.dma_start(out=wv, in_=sguw_d.rearrange("g t s -> t g s")),
              writes=[tmpf_b[0]], chan=Chan(sem("c_sguw")))
        S.add("sp", lambda e: e.dma_start(out=tmpf[:, 1, 0:P], in_=tril_d[:, :]),
              writes=[tmpf_b[1]], chan=Chan(sem("c_tril")))
        pt, pb = ps_next()
        for g in range(4):
            S.add("dve", lambda e, g=g: e.tensor_tensor(out=wv[:, g, :], in0=wv[:, g, :], in1=tmpf[:, 1, 0:P],
                                                        op=ALU.mult),
                  reads=[tmpf_b[0], tmpf_b[1]], writes=[tmpf_b[0]])
        for g in range(4):
            S.add("pe", lambda e, g=g: e.transpose(pt[:, g * P:(g + 1) * P], wv[:, g, :], ident[:, :]),
                  reads=[tmpf_b[0], ident_b], writes=[pb])
        S.add("dve", lambda e: e.tensor_copy(WsT[:, :, :].rearrange("p g t -> p (g t)"), pt[:, :]),
              reads=[pb], writes=[WsT_b])
        S.add("dve", lambda e: e.tensor_copy(identb[:, :], ident[:, :]), reads=[ident_b], writes=[identb_b])
        p2, p2b = ps_next()
        for g in range(4):
            mm(p2[:, g * P:(g + 1) * P], ones_bf[:, :], WsT[:, g, :], True, True, [ones_b, WsT_b], [p2b])
        for c in range(DC):
            g = c // 2
            S.add("dve", lambda e, c=c, g=g: e.scalar_tensor_tensor(
                out=B2[:, c, :], in0=p2[:, g * P:(g + 1) * P], scalar=vcol(V_SLNB + c),
                in1=bct[:, g * P:(g + 1) * P], op0=ALU.mult, op1=ALU.add),
                reads=[p2b, vecs_b, bct_b], writes=[B2_b])

    def kv_prep():
        load_x_tile(mem_d, 0, NMEM)
        rmsnorm(hT, hT_b, V_G_MEM, NMEM, xn, xn_b)
        for p in range(2):
            sn, s = W.next(("cols", "wkv", p * 512, 512))
            wv = slot_cols(s, 512)
            for cc in range(4):
                c = 4 * p + cc
                pk, pkb = ps_next()
                for k in range(DC):
                    mm(pk[:, 0:NMEM], wv[:, k, cc * P:(cc + 1) * P], xn[:, k, 0:NMEM],
                       k == 0, k == DC - 1, [slot_buf[s], xn_b[k]], [pkb])
                S.add("act", lambda e, c=c, pk=pk: e.activation(KT[:, c, :], pk[:, 0:NMEM], AF.Copy),
                      reads=[pkb], writes=[KT_b])
            W.done(sn)
        for p in range(2):
            sn, s = W.next(("cols", "wkv", 1024 + p * 512, 512))
            wv = slot_cols(s, 512)
            for mc in range(2):
                pv, pvb = ps_next()
                for k in range(DC):
                    mm(pv[:, :], xn[:, k, mc * P:(mc + 1) * P], wv[:, k, :],
                       k == 0, k == DC - 1, [slot_buf[s], xn_b[k]], [pvb])
                S.add("act", lambda e, mc=mc, p=p, pv=pv: e.activation(
                    Vt[:, mc, p * 512:(p + 1) * 512], pv[:, :], AF.Copy), reads=[pvb], writes=[Vt_b])
            W.done(sn)

    def mixer_a_proj(n, col0, halo=False):
        streams = [(xn, xn_b, n, col0)]
        if halo:
            streams.append((xnh, xnh_b, HALO, 0))
        for p in range(2):
            snv, sv = W.next(("cols", "win", p * 512, 512))
            sng, sg = W.next(("cols", "win", 1024 + p * 512, 512))
            wvv, wvg = slot_cols(sv, 512), slot_cols(sg, 512)
            for cc in range(4):
                c = 4 * p + cc
                for (sx, sx_b, sn_, sc0) in streams:
                    pv, pvb = ps_next()
                    pg, pgb = ps_next()
                    for k in range(DC):
                        mm(pv[:, 0:sn_], wvv[:, k, cc * P:(cc + 1) * P], sx[:, k, 0:sn_],
                           k == 0, k == DC - 1, [slot_buf[sv], sx_b[k]], [pvb])
                    for k in range(DC):
                        mm(pg[:, 0:sn_], wvg[:, k, cc * P:(cc + 1) * P], sx[:, k, 0:sn_],
                           k == 0, k == DC - 1, [slot_buf[sg], sx_b[k]], [pgb])
                    t = ring("tmpf", 3)
                    S.add("act", lambda e, t=t, pg=pg, c=c, sn_=sn_: e.activation(
                        tmpf[:, t, 0:sn_], pg[:, 0:sn_], AF.Sigmoid, bias=vcol(V_BIN + 8 + c)),
                        reads=[pgb, vecs_b], writes=[tmpf_b[t]])
                    S.add("dve", lambda e, t=t, pv=pv, c=c, sn_=sn_, sc0=sc0: e.scalar_tensor_tensor(
                        out=aT[:, c, sc0:sc0 + sn_], in0=pv[:, 0:sn_], scalar=vcol(V_BIN + c),
                        in1=tmpf[:, t, 0:sn_], op0=ALU.add, op1=ALU.mult),
                        reads=[pvb, tmpf_b[t], vecs_b], writes=[aT_b[c]])
                if halo:
                    S.add("dve", lambda e, c=c: e.tensor_scalar(aT[:, c, 0:HALO], aT[:, c, 0:HALO],
                                                                hmask[:, 0:1], None, ALU.mult),
                          reads=[aT_b[c], hmask_b], writes=[aT_b[c]])
            W.done(snv)
            W.done(sng)

    def mixer(halo=False):
        n = TT
        S.label = "mix_norm"
        rmsnorm(hT, hT_b, V_G_MIX, n, xn, xn_b, pre=True)
        if halo:
            rmsnorm(hTh, hTh_b, V_G_MIX, HALO, xnh, xnh_b)
        S.label = "mix_aproj"
        mixer_a_proj(n, HALO, halo=halo)
        S.label = "mix_v"
        for p in range(2):
            sn, s = W.next(("cols", "win", 3072 + p * 512, 512))
            wv = slot_cols(s, 512)
            for tc in range(4):
                pv, pvb = ps_next()
                for k in range(DC):
                    mm(pv[:, :], xn[:, k, tc * P:(tc + 1) * P], wv[:, k, :], k == 0, False,
                       [slot_buf[s], xn_b[k]], [pvb])
                mm(pv[:, :], ones_bf[:, :], brow[:, 0, p * 512:(p + 1) * 512], False, False,
                   [ones_b, brow_b], [pvb])
                mm(pv[:, :], ones_bf[:, :], brow[:, 1, p * 512:(p + 1) * 512], False, True,
                   [ones_b, brow_b], [pvb])
                S.add("act", lambda e, pv=pv, tc=tc, p=p: e.activation(
                    gv[:, tc, p * 512:(p + 1) * 512], pv[:, :], AF.Gelu_apprx_tanh),
                    reads=[pvb], writes=[gv_b[tc]])
            W.done(sn)
        for tc in range(4):
            for hh in range(2):
                S.add("dve", lambda e, tc=tc, hh=hh: e.bn_stats(
                    bst[:, tc, hh * 6:(hh + 1) * 6], gv[:, tc, hh * 512:(hh + 1) * 512]),
                    reads=[gv_b[tc]], writes=[bst_b])
            S.add("dve", lambda e, tc=tc: e.bn_aggr(
                mv[:, tc, :], bst[:, tc, :].rearrange("p (a b) -> p a b", b=6)),
                reads=[bst_b], writes=[mv_b])
        S.add("act", lambda e: e.activation(sdv[:, 0:4], mv[:, :, 1], AF.Sqrt, bias=eps_rms[:, 1:2]),
              reads=[mv_b, eps_b], writes=[sdv_b])
        S.add("dve", lambda e: e.reciprocal(sdv[:, 4:8], sdv[:, 0:4]), reads=[sdv_b], writes=[sdv_b])
        for tc in range(4):
            S.add("dve", lambda e, tc=tc: e.tensor_scalar(
                vtok[:, tc, :], gv[:, tc, :], mv[:, tc, 0:1], sdv[:, 4 + tc:5 + tc], ALU.subtract, ALU.mult),
                reads=[gv_b[tc], mv_b, sdv_b], writes=[vtok_b[tc]])
        S.label = "mix_u"
        for p in range(2):
            sn, s = W.next(("cols", "win", 2048 + p * 512, 512))
            wv = slot_cols(s, 512)
            for cc in range(4):
                c = 4 * p + cc
                pu, pub = ps_next()
                for k in range(DC):
                    mm(pu[:, :], wv[:, k, cc * P:(cc + 1) * P], xn[:, k, :],
                       k == 0, k == DC - 1, [slot_buf[s], xn_b[k]], [pub])
                S.add("act", lambda e, pu=pu, c=c: e.activation(
                    hid[:, c, :], pu[:, :], AF.Gelu_apprx_tanh, bias=vcol(V_BIN + 16 + c)),
                    reads=[pub, vecs_b], writes=[hid_b[c]])
            W.done(sn)
        S.label = "mix_conv"
        ps_s, ps_sb = psum[6], psum_b[6]
        ps_q, ps_qb = psum[7], psum_b[7]
        cstat_pending = []

        def cstat_flush_one():
            j1, j2, cj = cstat_pending.pop(0)
            mm(ps_s[:, :], ones_bf[:, :], sq[:, j1, :], cj == 0, cj == DC - 1, [sq_b[j1], ones_b], [ps_sb])
            mm(ps_q[:, :], ones_bf[:, :], sq[:, j2, :], cj == 0, cj == DC - 1, [sq_b[j2], ones_b], [ps_qb])
        def conv_builds(c_lo, c_hi, act_chunk):
            for c in range(c_lo, c_hi):
                for k in range(CW):
                    r = ring("dg", 6)
                    if k % 3 == 0 or c == act_chunk:
                        S.add("act", lambda e, r=r, k=k, c=c: e.activation(
                            dg[:, r, :], identb[:, :], AF.Copy, scale=vcol(V_CONVW + k * 8 + c)),
                            reads=[identb_b, vecs_b], writes=[dg_b[r]])
                    else:
                        S.add("dve", lambda e, r=r, k=k, c=c: e.tensor_scalar(
                            dg[:, r, :], identb[:, :], vcol(V_CONVW + k * 8 + c), None, ALU.mult),
                            reads=[identb_b, vecs_b], writes=[dg_b[r]])
                    yield r

        builds = conv_builds(0, 4, -1)
        ready = []
        for _ in range(5):
            ready.append(next(builds))
        for c in range(0, 4):
            pc, pcb = ps_next()
            for k in range(CW):
                nb = next(builds, None)
                if nb is not None:
                    ready.append(nb)
                r = ready.pop(0)
                mm(pc[:, :], dg[:, r, :], aT[:, c, 2 + k:2 + k + TT], k == 0, k == CW - 1,
                   [dg_b[r], aT_b[c]], [pcb])
            S.add("act", lambda e, pc=pc, c=c: e.activation(acc[:, c, :], pc[:, :], AF.Identity,
                                                            bias=vcol(V_CONVB + c)),
                  reads=[pcb, vecs_b], writes=[acc_b[c]])
            i1 = ring("sq", 4)
            S.add("act", lambda e, i=i1, pc=pc, c=c: e.activation(sq[:, i, :], pc[:, :], AF.Identity,
                                                                  bias=vcol(V_CONVB + c)),
                  reads=[pcb, vecs_b], writes=[sq_b[i1]])
            i2 = ring("sq", 4)
            S.add("act", lambda e, i=i2, pc=pc, c=c: e.activation(sq[:, i, :], pc[:, :], AF.Square,
                                                                  bias=vcol(V_CONVB + c)),
                  reads=[pcb, vecs_b], writes=[sq_b[i2]])
            cstat_pending.append((i1, i2, c))
            if len(cstat_pending) > 1:
                cstat_flush_one()
        S.label = "mix_sgu"
        for c in range(DC):
            pm, pmb = ps_next()
            g = c // 2
            for tc in range(4):
                mm(pm[:, tc * P:(tc + 1) * P], vtok[:, tc, c * P:(c + 1) * P], WsT[:, g, :], True, True,
                   [vtok_b[tc], WsT_b], [pmb])
            t = ring("tmpf", 3)
            S.add("dve", lambda e, t=t, pm=pm, c=c: e.scalar_tensor_tensor(
                out=tmpf[:, t, :].rearrange("p (a b) -> p a b", b=P),
                in0=pm[:, :].rearrange("p (a b) -> p a b", b=P), scalar=vcol(V_SLNG + c),
                in1=B2[:, c, :].unsqueeze(1).broadcast_to([P, 4, P]), op0=ALU.mult, op1=ALU.add),
                reads=[pmb, vecs_b, B2_b], writes=[tmpf_b[t]])
            S.add("dve", lambda e, t=t, c=c: e.tensor_tensor(
                out=hid[:, 16 + c, :], in0=tmpf[:, t, :], in1=hid[:, c, :], op=ALU.mult),
                reads=[tmpf_b[t], hid_b[c]], writes=[hid_b[16 + c]])
        S.label = "mix_conv"
        builds = conv_builds(4, DC, 4)
        ready = []
        for _ in range(5):
            ready.append(next(builds))
        for c in range(4, DC):
            pc, pcb = ps_next()
            for k in range(CW):
                nb = next(builds, None)
                if nb is not None:
                    ready.append(nb)
                r = ready.pop(0)
                mm(pc[:, :], dg[:, r, :], aT[:, c, 2 + k:2 + k + TT], k == 0, k == CW - 1,
                   [dg_b[r], aT_b[c]], [pcb])
            S.add("act", lambda e, pc=pc, c=c: e.activation(acc[:, c, :], pc[:, :], AF.Identity,
                                                            bias=vcol(V_CONVB + c)),
                  reads=[pcb, vecs_b], writes=[acc_b[c]])
            i1 = ring("sq", 4)
            S.add("act", lambda e, i=i1, pc=pc, c=c: e.activation(sq[:, i, :], pc[:, :], AF.Identity,
                                                                  bias=vcol(V_CONVB + c)),
                  reads=[pcb, vecs_b], writes=[sq_b[i1]])
            i2 = ring("sq", 4)
            S.add("act", lambda e, i=i2, pc=pc, c=c: e.activation(sq[:, i, :], pc[:, :], AF.Square,
                                                                  bias=vcol(V_CONVB + c)),
                  reads=[pcb, vecs_b], writes=[sq_b[i2]])
            cstat_pending.append((i1, i2, c))
            if len(cstat_pending) > 1:
                cstat_flush_one()
        while cstat_pending:
            cstat_flush_one()
        S.label = "mix_convln"
        S.add("dve", lambda e: e.tensor_scalar(lnst[:, 0, :], ps_s[:, :], 1.0 / D, None, ALU.mult),
              reads=[ps_sb], writes=[lnst_b[0]])
        S.add("dve", lambda e: e.tensor_tensor(out=lnst[:, 1, :], in0=lnst[:, 0, :], in1=lnst[:, 0, :],
                                               op=ALU.mult), reads=[lnst_b[0]], writes=[lnst_b[1]])
        S.add("dve", lambda e: e.scalar_tensor_tensor(
            out=lnst[:, 1, :], in0=ps_q[:, :], scalar=1.0 / D, in1=lnst[:, 1, :],
            op0=ALU.mult, op1=ALU.subtract), reads=[ps_qb, lnst_b[1]], writes=[lnst_b[1]])
        tq = ring("tmpf", 3)
        S.add("act", lambda e: e.activation(tmpf[:, tq, :], lnst[:, 1, :], AF.Sqrt, bias=eps_rms[:, 1:2]),
              reads=[lnst_b[1], eps_b], writes=[tmpf_b[tq]])
        S.add("dve", lambda e: e.reciprocal(lnst[:, 1, :], tmpf[:, tq, :]), reads=[tmpf_b[tq]], writes=[lnst_b[1]])
        S.add("dve", lambda e: e.tensor_tensor(out=lnst[:, 0, :], in0=lnst[:, 0, :], in1=lnst[:, 1, :],
                                               op=ALU.mult), reads=[lnst_b[0], lnst_b[1]], writes=[lnst_b[0]])
        for c in range(DC):
            S.add("pool", lambda e, c=c: e.tensor_tensor(out=acc[:, c, :], in0=acc[:, c, :], in1=lnst[:, 1, :],
                                                         op=ALU.mult),
                  reads=[acc_b[c], lnst_b[1]], writes=[acc_b[c]])

        def convln_chunk(c):
            S.add("dve", lambda e, c=c: e.tensor_tensor(out=acc[:, c, :], in0=acc[:, c, :], in1=lnst[:, 0, :],
                                                        op=ALU.subtract),
                  reads=[acc_b[c], lnst_b[0]], writes=[acc_b[c]])
            S.add("act", lambda e, c=c: e.activation(hid[:, 8 + c, :], acc[:, c, :], AF.Silu,
                                                     bias=vcol(V_CLNB + c), scale=vcol(V_CLNG + c)),
                  reads=[acc_b[c], vecs_b], writes=[hid_b[8 + c]])
        S.label = "mix_wb"
        for p in range(2):
            snb, sbb = W.next(("cols", "wb", p * 512, 512))
            sng, sg = W.next(("cols", "win", 5120 + p * 512, 512))
            wvb, wvg = slot_cols(sbb, 512), slot_cols(sg, 512)
            pgs = []
            for cc in range(4):
                pg, pgb = ps_next()
                for k in range(DC):
                    mm(pg[:, :], wvg[:, k, cc * P:(cc + 1) * P], xn[:, k, :],
                       k == 0, k == DC - 1, [slot_buf[sg], xn_b[k]], [pgb])
                pgs.append((pg, pgb))
            for cc in range(4):
                c = 4 * p + cc
                pg, pgb = pgs[cc]
                py, pyb = ps_next()
                for k in range(DC):
                    mm(py[:, :], wvb[:, k, cc * P:(cc + 1) * P], hid[:, 16 + k, :],
                       k == 0, k == DC - 1, [slot_buf[sbb], hid_b[16 + k]], [pyb])
                S.label = "mix_convln"
                convln_chunk(c)
                S.label = "mix_wb"
                t = ring("tmpf", 3)
                S.add("act", lambda e, t=t, pg=pg, c=c: e.activation(
                    tmpf[:, t, :], pg[:, :], AF.Tanh, bias=bhalf[:, 8 + c:9 + c], scale=0.5),
                    reads=[pgb, bhalf_b], writes=[tmpf_b[t]])
                S.add("dve", lambda e, t=t, py=py, c=c: e.scalar_tensor_tensor(
                    out=t2v[:, c, :], in0=tmpf[:, t, :], scalar=1.0, in1=py[:, :],
                    op0=ALU.add, op1=ALU.mult),
                    reads=[pyb, tmpf_b[t]], writes=[gv_b[c // 2]])
            W.done(snb)
            W.done(sng)
        S.label = "mix_wa"
        for p in range(2):
            sna, sa = W.next(("cols", "wa", p * 512, 512))
            sng, sg = W.next(("cols", "win", 4096 + p * 512, 512))
            wva, wvg = slot_cols(sa, 512), slot_cols(sg, 512)
            for cc in range(4):
                c = 4 * p + cc
                py, pyb = ps_next()
                pg, pgb = ps_next()
                for k in range(DC):
                    mm(py[:, :], wva[:, k, cc * P:(cc + 1) * P], hid[:, 8 + k, :],
                       k == 0, k == DC - 1, [slot_buf[sa], hid_b[8 + k]], [pyb])
                for k in range(DC):
                    mm(pg[:, :], wvg[:, k, cc * P:(cc + 1) * P], xn[:, k, :],
                       k == 0, k == DC - 1, [slot_buf[sg], xn_b[k]], [pgb])
                t = ring("tmpf", 3)
                S.add("act", lambda e, t=t, pg=pg, c=c: e.activation(
                    tmpf[:, t, :], pg[:, :], AF.Tanh, bias=bhalf[:, c:c + 1], scale=0.5),
                    reads=[pgb, bhalf_b], writes=[tmpf_b[t]])
                S.add("dve", lambda e, t=t, py=py: e.scalar_tensor_tensor(
                    out=tmpf[:, t, :], in0=tmpf[:, t, :], scalar=1.0, in1=py[:, :],
                    op0=ALU.add, op1=ALU.mult),
                    reads=[pyb, tmpf_b[t]], writes=[tmpf_b[t]])
                S.add("pool", lambda e, t=t, c=c: e.tensor_tensor(
                    out=hid[:, c, :], in0=tmpf[:, t, :], in1=t2v[:, c, :], op=ALU.add),
                    reads=[tmpf_b[t], gv_b[c // 2]], writes=[hid_b[c]])
            W.done(sna)
            W.done(sng)
        S.label = "mix_wout"
        proj_residual("wout", 0, scale=0.5)
        for c in range(DC):
            S.add("pool", lambda e, c=c: e.tensor_copy(aT[:, c, 0:HALO], aT[:, c, TT:TT + HALO]),
                  reads=[aT_b[c]], writes=[aT_b[c]])

    def proj_residual(key, src0, scale=1.0):
        for p in range(2):
            sn, s = W.next(("cols", key, p * 512, 512))
            wv = slot_cols(s, 512)
            for cc in range(4):
                c = 4 * p + cc
                po, pob = ps_next()
                for k in range(DC):
                    mm(po[:, :], wv[:, k, cc * P:(cc + 1) * P], hid[:, src0 + k, :],
                       k == 0, k == DC - 1, [slot_buf[s], hid_b[src0 + k]], [pob])
                S.add("dve", lambda e, c=c, po=po: e.scalar_tensor_tensor(
                    out=hT[:, c, :], in0=po[:, :], scalar=scale, in1=hT[:, c, :],
                    op0=ALU.mult, op1=ALU.add),
                    reads=[pob, hT_b[c]], writes=[hT_b[c]])
                stat_add(hT, hT_b, c, TT, delay=2)
            W.done(sn)
        stat_flush()

    def xattn():
        n = TT
        S.label = "xat_norm"
        rmsnorm(hT, hT_b, V_G_XAT, n, xn, xn_b, pre=True)
        S.label = "xat_q"
        for p in range(2):
            sn, s = W.next(("cols", "wq", p * 512, 512))
            wv = slot_cols(s, 512)
            for cc in range(4):
                c = 4 * p + cc
                pq, pqb = ps_next()
                for k in range(DC):
                    mm(pq[:, :], wv[:, k, cc * P:(cc + 1) * P], xn[:, k, :],
                       k == 0, k == DC - 1, [slot_buf[s], xn_b[k]], [pqb])
                S.add("act", lambda e, c=c, pq=pq: e.activation(hid[:, 8 + c, :], pq[:, :], AF.Copy,
                                                                scale=1.0 / 16.0),
                      reads=[pqb], writes=[hid_b[8 + c]])
            W.done(sn)
        S.label = "xat_attn"

        def head_scores(h):
            for mc in range(2):
                psc, pscb = ps_next()
                for dc in range(2):
                    mm(psc[:, :], KT[:, 2 * h + dc, mc * P:(mc + 1) * P], hid[:, 8 + 2 * h + dc, :],
                       dc == 0, dc == 1, [KT_b, hid_b[8 + 2 * h + dc]], [pscb])
                S.add("act", lambda e, psc=psc, h=h, mc=mc: e.activation(
                    hid[:, 16 + 2 * h + mc, :], psc[:, :], AF.Exp),
                    reads=[pscb], writes=[hid_b[16 + 2 * h + mc]])

        def head_rest(h):
            pss, pssb = ps_next()
            for mc in range(2):
                mm(pss[:, :], ones_bf[:, :], hid[:, 16 + 2 * h + mc, :], mc == 0, mc == 1,
                   [ones_b, hid_b[16 + 2 * h + mc]], [pssb])
            i = ring("rstd", 2)
            S.add("dve", lambda e, i=i, pss=pss: e.reciprocal(rstd[:, i, :], pss[:, :]),
                  reads=[pssb], writes=[rstd_b[i]])
            for dc in range(2):
                po, pob = ps_next()
                for mc in range(2):
                    mm(po[:, :], Vt[:, mc, (2 * h + dc) * P:(2 * h + dc + 1) * P], hid[:, 16 + 2 * h + mc, :],
                       mc == 0, mc == 1, [Vt_b, hid_b[16 + 2 * h + mc]], [pob])
                S.add("dve", lambda e, i=i, po=po, h=h, dc=dc: e.tensor_tensor(
                    out=hid[:, 2 * h + dc, :], in0=po[:, :], in1=rstd[:, i, :], op=ALU.mult),
                    reads=[pob, rstd_b[i]], writes=[hid_b[2 * h + dc]])

        head_scores(0)
        for h in range(4):
            if h + 1 < 4:
                head_scores(h + 1)
            head_rest(h)
        S.label = "xat_wo"
        proj_residual("wo", 0)

    full = stages in ("all", "mixer", "xattn")
    if full:
        setup_consts()
    if stages in ("all", "xattn"):
        kv_prep()
    fuse_halo = do_halo and full
    if fuse_halo:
        load_x_tile(xh_d, 0, HALO, hTh, hTh_b)
    def tile_head(it):
        load_x_tile(x_d, it * TT, TT)
        S.label = "ffn_norm"
        rmsnorm(hT, hT_b, V_G_FFN1, TT, xn, xn_b)

    for it in range(NT):
        if stages != "all":
            load_x_tile(x_d, it * TT, TT)
        if stages == "xpose":
            store_tile(hT, hT_b, it * TT)
            continue
        if stages == "norm":
            rmsnorm(hT, hT_b, V_G_FFN1, TT, hT, hT_b)
            store_tile(hT, hT_b, it * TT)
            continue
        if stages != "all":
            ffn("1", V_G_FFN1, TT, halo=(fuse_halo and it == 0))
        else:
            if it == 0:
                tile_head(0)
            ffn("1", V_G_FFN1, TT, pre="done", halo=(fuse_halo and it == 0))
        if stages == "ffn1":
            store_tile(hT, hT_b, it * TT)
            continue
        mixer(halo=(fuse_halo and it == 0))
        if stages == "mixer":
            store_tile(hT, hT_b, it * TT)
            continue
        xattn()
        if stages == "xattn":
            store_tile(hT, hT_b, it * TT)
            continue
        ffn("2", V_G_FFN2, TT, pre=True)
        S.label = "final_norm"
        rmsnorm(hT, hT_b, V_G_FIN, TT, acc, acc_b, pre=True)
        if it + 1 < NT:
            tile_head(it + 1)
        store_tile(acc, acc_b, it * TT)

    S.add("sp", lambda e: e.nop(), reads=ystore_b)

    with nc.Block() as block:
        S.emit(nc, block, esem)
    es.close()
    nc._pe_labels = S.labels["pe"]
    return nc


def _pack_inputs(inputs, NT):
    f = lambda a: np.ascontiguousarray(np.asarray(a, dtype=np.float32))
    x = f(inputs["x"])
    mem = f(inputs["mem"])

    def fm(v):
        return f(v).reshape(-1, P).T

    vecs = np.zeros((P, NV), np.float32)
    vecs[:, V_G_FFN1:V_G_FFN1 + 8] = fm(inputs["ffn1_norm"][0])
    vecs[:, V_G_MIX:V_G_MIX + 8] = fm(inputs["mix_norm"][0])
    vecs[:, V_G_XAT:V_G_XAT + 8] = fm(inputs["xattn_norm"][0])
    vecs[:, V_G_MEM:V_G_MEM + 8] = fm(inputs["mem_norm"][0])
    vecs[:, V_G_FFN2:V_G_FFN2 + 8] = fm(inputs["ffn2_norm"][0])
    vecs[:, V_G_FIN:V_G_FIN + 8] = fm(inputs["final_norm"])
    vecs[:, V_BIN:V_BIN + 48] = fm(inputs["b_in"][0])
    vecs[:, V_CONVB:V_CONVB + 8] = fm(inputs["conv_b"][0])
    vecs[:, V_CLNG:V_CLNG + 8] = fm(inputs["conv_ln_g"][0])
    vecs[:, V_CLNB:V_CLNB + 8] = fm(inputs["conv_ln_b"][0])
    cw = f(inputs["conv_w"][0])
    vecs[:, V_CONVW:V_CONVW + CW * 8] = cw.reshape(CW, 8, P).transpose(2, 0, 1).reshape(P, CW * 8)
    vecs[:, V_SLNG:V_SLNG + 8] = fm(inputs["sgu_ln_g"][0])
    vecs[:, V_SLNB:V_SLNB + 8] = fm(inputs["sgu_ln_b"][0])
    rows = np.concatenate([f(inputs["b_in"][0])[3072:4096], f(inputs["sgu_b"][0]).reshape(-1)])[None, :]
    bc = np.broadcast_to(f(inputs["sgu_b"][0]).reshape(1, -1), (P, 512))
    shared = {
        "vecs": vecs, "rows": np.ascontiguousarray(rows), "bc": np.ascontiguousarray(bc),
        "ident": np.eye(P, dtype=np.float32),
        "tril": np.tril(np.ones((P, P), np.float32)),
        "sgu_w": f(inputs["sgu_w"][0]),
        "ffn1_w_gu": f(inputs["ffn1_w_gu"][0]), "ffn1_w_down": f(inputs["ffn1_w_down"][0]),
        "w_in": f(inputs["w_in"][0]), "w_a_out": f(inputs["w_a_out"][0]),
        "w_b_out": f(inputs["w_b_out"][0]), "w_out": f(inputs["w_out"][0]),
        "w_q": f(inputs["w_q"][0]), "w_kv": f(inputs["w_kv"][0]), "w_o": f(inputs["w_o"][0]),
        "ffn2_w_gu": f(inputs["ffn2_w_gu"][0]), "ffn2_w_down": f(inputs["ffn2_w_down"][0]),
    }
    per = NT * TT
    in_maps = []
    for i in range(NCORES):
        b, q = i // 4, i % 4
        t0 = q * per
        m = dict(shared)
        m["x"] = np.ascontiguousarray(x[b, t0:t0 + per])
        if q == 0:
            m["xh"] = np.zeros((HALO, D), np.float32)
            m["hmask"] = np.zeros((P, 1), np.float32)
        else:
            m["xh"] = np.ascontiguousarray(x[b, t0 - HALO:t0])
            m["hmask"] = np.ones((P, 1), np.float32)
        m["mem"] = np.ascontiguousarray(mem[b])
        in_maps.append(m)
    return in_maps


_PROG_CACHE = {}


def _run(inputs, NT=8, stages="all", trace=False):
    key = (NT, stages)
    if key not in _PROG_CACHE:
        _PROG_CACHE[key] = build_program(NT, stages)
    nc = _PROG_CACHE[key]
    in_maps = _pack_inputs(inputs, NT)
    res = run_bass_kernel_spmd(nc, in_maps, core_ids=list(range(NCORES)), trace=trace)
    per = NT * TT
    out = np.zeros((2, 4 * per, D), np.float32)
    for i in range(NCORES):
        b, q = i // 4, i % 4
        out[b, q * per:(q + 1) * per] = res.results[i]["y"]
    return out, res


def kernel(**inputs):
    out, _ = _run(inputs, NT=SEQ // 4 // TT, stages="all")
    return out
```

```python
import contextlib
import os
import numpy as np
import concourse.bass as bass
import concourse.mybir as mybir
from concourse.bass_utils import run_bass_kernel_spmd

F32 = mybir.dt.float32
BF16 = mybir.dt.bfloat16
AF = mybir.ActivationFunctionType
ALU = mybir.AluOpType

P = 128
D = 1024
DC = 8
DFF = 2816
FC = 22
TT = 512
NMEM = 256
HALO = 32
CW = 31
NCORES = 8
SEQ = 16384
NSLOT = 6
SLOT_ELEMS = 4096
EPS_RMS = 1e-6
EPS_LN = 1e-5

V_G_FFN1, V_G_MIX, V_G_XAT, V_G_MEM, V_G_FFN2, V_G_FIN = 0, 8, 16, 24, 32, 40
V_BIN = 48
V_CONVB, V_CLNG, V_CLNB = 96, 104, 112
V_CONVW = 120
V_SLNG = 120 + CW * 8
V_SLNB = V_SLNG + 8
NV = V_SLNB + 8


class Buf:
    __slots__ = ("w", "r", "const")

    def __init__(self, const=False):
        self.w = None
        self.r = []
        self.const = const


class Chan:
    def __init__(self, sem):
        self.sem = sem
        self.count = 0


class Sched:
    ENG = ("pe", "act", "dve", "pool", "sp")

    def __init__(self):
        self.q = {e: [] for e in self.ENG}
        self.label = ""
        self.labels = {e: [] for e in self.ENG}

    def add(self, eng, fn, reads=(), writes=(), chan=None):
        raw, other = [], []
        for b in reads:
            if b.w is not None:
                raw.append(b.w)
        for b in writes:
            if b.w is not None:
                other.append(b.w)
            other.extend(b.r)
        idx = len(self.q[eng])
        if chan is not None:
            chan.count += 16
            ev = ("d", chan, chan.count)
        else:
            ev = ("e", eng, idx)
        self.q[eng].append((fn, raw, other, chan))
        self.labels[eng].append(self.label)
        for b in reads:
            if not b.const:
                b.r.append(ev)
        for b in writes:
            b.w = ev
            b.r = []
        return ev

    def emit(self, nc, block, esem):
        waits = {e: [] for e in self.ENG}
        flagged = {e: set() for e in self.ENG}
        for E in self.ENG:
            seen_e, seen_d = {}, {}
            for idx, (fn, raw, other, chan) in enumerate(self.q[E]):
                need_e, need_d = {}, {}
                for deps, is_raw in ((raw, True), (other, False)):
                    for d in deps:
                        if d[0] == "e":
                            Pn, j = d[1], d[2]
                            if Pn == E and E == "pe":
                                continue
                            if j > need_e.get(Pn, -1):
                                need_e[Pn] = j
                        else:
                            ch, v = d[1], d[2]
                            if v > need_d.get(ch, 0):
                                need_d[ch] = v
                w = []
                for Pn, j in need_e.items():
                    if seen_e.get(Pn, -1) >= j:
                        continue
                    seen_e[Pn] = j
                    flagged[Pn].add(j)
                    w.append(("e", Pn, j))
                for ch, v in need_d.items():
                    if seen_d.get(ch, 0) >= v:
                        continue
                    seen_d[ch] = v
                    w.append(("d", ch, v))
                waits[E].append(w)
        count_at = {}
        for E in self.ENG:
            c = 0
            m = {}
            for idx in range(len(self.q[E])):
                if idx in flagged[E]:
                    c += 1
                    m[idx] = c
            count_at[E] = m

        def run(E, eng):
            for idx, (fn, raw, other, chan) in enumerate(self.q[E]):
                for w in waits[E][idx]:
                    if w[0] == "e":
                        eng.wait_ge(esem[w[1]], count_at[w[1]][w[2]])
                    else:
                        eng.wait_ge(w[1].sem, w[2])
                ins = fn(eng)
                if chan is not None:
                    ins.then_inc(chan.sem, 16)
                elif idx in flagged[E]:
                    ins.then_inc(esem[E], 1)

        @block.sync
        def _(e):
            run("sp", e)

        @block.tensor
        def _(e):
            run("pe", e)

        @block.scalar
        def _(e):
            run("act", e)

        @block.vector
        def _(e):
            run("dve", e)

        @block.gpsimd
        def _(e):
            run("pool", e)


def build_program(NT, stages="all", do_halo=True):
    nc = bass.Bass("TRN2", target_bir_lowering=False)
    S = Sched()
    es = contextlib.ExitStack()

    def dram_in(name, shape, dt=F32):
        return nc.dram_tensor(name, list(shape), dt, kind="ExternalInput").ap()

    NTOK = NT * TT
    x_d = dram_in("x", [NTOK, D])
    xh_d = dram_in("xh", [HALO, D])
    mem_d = dram_in("mem", [NMEM, D])
    vecs_d = dram_in("vecs", [P, NV])
    rows_d = dram_in("rows", [1, 1536])
    bc_d = dram_in("bc", [P, 512])
    hmask_d = dram_in("hmask", [P, 1])
    ident_d = dram_in("ident", [P, P])
    tril_d = dram_in("tril", [P, P])
    sguw_d = dram_in("sgu_w", [4, P, P])
    w_d = {
        "gu1": dram_in("ffn1_w_gu", [D, 2 * DFF]),
        "dn1": dram_in("ffn1_w_down", [DFF, D]),
        "win": dram_in("w_in", [D, 6 * D]),
        "wa": dram_in("w_a_out", [D, D]),
        "wb": dram_in("w_b_out", [D, D]),
        "wout": dram_in("w_out", [D, D]),
        "wq": dram_in("w_q", [D, D]),
        "wkv": dram_in("w_kv", [D, 2 * D]),
        "wo": dram_in("w_o", [D, D]),
        "gu2": dram_in("ffn2_w_gu", [D, 2 * DFF]),
        "dn2": dram_in("ffn2_w_down", [DFF, D]),
    }
    y_d = nc.dram_tensor("y", [NTOK, D], F32, kind="ExternalOutput").ap()

    def sb(name, shape, dt):
        return es.enter_context(nc.sbuf_tensor("sb_" + name, list(shape), dt))

    def sem(name):
        return es.enter_context(nc.semaphore(name))

    esem = {e: sem("prog_" + e) for e in ("pe", "act", "dve", "pool")}

    slice_ids = {}
    slice_list = []

    def slice_id(desc):
        if desc not in slice_ids:
            slice_ids[desc] = len(slice_list)
            slice_list.append(desc)
        return slice_ids[desc]

    def ffn_slices(tag):
        out = []
        for jj in range(11):
            out.append(("gu", "gu" + tag, jj))
        for c in range(DC):
            out.append(("down", "dn" + tag, c))
        return out

    def cols_slices(key, lo, n):
        return [("cols", key, lo + i * 512, 512) for i in range(n)]

    mixer_slices = (
        [("cols", "win", 0, 512), ("cols", "win", 1024, 512),
         ("cols", "win", 512, 512), ("cols", "win", 1536, 512)]
        + cols_slices("win", 3072, 2)
        + cols_slices("win", 2048, 2)
        + [("cols", "wb", 0, 512), ("cols", "win", 5120, 512),
           ("cols", "wb", 512, 512), ("cols", "win", 5632, 512)]
        + [("cols", "wa", 0, 512), ("cols", "win", 4096, 512),
           ("cols", "wa", 512, 512), ("cols", "win", 4608, 512)]
        + cols_slices("wout", 0, 2)
    )
    xattn_slices = cols_slices("wq", 0, 2) + cols_slices("wo", 0, 2)
    kv_slices = cols_slices("wkv", 0, 4)
    halo_slices = ffn_slices("1") + mixer_slices[:4]

    tile_slices = ffn_slices("1")
    if stages in ("xpose", "norm"):
        tile_slices = []
    if stages in ("all", "mixer", "xattn"):
        tile_slices = tile_slices + mixer_slices
    if stages in ("all", "xattn"):
        tile_slices = tile_slices + xattn_slices
    if stages == "all":
        tile_slices = tile_slices + ffn_slices("2")

    plan = []
    if stages in ("all", "xattn"):
        plan += kv_slices
    for _ in range(NT):
        plan += tile_slices
    for d_ in plan:
        slice_id(d_)
    NSL = len(slice_list)

    scratch = nc.dram_tensor("wscratch", [max(NSL, 1), P, SLOT_ELEMS], BF16, kind="Internal").ap()
    scratch_buf = [Buf() for _ in range(NSL)]
    in_scratch = [False] * NSL
    n_uses = [0] * NSL
    for d_ in plan:
        n_uses[slice_ids[d_]] += 1
    slots_t = sb("wslots", [P, NSLOT, SLOT_ELEMS], BF16)
    slot_buf = [Buf() for _ in range(NSLOT)]
    slot_chan = [Chan(sem("wslot%d" % i)) for i in range(NSLOT)]
    store_chan = [Chan(sem("wstore%d" % i)) for i in range(NSLOT)]
    cast_chan = [Chan(sem("wcast%d" % i)) for i in range(NSLOT)]

    def slice_elems(desc):
        if desc[0] == "cols":
            return 8 * desc[3]
        return {"gu": 4096, "down": FC * P}[desc[0]]

    def emit_first_fetch(sid, s):
        desc = slice_list[sid]
        kind = desc[0]
        pieces = []
        if kind == "cols":
            _, key, c0, ncols = desc
            src_ = w_d[key][:, c0:c0 + ncols].rearrange("(kc p) n -> p kc n", p=P)
            dst = slots_t[:, s, 0:8 * ncols].rearrange("p (kc n) -> p kc n", n=ncols)
            pieces.append((dst, src_))
        elif kind == "gu":
            _, key, jj = desc
            dst_all = slots_t[:, s, 0:4096].rearrange("p (kc n) -> p kc n", n=512)
            for half in range(2):
                c0 = half * DFF + jj * 256
                src_ = w_d[key][:, c0:c0 + 256].rearrange("(kc p) n -> p kc n", p=P)
                pieces.append((dst_all[:, :, half * 256:(half + 1) * 256], src_))
        else:
            _, key, c = desc
            src_ = w_d[key][:, c * P:(c + 1) * P].rearrange("(kc p) n -> p kc n", p=P)
            dst = slots_t[:, s, 0:FC * P].rearrange("p (kc n) -> p kc n", n=P)
            pieces.append((dst, src_))
        for dst, src_ in pieces:
            S.add("pool", lambda e, dst=dst, src_=src_: e.dma_start(out=dst, in_=src_),
                  writes=[slot_buf[s]], chan=cast_chan[s])
        if n_uses[sid] > 1:
            ne = slice_elems(desc)
            S.add("sp", lambda e, ne=ne: e.dma_start(out=scratch[sid, :, 0:ne], in_=slots_t[:, s, 0:ne]),
                  reads=[slot_buf[s]], writes=[scratch_buf[sid]], chan=store_chan[s])
        in_scratch[sid] = True

    class WStream:
        def __init__(self):
            self.next_load = 0
            self.next_use = 0
            self.done_upto = 0
            self.done_flags = [False] * len(plan)

        def pump(self):
            while self.next_load < len(plan) and self.next_load - NSLOT < self.done_upto:
                n = self.next_load
                sid = slice_ids[plan[n]]
                s = n % NSLOT
                desc = plan[n]
                if not in_scratch[sid]:
                    emit_first_fetch(sid, s)
                else:
                    ne = slice_elems(desc)
                    dst = slots_t[:, s, 0:ne]
                    src = scratch[sid, :, 0:ne]
                    S.add("sp", lambda e, dst=dst, src=src: e.dma_start(out=dst, in_=src),
                          reads=[scratch_buf[sid]], writes=[slot_buf[s]], chan=slot_chan[s])
                self.next_load += 1

        def next(self, desc):
            n = self.next_use
            assert plan[n] == desc, (n, plan[n], desc)
            self.next_use += 1
            self.pump()
            assert self.next_load > n
            s = n % NSLOT
            return n, s

        def done(self, n):
            self.done_flags[n] = True
            while self.done_upto < len(plan) and self.done_flags[self.done_upto]:
                self.done_upto += 1
            self.pump()

    W = WStream()

    def slot_cols(s, ncols):
        return slots_t[:, s, 0:8 * ncols].rearrange("p (kc n) -> p kc n", n=ncols)

    def slot_down(s):
        return slots_t[:, s, 0:FC * P].rearrange("p (kc n) -> p kc n", n=P)

    vecs = sb("vecs", [P, NV], F32)
    vecs_b = Buf(const=True)
    ident = sb("ident", [P, P], F32)
    ident_b = Buf(const=True)
    ones_bf = sb("ones_bf", [P, P], BF16)
    ones_b = Buf(const=True)
    hmask = sb("hmask", [P, 1], F32)
    hmask_b = Buf(const=True)
    S.add("sp", lambda e: e.dma_start(out=vecs[:, :], in_=vecs_d[:, :]), writes=[vecs_b], chan=Chan(sem("c_vecs")))
    S.add("sp", lambda e: e.dma_start(out=ident[:, :], in_=ident_d[:, :]), writes=[ident_b], chan=Chan(sem("c_ident")))
    DBG = os.environ.get("KDBG", "")
    if "B" not in DBG:
        S.add("sp", lambda e: e.dma_start(out=hmask[:, :], in_=hmask_d[:, :]), writes=[hmask_b], chan=Chan(sem("c_hmask")))
    if "A" not in DBG:
        S.add("dve", lambda e: e.memset(ones_bf[:, :], 1.0), writes=[ones_b])

    def vcol(c):
        return vecs[:, c:c + 1]

    bhalf = sb("bhalf", [P, 16], F32)
    bhalf_b = Buf(const=True)
    S.add("dve", lambda e: e.tensor_scalar(bhalf[:, :], vecs[:, V_BIN + 32:V_BIN + 48], 0.5, None, ALU.mult),
          reads=[vecs_b], writes=[bhalf_b])

    hT = sb("hT", [P, DC, TT], F32)
    hT_b = [Buf() for _ in range(DC)]
    hT_main, hT_main_b = hT, hT_b
    hTh = sb("hTh", [P, DC, HALO], F32)
    hTh_b = [Buf() for _ in range(DC)]
    xnh = sb("xnh", [P, DC, HALO], BF16)
    xnh_b = [Buf() for _ in range(DC)]
    hidh = sb("hidh", [P, FC, HALO], BF16)
    hidh_b = [Buf() for _ in range(FC)]
    xn = sb("xn", [P, DC, TT], BF16)
    xn_b = [Buf() for _ in range(DC)]
    hid = sb("hid", [P, 24, TT], BF16)
    hid_b = [Buf() for _ in range(24)]
    sq = sb("sq", [P, 4, TT], BF16)
    sq_b = [Buf() for _ in range(4)]
    rstd = sb("rstd", [P, 2, TT], F32)
    rstd_b = [Buf(), Buf()]
    tmpf = sb("tmpf", [P, 3, TT], F32)
    tmpf_b = [Buf() for _ in range(3)]
    xs = sb("xs", [P, 3, D], F32)
    xs_b = [Buf() for _ in range(3)]
    xs_chan = [Chan(sem("xs%d" % i)) for i in range(3)]
    os_chan = [Chan(sem("os%d" % i)) for i in range(4)]
    ystore_b = [Buf() for _ in range(4)]

    psum = [es.enter_context(nc.psum_tensor("ps%d" % i, [P, TT], F32)) for i in range(8)]
    psum_b = [Buf() for _ in range(8)]
    ps_ctr = [0]

    def ps_next():
        i = ps_ctr[0] % 6
        ps_ctr[0] += 1
        return psum[i], psum_b[i]

    rr = {"sq": 0, "rstd": 0, "tmpf": 0, "xs": 0, "os": 0}

    def ring(name, n):
        i = rr[name] % n
        rr[name] += 1
        return i

    def mm(out, lhsT, rhs, start, stop, reads, writes):
        S.add("pe", lambda e: e.matmul(out, lhsT, rhs, start=start, stop=stop),
              reads=reads, writes=writes)

    def load_x_tile(src_d, r0, ntok, hT=None, hT_b=None):
        if hT is None:
            hT, hT_b = hT_main, hT_main_b
        S.label = "load_x"
        nch = (ntok + P - 1) // P
        for tc in range(nch):
            n = min(P, ntok - tc * P)
            i = ring("xs", 3)
            S.add("sp", lambda e, i=i, n=n, tc=tc: e.dma_start(
                out=xs[0:n, i, :], in_=src_d[r0 + tc * P:r0 + tc * P + n, :]),
                writes=[xs_b[i]], chan=xs_chan[i])
            for g in range(2):
                pt, pb = ps_next()
                for j in range(4):
                    c = 4 * g + j
                    S.add("pe", lambda e, pt=pt, j=j, i=i, n=n, c=c: e.transpose(
                        pt[:, j * P:j * P + n], xs[0:n, i, c * P:(c + 1) * P], ident[0:n, 0:n]),
                        reads=[xs_b[i], ident_b], writes=[pb])
                for j in range(4):
                    c = 4 * g + j
                    eng = "dve"
                    if eng == "act":
                        S.add("act", lambda e, pt=pt, j=j, n=n, c=c, tc=tc: e.activation(
                            hT[:, c, tc * P:tc * P + n], pt[:, j * P:j * P + n], AF.Copy),
                            reads=[pb], writes=[hT_b[c]])
                    else:
                        S.add("dve", lambda e, pt=pt, j=j, n=n, c=c, tc=tc: e.tensor_copy(
                            hT[:, c, tc * P:tc * P + n], pt[:, j * P:j * P + n]),
                            reads=[pb], writes=[hT_b[c]])

    stat_pending = []

    def stat_add(src, src_b, c, n, delay=0):
        i = ring("sq", 4)
        S.add("act", lambda e, i=i, c=c: e.activation(sq[:, i, 0:n], src[:, c, 0:n], AF.Square),
              reads=[src_b[c]], writes=[sq_b[i]])
        stat_pending.append((i, c, n))
        while len(stat_pending) > delay:
            stat_flush_one()

    def stat_flush_one():
        i, c, n = stat_pending.pop(0)
        mm(psum[6][:, 0:n], ones_bf[:, :], sq[:, i, 0:n], c == 0, c == DC - 1,
           [sq_b[i], ones_b], [psum_b[6]])

    def stat_flush():
        while stat_pending:
            stat_flush_one()

    def rmsnorm(src, src_b, gcol, n, dst, dst_b, nchunks=DC, pre=False):
        pt, pb = psum[6], psum_b[6]
        if not pre:
            for c in range(nchunks):
                stat_add(src, src_b, c, n)
        i = ring("rstd", 2)
        t = ring("tmpf", 3)
        S.add("act", lambda e: e.activation(tmpf[:, t, 0:n], pt[:, 0:n], AF.Sqrt,
                                            bias=eps_rms[:, 0:1], scale=1.0 / D),
              reads=[pb, eps_b], writes=[tmpf_b[t]])
        S.add("dve", lambda e: e.reciprocal(rstd[:, i, 0:n], tmpf[:, t, 0:n]),
              reads=[tmpf_b[t]], writes=[rstd_b[i]])
        for c in range(nchunks):
            S.add("dve", lambda e, c=c: e.scalar_tensor_tensor(
                out=dst[:, c, 0:n], in0=src[:, c, 0:n], scalar=vcol(gcol + c),
                in1=rstd[:, i, 0:n], op0=ALU.mult, op1=ALU.mult),
                reads=[src_b[c], rstd_b[i], vecs_b], writes=[dst_b[c]])

    eps_rms = sb("eps_rms", [P, 2], F32)
    eps_b = Buf(const=True)
    if "A" not in DBG:
        S.add("dve", lambda e: e.memset(eps_rms[:, 0:1], EPS_RMS), writes=[eps_b])
        S.add("dve", lambda e: e.memset(eps_rms[:, 1:2], EPS_LN), writes=[eps_b])

    def ffn(tag, gcol, n, pre=False, halo=False):
        S.label = "ffn_norm"
        if pre != "done":
            rmsnorm(hT, hT_b, gcol, n, xn, xn_b, pre=pre)
        if halo:
            rmsnorm(hTh, hTh_b, gcol, HALO, xnh, xnh_b)
        S.label = "ffn_up"
        for jj in range(11):
            sn, s = W.next(("gu", "gu" + tag, jj))
            wv = slot_cols(s, 512)
            for sub in range(2):
                j = 2 * jj + sub
                pg, pgb = ps_next()
                pu, pub = ps_next()
                for k in range(DC):
                    mm(pg[:, 0:n], wv[:, k, sub * P:(sub + 1) * P], xn[:, k, 0:n],
                       k == 0, k == DC - 1, [slot_buf[s], xn_b[k]], [pgb])
                for k in range(DC):
                    mm(pu[:, 0:n], wv[:, k, 256 + sub * P:256 + (sub + 1) * P], xn[:, k, 0:n],
                       k == 0, k == DC - 1, [slot_buf[s], xn_b[k]], [pub])
                t = ring("tmpf", 3)
                S.add("act", lambda e, t=t, pg=pg: e.activation(tmpf[:, t, 0:n], pg[:, 0:n], AF.Silu),
                      reads=[pgb], writes=[tmpf_b[t]])
                S.add("dve", lambda e, t=t, pu=pu, j=j: e.tensor_tensor(
                    out=hid[:, j, 0:n], in0=pu[:, 0:n], in1=tmpf[:, t, 0:n], op=ALU.mult),
                    reads=[pub, tmpf_b[t]], writes=[hid_b[j]])
                if halo:
                    ph, phb = ps_next()
                    for k in range(DC):
                        mm(ph[:, 0:HALO], wv[:, k, sub * P:(sub + 1) * P], xnh[:, k, :],
                           k == 0, k == DC - 1, [slot_buf[s], xnh_b[k]], [phb])
                    for k in range(DC):
                        mm(ph[:, HALO:2 * HALO], wv[:, k, 256 + sub * P:256 + (sub + 1) * P], xnh[:, k, :],
                           k == 0, k == DC - 1, [slot_buf[s], xnh_b[k]], [phb])
                    t = ring("tmpf", 3)
                    S.add("act", lambda e, t=t, ph=ph: e.activation(tmpf[:, t, 0:HALO], ph[:, 0:HALO], AF.Silu),
                          reads=[phb], writes=[tmpf_b[t]])
                    S.add("dve", lambda e, t=t, ph=ph, j=j: e.tensor_tensor(
                        out=hidh[:, j, :], in0=ph[:, HALO:2 * HALO], in1=tmpf[:, t, 0:HALO], op=ALU.mult),
                        reads=[phb, tmpf_b[t]], writes=[hidh_b[j]])
            W.done(sn)
        S.label = "ffn_down"
        for c in range(DC):
            sn, s = W.next(("down", "dn" + tag, c))
            wv = slot_down(s)
            po, pob = ps_next()
            for j in range(FC):
                mm(po[:, 0:n], wv[:, j, :], hid[:, j, 0:n], j == 0, j == FC - 1,
                   [slot_buf[s], hid_b[j]], [pob])
            if halo:
                ph, phb = ps_next()
                for j in range(FC):
                    mm(ph[:, 0:HALO], wv[:, j, :], hidh[:, j, :], j == 0, j == FC - 1,
                       [slot_buf[s], hidh_b[j]], [phb])
            W.done(sn)
            S.add("dve", lambda e, c=c, po=po: e.scalar_tensor_tensor(
                out=hT[:, c, 0:n], in0=po[:, 0:n], scalar=0.5, in1=hT[:, c, 0:n],
                op0=ALU.mult, op1=ALU.add),
                reads=[pob, hT_b[c]], writes=[hT_b[c]])
            stat_add(hT, hT_b, c, n, delay=1)
            if halo:
                S.add("dve", lambda e, c=c, ph=ph: e.scalar_tensor_tensor(
                    out=hTh[:, c, :], in0=ph[:, 0:HALO], scalar=0.5, in1=hTh[:, c, :],
                    op0=ALU.mult, op1=ALU.add),
                    reads=[phb, hTh_b[c]], writes=[hTh_b[c]])
        stat_flush()

    def store_tile(src, src_b, r0):
        S.label = "store"
        for tc in range(TT // P):
            i = ring("os", 4)
            for g in range(2):
                pt, pb = ps_next()
                for j in range(4):
                    c = 4 * g + j
                    S.add("pe", lambda e, pt=pt, j=j, c=c, tc=tc: e.transpose(
                        pt[:, j * P:(j + 1) * P], src[:, c, tc * P:(tc + 1) * P], ident[:, :]),
                        reads=[src_b[c], ident_b], writes=[pb])
                if g == 0:
                    S.add("act", lambda e, pt=pt, i=i: e.activation(osb[:, i, 0:512], pt[:, :], AF.Copy),
                          reads=[pb], writes=[os_b[i]])
                else:
                    S.add("act", lambda e, pt=pt, i=i: e.activation(osb[:, i, 512:1024], pt[:, :], AF.Copy),
                          reads=[pb], writes=[os_b[i]])
            S.add("sp", lambda e, i=i, tc=tc: e.dma_start(
                out=y_d[r0 + tc * P:r0 + (tc + 1) * P, :], in_=osb[:, i, :]),
                reads=[os_b[i]], writes=[ystore_b[i]], chan=os_chan[i])


    aT = sb("aT", [P, DC, HALO + TT], BF16)
    aT_b = [Buf() for _ in range(DC)]
    acc = sb("acc", [P, DC, TT], F32)
    acc_b = [Buf() for _ in range(DC)]
    gv = sb("gv", [P, 4, D], F32)
    gv_b = [Buf() for _ in range(4)]
    t2v = gv[:, :, :].rearrange("p a (b n) -> p (a b) n", n=TT)
    osb = gv
    os_b = gv_b
    vtok = sb("vtok", [P, 4, D], BF16)
    vtok_b = [Buf() for _ in range(4)]
    bct = sb("bct", [P, 512], F32)
    bct_b = Buf(const=True)
    brow = sb("brow", [P, 2, 1024], BF16)
    brow_b = Buf(const=True)
    rowsf = sb("rowsf", [1, 1536], F32)
    rowsf_b = Buf()
    WsT = sb("WsT", [P, 4, P], BF16)
    WsT_b = Buf(const=True)
    KT = sb("KT", [P, DC, NMEM], BF16)
    KT_b = Buf(const=True)
    Vt = sb("Vt", [P, 2, D], BF16)
    Vt_b = Buf(const=True)
    lnst = sb("lnst", [P, 2, TT], F32)
    lnst_b = [Buf() for _ in range(2)]
    bst = sb("bst", [P, 4, 12], F32)
    bst_b = Buf()
    mv = sb("mv", [P, 4, 2], F32)
    mv_b = Buf()
    sdv = sb("sdv", [P, 8], F32)
    sdv_b = Buf()
    identb = sb("identb", [P, P], BF16)
    identb_b = Buf(const=True)
    dg = sb("dg", [P, 6, P], BF16)
    dg_b = [Buf() for _ in range(6)]
    rr["dg"] = 0
    B2 = sb("B2", [P, DC, P], F32)
    B2_b = Buf(const=True)

    def setup_consts():
        S.add("sp", lambda e: e.dma_start(out=bct[:, :], in_=bc_d[:, :]), writes=[bct_b], chan=Chan(sem("c_bc")))
        S.add("sp", lambda e: e.dma_start(out=rowsf[:, :], in_=rows_d[:, :]), writes=[rowsf_b], chan=Chan(sem("c_rows")))
        S.add("pool", lambda e: e.memset(brow[:, :, :], 0.0), writes=[brow_b])
        S.add("dve", lambda e: e.tensor_copy(brow[0:1, 0, :], rowsf[0:1, 0:1024]), reads=[rowsf_b, brow_b], writes=[brow_b])
        S.add("dve", lambda e: e.tensor_tensor(out=brow[0:1, 1, :], in0=rowsf[0:1, 0:1024], in1=brow[0:1, 0, :],
                                               op=ALU.subtract), reads=[rowsf_b, brow_b], writes=[brow_b])
        wv = tmpf[:, 0, :].rearrange("p (g s) -> p g s", s=P)
        S.add("sp", lambda e: e.dma_start(out=wv, in_=sguw_d.rearrange("g t s -> t g s")),
              writes=[tmpf_b[0]], chan=Chan(sem("c_sguw")))
        S.add("sp", lambda e: e.dma_start(out=tmpf[:, 1, 0:P], in_=tril_d[:, :]),
              writes=[tmpf_b[1]], chan=Chan(sem("c_tril")))
        pt, pb = ps_next()
        for g in range(4):
            S.add("dve", lambda e, g=g: e.tensor_tensor(out=wv[:, g, :], in0=wv[:, g, :], in1=tmpf[:, 1, 0:P],
                                                        op=ALU.mult),
                  reads=[tmpf_b[0], tmpf_b[1]], writes=[tmpf_b[0]])
        for g in range(4):
            S.add("pe", lambda e, g=g: e.transpose(pt[:, g * P:(g + 1) * P], wv[:, g, :], ident[:, :]),
                  reads=[tmpf_b[0], ident_b], writes=[pb])
        S.add("dve", lambda e: e.tensor_copy(WsT[:, :, :].rearrange("p g t -> p (g t)"), pt[:, :]),
              reads=[pb], writes=[WsT_b])
        S.add("dve", lambda e: e.tensor_copy(identb[:, :], ident[:, :]), reads=[ident_b], writes=[identb_b])
        p2, p2b = ps_next()
        for g in range(4):
            mm(p2[:, g * P:(g + 1) * P], ones_bf[:, :], WsT[:, g, :], True, True, [ones_b, WsT_b], [p2b])
        for c in range(DC):
            g = c // 2
            S.add("dve", lambda e, c=c, g=g: e.scalar_tensor_tensor(
                out=B2[:, c, :], in0=p2[:, g * P:(g + 1) * P], scalar=vcol(V_SLNB + c),
                in1=bct[:, g * P:(g + 1) * P], op0=ALU.mult, op1=ALU.add),
                reads=[p2b, vecs_b, bct_b], writes=[B2_b])

    def kv_prep():
        load_x_tile(mem_d, 0, NMEM)
        rmsnorm(hT, hT_b, V_G_MEM, NMEM, xn, xn_b)
        for p in range(2):
            sn, s = W.next(("cols", "wkv", p * 512, 512))
            wv = slot_cols(s, 512)
            for cc in range(4):
                c = 4 * p + cc
                pk, pkb = ps_next()
                for k in range(DC):
                    mm(pk[:, 0:NMEM], wv[:, k, cc * P:(cc + 1) * P], xn[:, k, 0:NMEM],
                       k == 0, k == DC - 1, [slot_buf[s], xn_b[k]], [pkb])
                S.add("act", lambda e, c=c, pk=pk: e.activation(KT[:, c, :], pk[:, 0:NMEM], AF.Copy),
                      reads=[pkb], writes=[KT_b])
            W.done(sn)
        for p in range(2):
            sn, s = W.next(("cols", "wkv", 1024 + p * 512, 512))
            wv = slot_cols(s, 512)
            for mc in range(2):
                pv, pvb = ps_next()
                for k in range(DC):
                    mm(pv[:, :], xn[:, k, mc * P:(mc + 1) * P], wv[:, k, :],
                       k == 0, k == DC - 1, [slot_buf[s], xn_b[k]], [pvb])
                S.add("act", lambda e, mc=mc, p=p, pv=pv: e.activation(
                    Vt[:, mc, p * 512:(p + 1) * 512], pv[:, :], AF.Copy), reads=[pvb], writes=[Vt_b])
            W.done(sn)

    def mixer_a_proj(n, col0, halo=False):
        streams = [(xn, xn_b, n, col0)]
        if halo:
            streams.append((xnh, xnh_b, HALO, 0))
        for p in range(2):
            snv, sv = W.next(("cols", "win", p * 512, 512))
            sng, sg = W.next(("cols", "win", 1024 + p * 512, 512))
            wvv, wvg = slot_cols(sv, 512), slot_cols(sg, 512)
            for cc in range(4):
                c = 4 * p + cc
                for (sx, sx_b, sn_, sc0) in streams:
                    pv, pvb = ps_next()
                    pg, pgb = ps_next()
                    for k in range(DC):
                        mm(pv[:, 0:sn_], wvv[:, k, cc * P:(cc + 1) * P], sx[:, k, 0:sn_],
                           k == 0, k == DC - 1, [slot_buf[sv], sx_b[k]], [pvb])
                    for k in range(DC):
                        mm(pg[:, 0:sn_], wvg[:, k, cc * P:(cc + 1) * P], sx[:, k, 0:sn_],
                           k == 0, k == DC - 1, [slot_buf[sg], sx_b[k]], [pgb])
                    t = ring("tmpf", 3)
                    S.add("act", lambda e, t=t, pg=pg, c=c, sn_=sn_: e.activation(
                        tmpf[:, t, 0:sn_], pg[:, 0:sn_], AF.Sigmoid, bias=vcol(V_BIN + 8 + c)),
                        reads=[pgb, vecs_b], writes=[tmpf_b[t]])
                    S.add("dve", lambda e, t=t, pv=pv, c=c, sn_=sn_, sc0=sc0: e.scalar_tensor_tensor(
                        out=aT[:, c, sc0:sc0 + sn_], in0=pv[:, 0:sn_], scalar=vcol(V_BIN + c),
                        in1=tmpf[:, t, 0:sn_], op0=ALU.add, op1=ALU.mult),
                        reads=[pvb, tmpf_b[t], vecs_b], writes=[aT_b[c]])
                if halo:
                    S.add("dve", lambda e, c=c: e.tensor_scalar(aT[:, c, 0:HALO], aT[:, c, 0:HALO],
                                                                hmask[:, 0:1], None, ALU.mult),
                          reads=[aT_b[c], hmask_b], writes=[aT_b[c]])
            W.done(snv)
            W.done(sng)

    def mixer(halo=False):
        n = TT
        S.label = "mix_norm"
        rmsnorm(hT, hT_b, V_G_MIX, n, xn, xn_b, pre=True)
        if halo:
            rmsnorm(hTh, hTh_b, V_G_MIX, HALO, xnh, xnh_b)
        S.label = "mix_aproj"
        mixer_a_proj(n, HALO, halo=halo)
        S.label = "mix_v"
        for p in range(2):
            sn, s = W.next(("cols", "win", 3072 + p * 512, 512))
            wv = slot_cols(s, 512)
            for tc in range(4):
                pv, pvb = ps_next()
                for k in range(DC):
                    mm(pv[:, :], xn[:, k, tc * P:(tc + 1) * P], wv[:, k, :], k == 0, False,
                       [slot_buf[s], xn_b[k]], [pvb])
                mm(pv[:, :], ones_bf[:, :], brow[:, 0, p * 512:(p + 1) * 512], False, False,
                   [ones_b, brow_b], [pvb])
                mm(pv[:, :], ones_bf[:, :], brow[:, 1, p * 512:(p + 1) * 512], False, True,
                   [ones_b, brow_b], [pvb])
                S.add("act", lambda e, pv=pv, tc=tc, p=p: e.activation(
                    gv[:, tc, p * 512:(p + 1) * 512], pv[:, :], AF.Gelu_apprx_tanh),
                    reads=[pvb], writes=[gv_b[tc]])
            W.done(sn)
        for tc in range(4):
            for hh in range(2):
                S.add("dve", lambda e, tc=tc, hh=hh: e.bn_stats(
                    bst[:, tc, hh * 6:(hh + 1) * 6], gv[:, tc, hh * 512:(hh + 1) * 512]),
                    reads=[gv_b[tc]], writes=[bst_b])
            S.add("dve", lambda e, tc=tc: e.bn_aggr(
                mv[:, tc, :], bst[:, tc, :].rearrange("p (a b) -> p a b", b=6)),
                reads=[bst_b], writes=[mv_b])
        S.add("act", lambda e: e.activation(sdv[:, 0:4], mv[:, :, 1], AF.Sqrt, bias=eps_rms[:, 1:2]),
              reads=[mv_b, eps_b], writes=[sdv_b])
        S.add("dve", lambda e: e.reciprocal(sdv[:, 4:8], sdv[:, 0:4]), reads=[sdv_b], writes=[sdv_b])
        for tc in range(4):
            S.add("dve", lambda e, tc=tc: e.tensor_scalar(
                vtok[:, tc, :], gv[:, tc, :], mv[:, tc, 0:1], sdv[:, 4 + tc:5 + tc], ALU.subtract, ALU.mult),
                reads=[gv_b[tc], mv_b, sdv_b], writes=[vtok_b[tc]])
        S.label = "mix_u"
        for p in range(2):
            sn, s = W.next(("cols", "win", 2048 + p * 512, 512))
            wv = slot_cols(s, 512)
            for cc in range(4):
                c = 4 * p + cc
                pu, pub = ps_next()
                for k in range(DC):
                    mm(pu[:, :], wv[:, k, cc * P:(cc + 1) * P], xn[:, k, :],
                       k == 0, k == DC - 1, [slot_buf[s], xn_b[k]], [pub])
                S.add("act", lambda e, pu=pu, c=c: e.activation(
                    hid[:, c, :], pu[:, :], AF.Gelu_apprx_tanh, bias=vcol(V_BIN + 16 + c)),
                    reads=[pub, vecs_b], writes=[hid_b[c]])
            W.done(sn)
        S.label = "mix_conv"
        ps_s, ps_sb = psum[6], psum_b[6]
        ps_q, ps_qb = psum[7], psum_b[7]
        cstat_pending = []

        def cstat_flush_one():
            j1, j2, cj = cstat_pending.pop(0)
            mm(ps_s[:, :], ones_bf[:, :], sq[:, j1, :], cj == 0, cj == DC - 1, [sq_b[j1], ones_b], [ps_sb])
            mm(ps_q[:, :], ones_bf[:, :], sq[:, j2, :], cj == 0, cj == DC - 1, [sq_b[j2], ones_b], [ps_qb])
        def conv_builds(c_lo, c_hi, act_chunk):
            for c in range(c_lo, c_hi):
                for k in range(CW):
                    r = ring("dg", 6)
                    if k % 3 == 0 or c == act_chunk:
                        S.add("act", lambda e, r=r, k=k, c=c: e.activation(
                            dg[:, r, :], identb[:, :], AF.Copy, scale=vcol(V_CONVW + k * 8 + c)),
                            reads=[identb_b, vecs_b], writes=[dg_b[r]])
                    else:
                        S.add("dve", lambda e, r=r, k=k, c=c: e.tensor_scalar(
                            dg[:, r, :], identb[:, :], vcol(V_CONVW + k * 8 + c), None, ALU.mult),
                            reads=[identb_b, vecs_b], writes=[dg_b[r]])
                    yield r

        builds = conv_builds(0, 4, -1)
        ready = []
        for _ in range(5):
            ready.append(next(builds))
        for c in range(0, 4):
            pc, pcb = ps_next()
            for k in range(CW):
                nb = next(builds, None)
                if nb is not None:
                    ready.append(nb)
                r = ready.pop(0)
                mm(pc[:, :], dg[:, r, :], aT[:, c, 2 + k:2 + k + TT], k == 0, k == CW - 1,
                   [dg_b[r], aT_b[c]], [pcb])
            S.add("act", lambda e, pc=pc, c=c: e.activation(acc[:, c, :], pc[:, :], AF.Identity,
                                                            bias=vcol(V_CONVB + c)),
                  reads=[pcb, vecs_b], writes=[acc_b[c]])
            i1 = ring("sq", 4)
            S.add("act", lambda e, i=i1, pc=pc, c=c: e.activation(sq[:, i, :], pc[:, :], AF.Identity,
                                                                  bias=vcol(V_CONVB + c)),
                  reads=[pcb, vecs_b], writes=[sq_b[i1]])
            i2 = ring("sq", 4)
            S.add("act", lambda e, i=i2, pc=pc, c=c: e.activation(sq[:, i, :], pc[:, :], AF.Square,
                                                                  bias=vcol(V_CONVB + c)),
                  reads=[pcb, vecs_b], writes=[sq_b[i2]])
            cstat_pending.append((i1, i2, c))
            if len(cstat_pending) > 1:
                cstat_flush_one()
        S.label = "mix_sgu"
        for c in range(DC):
            pm, pmb = ps_next()
            g = c // 2
            for tc in range(4):
                mm(pm[:, tc * P:(tc + 1) * P], vtok[:, tc, c * P:(c + 1) * P], WsT[:, g, :], True, True,
                   [vtok_b[tc], WsT_b], [pmb])
            t = ring("tmpf", 3)
            S.add("dve", lambda e, t=t, pm=pm, c=c: e.scalar_tensor_tensor(
                out=tmpf[:, t, :].rearrange("p (a b) -> p a b", b=P),
                in0=pm[:, :].rearrange("p (a b) -> p a b", b=P), scalar=vcol(V_SLNG + c),
                in1=B2[:, c, :].unsqueeze(1).broadcast_to([P, 4, P]), op0=ALU.mult, op1=ALU.add),
                reads=[pmb, vecs_b, B2_b], writes=[tmpf_b[t]])
            S.add("dve", lambda e, t=t, c=c: e.tensor_tensor(
                out=hid[:, 16 + c, :], in0=tmpf[:, t, :], in1=hid[:, c, :], op=ALU.mult),
                reads=[tmpf_b[t], hid_b[c]], writes=[hid_b[16 + c]])
        S.label = "mix_conv"
        builds = conv_builds(4, DC, 4)
        ready = []
        for _ in range(5):
            ready.append(next(builds))
        for c in range(4, DC):
            pc, pcb = ps_next()
            for k in range(CW):
                nb = next(builds, None)
                if nb is not None:
                    ready.append(nb)
                r = ready.pop(0)
                mm(pc[:, :], dg[:, r, :], aT[:, c, 2 + k:2 + k + TT], k == 0, k == CW - 1,
                   [dg_b[r], aT_b[c]], [pcb])
            S.add("act", lambda e, pc=pc, c=c: e.activation(acc[:, c, :], pc[:, :], AF.Identity,
                                                            bias=vcol(V_CONVB + c)),
                  reads=[pcb, vecs_b], writes=[acc_b[c]])
            i1 = ring("sq", 4)
            S.add("act", lambda e, i=i1, pc=pc, c=c: e.activation(sq[:, i, :], pc[:, :], AF.Identity,
                                                                  bias=vcol(V_CONVB + c)),
                  reads=[pcb, vecs_b], writes=[sq_b[i1]])
            i2 = ring("sq", 4)
            S.add("act", lambda e, i=i2, pc=pc, c=c: e.activation(sq[:, i, :], pc[:, :], AF.Square,
                                                                  bias=vcol(V_CONVB + c)),
                  reads=[pcb, vecs_b], writes=[sq_b[i2]])
            cstat_pending.append((i1, i2, c))
            if len(cstat_pending) > 1:
                cstat_flush_one()
        while cstat_pending:
            cstat_flush_one()
        S.label = "mix_convln"
        S.add("dve", lambda e: e.tensor_scalar(lnst[:, 0, :], ps_s[:, :], 1.0 / D, None, ALU.mult),
              reads=[ps_sb], writes=[lnst_b[0]])
        S.add("dve", lambda e: e.tensor_tensor(out=lnst[:, 1, :], in0=lnst[:, 0, :], in1=lnst[:, 0, :],
                                               op=ALU.mult), reads=[lnst_b[0]], writes=[lnst_b[1]])
        S.add("dve", lambda e: e.scalar_tensor_tensor(
            out=lnst[:, 1, :], in0=ps_q[:, :], scalar=1.0 / D, in1=lnst[:, 1, :],
            op0=ALU.mult, op1=ALU.subtract), reads=[ps_qb, lnst_b[1]], writes=[lnst_b[1]])
        tq = ring("tmpf", 3)
        S.add("act", lambda e: e.activation(tmpf[:, tq, :], lnst[:, 1, :], AF.Sqrt, bias=eps_rms[:, 1:2]),
              reads=[lnst_b[1], eps_b], writes=[tmpf_b[tq]])
        S.add("dve", lambda e: e.reciprocal(lnst[:, 1, :], tmpf[:, tq, :]), reads=[tmpf_b[tq]], writes=[lnst_b[1]])
        S.add("dve", lambda e: e.tensor_tensor(out=lnst[:, 0, :], in0=lnst[:, 0, :], in1=lnst[:, 1, :],
                                               op=ALU.mult), reads=[lnst_b[0], lnst_b[1]], writes=[lnst_b[0]])
        for c in range(DC):
            S.add("pool", lambda e, c=c: e.tensor_tensor(out=acc[:, c, :], in0=acc[:, c, :], in1=lnst[:, 1, :],
                                                         op=ALU.mult),
                  reads=[acc_b[c], lnst_b[1]], writes=[acc_b[c]])

        def convln_chunk(c):
            S.add("dve", lambda e, c=c: e.tensor_tensor(out=acc[:, c, :], in0=acc[:, c, :], in1=lnst[:, 0, :],
                                                        op=ALU.subtract),
                  reads=[acc_b[c], lnst_b[0]], writes=[acc_b[c]])
            S.add("act", lambda e, c=c: e.activation(hid[:, 8 + c, :], acc[:, c, :], AF.Silu,
                                                     bias=vcol(V_CLNB + c), scale=vcol(V_CLNG + c)),
                  reads=[acc_b[c], vecs_b], writes=[hid_b[8 + c]])
        S.label = "mix_wb"
        for p in range(2):
            snb, sbb = W.next(("cols", "wb", p * 512, 512))
            sng, sg = W.next(("cols", "win", 5120 + p * 512, 512))
            wvb, wvg = slot_cols(sbb, 512), slot_cols(sg, 512)
            pgs = []
            for cc in range(4):
                pg, pgb = ps_next()
                for k in range(DC):
                    mm(pg[:, :], wvg[:, k, cc * P:(cc + 1) * P], xn[:, k, :],
                       k == 0, k == DC - 1, [slot_buf[sg], xn_b[k]], [pgb])
                pgs.append((pg, pgb))
            for cc in range(4):
                c = 4 * p + cc
                pg, pgb = pgs[cc]
                py, pyb = ps_next()
                for k in range(DC):
                    mm(py[:, :], wvb[:, k, cc * P:(cc + 1) * P], hid[:, 16 + k, :],
                       k == 0, k == DC - 1, [slot_buf[sbb], hid_b[16 + k]], [pyb])
                S.label = "mix_convln"
                convln_chunk(c)
                S.label = "mix_wb"
                t = ring("tmpf", 3)
                S.add("act", lambda e, t=t, pg=pg, c=c: e.activation(
                    tmpf[:, t, :], pg[:, :], AF.Tanh, bias=bhalf[:, 8 + c:9 + c], scale=0.5),
                    reads=[pgb, bhalf_b], writes=[tmpf_b[t]])
                S.add("dve", lambda e, t=t, py=py, c=c: e.scalar_tensor_tensor(
                    out=t2v[:, c, :], in0=tmpf[:, t, :], scalar=1.0, in1=py[:, :],
                    op0=ALU.add, op1=ALU.mult),
                    reads=[pyb, tmpf_b[t]], writes=[gv_b[c // 2]])
            W.done(snb)
            W.done(sng)
        S.label = "mix_wa"
        for p in range(2):
            sna, sa = W.next(("cols", "wa", p * 512, 512))
            sng, sg = W.next(("cols", "win", 4096 + p * 512, 512))
            wva, wvg = slot_cols(sa, 512), slot_cols(sg, 512)
            for cc in range(4):
                c = 4 * p + cc
                py, pyb = ps_next()
                pg, pgb = ps_next()
                for k in range(DC):
                    mm(py[:, :], wva[:, k, cc * P:(cc + 1) * P], hid[:, 8 + k, :],
                       k == 0, k == DC - 1, [slot_buf[sa], hid_b[8 + k]], [pyb])
                for k in range(DC):
                    mm(pg[:, :], wvg[:, k, cc * P:(cc + 1) * P], xn[:, k, :],
                       k == 0, k == DC - 1, [slot_buf[sg], xn_b[k]], [pgb])
                t = ring("tmpf", 3)
                S.add("act", lambda e, t=t, pg=pg, c=c: e.activation(
                    tmpf[:, t, :], pg[:, :], AF.Tanh, bias=bhalf[:, c:c + 1], scale=0.5),
                    reads=[pgb, bhalf_b], writes=[tmpf_b[t]])
                S.add("dve", lambda e, t=t, py=py: e.scalar_tensor_tensor(
                    out=tmpf[:, t, :], in0=tmpf[:, t, :], scalar=1.0, in1=py[:, :],
                    op0=ALU.add, op1=ALU.mult),
                    reads=[pyb, tmpf_b[t]], writes=[tmpf_b[t]])
                S.add("pool", lambda e, t=t, c=c: e.tensor_tensor(
                    out=hid[:, c, :], in0=tmpf[:, t, :], in1=t2v[:, c, :], op=ALU.add),
                    reads=[tmpf_b[t], gv_b[c // 2]], writes=[hid_b[c]])
            W.done(sna)
            W.done(sng)
        S.label = "mix_wout"
        proj_residual("wout", 0, scale=0.5)
        for c in range(DC):
            S.add("pool", lambda e, c=c: e.tensor_copy(aT[:, c, 0:HALO], aT[:, c, TT:TT + HALO]),
                  reads=[aT_b[c]], writes=[aT_b[c]])

    def proj_residual(key, src0, scale=1.0):
        for p in range(2):
            sn, s = W.next(("cols", key, p * 512, 512))
            wv = slot_cols(s, 512)
            banks = [ps_next() for _ in range(4)]
            for k in range(DC):
                for cc in range(4):
                    po, pob = banks[cc]
                    mm(po[:, :], wv[:, k, cc * P:(cc + 1) * P], hid[:, src0 + k, :],
                       k == 0, k == DC - 1, [slot_buf[s], hid_b[src0 + k]], [pob])
            for cc in range(4):
                c = 4 * p + cc
                po, pob = banks[cc]
                S.add("dve", lambda e, c=c, po=po: e.scalar_tensor_tensor(
                    out=hT[:, c, :], in0=po[:, :], scalar=scale, in1=hT[:, c, :],
                    op0=ALU.mult, op1=ALU.add),
                    reads=[pob, hT_b[c]], writes=[hT_b[c]])
                stat_add(hT, hT_b, c, TT, delay=2)
            W.done(sn)
        stat_flush()

    def xattn():
        n = TT
        S.label = "xat_norm"
        rmsnorm(hT, hT_b, V_G_XAT, n, xn, xn_b, pre=True)
        S.label = "xat_q"
        for p in range(2):
            sn, s = W.next(("cols", "wq", p * 512, 512))
            wv = slot_cols(s, 512)
            for cc in range(4):
                c = 4 * p + cc
                pq, pqb = ps_next()
                for k in range(DC):
                    mm(pq[:, :], wv[:, k, cc * P:(cc + 1) * P], xn[:, k, :],
                       k == 0, k == DC - 1, [slot_buf[s], xn_b[k]], [pqb])
                S.add("act", lambda e, c=c, pq=pq: e.activation(hid[:, 8 + c, :], pq[:, :], AF.Copy,
                                                                scale=1.0 / 16.0),
                      reads=[pqb], writes=[hid_b[8 + c]])
            W.done(sn)
        S.label = "xat_attn"

        def head_scores(h):
            for mc in range(2):
                psc, pscb = ps_next()
                for dc in range(2):
                    mm(psc[:, :], KT[:, 2 * h + dc, mc * P:(mc + 1) * P], hid[:, 8 + 2 * h + dc, :],
                       dc == 0, dc == 1, [KT_b, hid_b[8 + 2 * h + dc]], [pscb])
                S.add("act", lambda e, psc=psc, h=h, mc=mc: e.activation(
                    hid[:, 16 + 2 * h + mc, :], psc[:, :], AF.Exp),
                    reads=[pscb], writes=[hid_b[16 + 2 * h + mc]])

        def head_rest(h):
            pss, pssb = ps_next()
            for mc in range(2):
                mm(pss[:, :], ones_bf[:, :], hid[:, 16 + 2 * h + mc, :], mc == 0, mc == 1,
                   [ones_b, hid_b[16 + 2 * h + mc]], [pssb])
            i = ring("rstd", 2)
            S.add("dve", lambda e, i=i, pss=pss: e.reciprocal(rstd[:, i, :], pss[:, :]),
                  reads=[pssb], writes=[rstd_b[i]])
            for dc in range(2):
                po, pob = ps_next()
                for mc in range(2):
                    mm(po[:, :], Vt[:, mc, (2 * h + dc) * P:(2 * h + dc + 1) * P], hid[:, 16 + 2 * h + mc, :],
                       mc == 0, mc == 1, [Vt_b, hid_b[16 + 2 * h + mc]], [pob])
                S.add("dve", lambda e, i=i, po=po, h=h, dc=dc: e.tensor_tensor(
                    out=hid[:, 2 * h + dc, :], in0=po[:, :], in1=rstd[:, i, :], op=ALU.mult),
                    reads=[pob, rstd_b[i]], writes=[hid_b[2 * h + dc]])

        head_scores(0)
        for h in range(4):
            if h + 1 < 4:
                head_scores(h + 1)
            head_rest(h)
        S.label = "xat_wo"
        proj_residual("wo", 0)

    full = stages in ("all", "mixer", "xattn")
    if full:
        setup_consts()
    if stages in ("all", "xattn"):
        kv_prep()
    fuse_halo = do_halo and full
    if fuse_halo:
        load_x_tile(xh_d, 0, HALO, hTh, hTh_b)
    def tile_head(it):
        load_x_tile(x_d, it * TT, TT)
        S.label = "ffn_norm"
        rmsnorm(hT, hT_b, V_G_FFN1, TT, xn, xn_b)

    for it in range(NT):
        if stages != "all":
            load_x_tile(x_d, it * TT, TT)
        if stages == "xpose":
            store_tile(hT, hT_b, it * TT)
            continue
        if stages == "norm":
            rmsnorm(hT, hT_b, V_G_FFN1, TT, hT, hT_b)
            store_tile(hT, hT_b, it * TT)
            continue
        if stages != "all":
            ffn("1", V_G_FFN1, TT, halo=(fuse_halo and it == 0))
        else:
            if it == 0:
                tile_head(0)
            ffn("1", V_G_FFN1, TT, pre="done", halo=(fuse_halo and it == 0))
        if stages == "ffn1":
            store_tile(hT, hT_b, it * TT)
            continue
        mixer(halo=(fuse_halo and it == 0))
        if stages == "mixer":
            store_tile(hT, hT_b, it * TT)
            continue
        xattn()
        if stages == "xattn":
            store_tile(hT, hT_b, it * TT)
            continue
        ffn("2", V_G_FFN2, TT, pre=True)
        S.label = "final_norm"
        rmsnorm(hT, hT_b, V_G_FIN, TT, acc, acc_b, pre=True)
        if it + 1 < NT:
            tile_head(it + 1)
        store_tile(acc, acc_b, it * TT)

    S.add("sp", lambda e: e.nop(), reads=ystore_b)

    with nc.Block() as block:
        S.emit(nc, block, esem)
    es.close()
    nc._pe_labels = S.labels["pe"]
    return nc


def _pack_inputs(inputs, NT):
    f = lambda a: np.ascontiguousarray(np.asarray(a, dtype=np.float32))
    x = f(inputs["x"])
    mem = f(inputs["mem"])

    def fm(v):
        return f(v).reshape(-1, P).T

    vecs = np.zeros((P, NV), np.float32)
    vecs[:, V_G_FFN1:V_G_FFN1 + 8] = fm(inputs["ffn1_norm"][0])
    vecs[:, V_G_MIX:V_G_MIX + 8] = fm(inputs["mix_norm"][0])
    vecs[:, V_G_XAT:V_G_XAT + 8] = fm(inputs["xattn_norm"][0])
    vecs[:, V_G_MEM:V_G_MEM + 8] = fm(inputs["mem_norm"][0])
    vecs[:, V_G_FFN2:V_G_FFN2 + 8] = fm(inputs["ffn2_norm"][0])
    vecs[:, V_G_FIN:V_G_FIN + 8] = fm(inputs["final_norm"])
    vecs[:, V_BIN:V_BIN + 48] = fm(inputs["b_in"][0])
    vecs[:, V_CONVB:V_CONVB + 8] = fm(inputs["conv_b"][0])
    vecs[:, V_CLNG:V_CLNG + 8] = fm(inputs["conv_ln_g"][0])
    vecs[:, V_CLNB:V_CLNB + 8] = fm(inputs["conv_ln_b"][0])
    cw = f(inputs["conv_w"][0])
    vecs[:, V_CONVW:V_CONVW + CW * 8] = cw.reshape(CW, 8, P).transpose(2, 0, 1).reshape(P, CW * 8)
    vecs[:, V_SLNG:V_SLNG + 8] = fm(inputs["sgu_ln_g"][0])
    vecs[:, V_SLNB:V_SLNB + 8] = fm(inputs["sgu_ln_b"][0])
    rows = np.concatenate([f(inputs["b_in"][0])[3072:4096], f(inputs["sgu_b"][0]).reshape(-1)])[None, :]
    bc = np.broadcast_to(f(inputs["sgu_b"][0]).reshape(1, -1), (P, 512))
    shared = {
        "vecs": vecs, "rows": np.ascontiguousarray(rows), "bc": np.ascontiguousarray(bc),
        "ident": np.eye(P, dtype=np.float32),
        "tril": np.tril(np.ones((P, P), np.float32)),
        "sgu_w": f(inputs["sgu_w"][0]),
        "ffn1_w_gu": f(inputs["ffn1_w_gu"][0]), "ffn1_w_down": f(inputs["ffn1_w_down"][0]),
        "w_in": f(inputs["w_in"][0]), "w_a_out": f(inputs["w_a_out"][0]),
        "w_b_out": f(inputs["w_b_out"][0]), "w_out": f(inputs["w_out"][0]),
        "w_q": f(inputs["w_q"][0]), "w_kv": f(inputs["w_kv"][0]), "w_o": f(inputs["w_o"][0]),
        "ffn2_w_gu": f(inputs["ffn2_w_gu"][0]), "ffn2_w_down": f(inputs["ffn2_w_down"][0]),
    }
    per = NT * TT
    in_maps = []
    for i in range(NCORES):
        b, q = i // 4, i % 4
        t0 = q * per
        m = dict(shared)
        m["x"] = np.ascontiguousarray(x[b, t0:t0 + per])
        if q == 0:
            m["xh"] = np.zeros((HALO, D), np.float32)
            m["hmask"] = np.zeros((P, 1), np.float32)
        else:
            m["xh"] = np.ascontiguousarray(x[b, t0 - HALO:t0])
            m["hmask"] = np.ones((P, 1), np.float32)
        m["mem"] = np.ascontiguousarray(mem[b])
        in_maps.append(m)
    return in_maps


_PROG_CACHE = {}


def _run(inputs, NT=8, stages="all", trace=False):
    key = (NT, stages)
    if key not in _PROG_CACHE:
        _PROG_CACHE[key] = build_program(NT, stages)
    nc = _PROG_CACHE[key]
    in_maps = _pack_inputs(inputs, NT)
    res = run_bass_kernel_spmd(nc, in_maps, core_ids=list(range(NCORES)), trace=trace)
    per = NT * TT
    out = np.zeros((2, 4 * per, D), np.float32)
    for i in range(NCORES):
        b, q = i // 4, i % 4
        out[b, q * per:(q + 1) * per] = res.results[i]["y"]
    return out, res


def kernel(**inputs):
    out, _ = _run(inputs, NT=SEQ // 4 // TT, stages="all")
    return out
```

```python
import contextlib
import os
import numpy as np
import concourse.bass as bass
import concourse.mybir as mybir
from concourse.bass_utils import run_bass_kernel_spmd

F32 = mybir.dt.float32
BF16 = mybir.dt.bfloat16
AF = mybir.ActivationFunctionType
ALU = mybir.AluOpType

P = 128
D = 1024
DC = 8
DFF = 2816
FC = 22
TT = 512
NMEM = 256
HALO = 32
CW = 31
NCORES = 8
SEQ = 16384
NSLOT = 6
SLOT_ELEMS = 4096
EPS_RMS = 1e-6
EPS_LN = 1e-5

V_G_FFN1, V_G_MIX, V_G_XAT, V_G_MEM, V_G_FFN2, V_G_FIN = 0, 8, 16, 24, 32, 40
V_BIN = 48
V_CONVB, V_CLNG, V_CLNB = 96, 104, 112
V_CONVW = 120
V_SLNG = 120 + CW * 8
V_SLNB = V_SLNG + 8
NV = V_SLNB + 8


class Buf:
    __slots__ = ("w", "r", "const")

    def __init__(self, const=False):
        self.w = None
        self.r = []
        self.const = const


class Chan:
    def __init__(self, sem):
        self.sem = sem
        self.count = 0


class Sched:
    ENG = ("pe", "act", "dve", "pool", "sp")

    def __init__(self):
        self.q = {e: [] for e in self.ENG}
        self.label = ""
        self.labels = {e: [] for e in self.ENG}

    def add(self, eng, fn, reads=(), writes=(), chan=None):
        raw, other = [], []
        for b in reads:
            if b.w is not None:
                raw.append(b.w)
        for b in writes:
            if b.w is not None:
                other.append(b.w)
            other.extend(b.r)
        idx = len(self.q[eng])
        if chan is not None:
            chan.count += 16
            ev = ("d", chan, chan.count)
        else:
            ev = ("e", eng, idx)
        self.q[eng].append((fn, raw, other, chan))
        self.labels[eng].append(self.label)
        for b in reads:
            if not b.const:
                b.r.append(ev)
        for b in writes:
            b.w = ev
            b.r = []
        return ev

    def emit(self, nc, block, esem):
        waits = {e: [] for e in self.ENG}
        flagged = {e: set() for e in self.ENG}
        for E in self.ENG:
            seen_e, seen_d = {}, {}
            for idx, (fn, raw, other, chan) in enumerate(self.q[E]):
                need_e, need_d = {}, {}
                for deps, is_raw in ((raw, True), (other, False)):
                    for d in deps:
                        if d[0] == "e":
                            Pn, j = d[1], d[2]
                            if Pn == E and E == "pe":
                                continue
                            if j > need_e.get(Pn, -1):
                                need_e[Pn] = j
                        else:
                            ch, v = d[1], d[2]
                            if v > need_d.get(ch, 0):
                                need_d[ch] = v
                w = []
                for Pn, j in need_e.items():
                    if seen_e.get(Pn, -1) >= j:
                        continue
                    seen_e[Pn] = j
                    flagged[Pn].add(j)
                    w.append(("e", Pn, j))
                for ch, v in need_d.items():
                    if seen_d.get(ch, 0) >= v:
                        continue
                    seen_d[ch] = v
                    w.append(("d", ch, v))
                waits[E].append(w)
        count_at = {}
        for E in self.ENG:
            c = 0
            m = {}
            for idx in range(len(self.q[E])):
                if idx in flagged[E]:
                    c += 1
                    m[idx] = c
            count_at[E] = m

        def run(E, eng):
            for idx, (fn, raw, other, chan) in enumerate(self.q[E]):
                for w in waits[E][idx]:
                    if w[0] == "e":
                        eng.wait_ge(esem[w[1]], count_at[w[1]][w[2]])
                    else:
                        eng.wait_ge(w[1].sem, w[2])
                ins = fn(eng)
                if chan is not None:
                    ins.then_inc(chan.sem, 16)
                elif idx in flagged[E]:
                    ins.then_inc(esem[E], 1)

        @block.sync
        def _(e):
            run("sp", e)

        @block.tensor
        def _(e):
            run("pe", e)

        @block.scalar
        def _(e):
            run("act", e)

        @block.vector
        def _(e):
            run("dve", e)

        @block.gpsimd
        def _(e):
            run("pool", e)


def build_program(NT, stages="all", do_halo=True):
    nc = bass.Bass("TRN2", target_bir_lowering=False)
    S = Sched()
    es = contextlib.ExitStack()

    def dram_in(name, shape, dt=F32):
        return nc.dram_tensor(name, list(shape), dt, kind="ExternalInput").ap()

    NTOK = NT * TT
    x_d = dram_in("x", [NTOK, D])
    xh_d = dram_in("xh", [HALO, D])
    mem_d = dram_in("mem", [NMEM, D])
    vecs_d = dram_in("vecs", [P, NV])
    rows_d = dram_in("rows", [1, 1536])
    bc_d = dram_in("bc", [P, 512])
    hmask_d = dram_in("hmask", [P, 1])
    ident_d = dram_in("ident", [P, P])
    tril_d = dram_in("tril", [P, P])
    sguw_d = dram_in("sgu_w", [4, P, P])
    w_d = {
        "gu1": dram_in("ffn1_w_gu", [D, 2 * DFF]),
        "dn1": dram_in("ffn1_w_down", [DFF, D]),
        "win": dram_in("w_in", [D, 6 * D]),
        "wa": dram_in("w_a_out", [D, D]),
        "wb": dram_in("w_b_out", [D, D]),
        "wout": dram_in("w_out", [D, D]),
        "wq": dram_in("w_q", [D, D]),
        "wkv": dram_in("w_kv", [D, 2 * D]),
        "wo": dram_in("w_o", [D, D]),
        "gu2": dram_in("ffn2_w_gu", [D, 2 * DFF]),
        "dn2": dram_in("ffn2_w_down", [DFF, D]),
    }
    y_d = nc.dram_tensor("y", [NTOK, D], F32, kind="ExternalOutput").ap()

    def sb(name, shape, dt):
        return es.enter_context(nc.sbuf_tensor("sb_" + name, list(shape), dt))

    def sem(name):
        return es.enter_context(nc.semaphore(name))

    esem = {e: sem("prog_" + e) for e in ("pe", "act", "dve", "pool")}

    slice_ids = {}
    slice_list = []

    def slice_id(desc):
        if desc not in slice_ids:
            slice_ids[desc] = len(slice_list)
            slice_list.append(desc)
        return slice_ids[desc]

    def ffn_slices(tag):
        out = []
        for jj in range(11):
            out.append(("gu", "gu" + tag, jj))
        for c in range(DC):
            out.append(("down", "dn" + tag, c))
        return out

    def cols_slices(key, lo, n):
        return [("cols", key, lo + i * 512, 512) for i in range(n)]

    mixer_slices = (
        [("cols", "win", 0, 512), ("cols", "win", 1024, 512),
         ("cols", "win", 512, 512), ("cols", "win", 1536, 512)]
        + cols_slices("win", 3072, 2)
        + cols_slices("win", 2048, 2)
        + [("cols", "wb", 0, 512), ("cols", "win", 5120, 512),
           ("cols", "wb", 512, 512), ("cols", "win", 5632, 512)]
        + [("cols", "wa", 0, 512), ("cols", "win", 4096, 512),
           ("cols", "wa", 512, 512), ("cols", "win", 4608, 512)]
        + cols_slices("wout", 0, 2)
    )
    xattn_slices = cols_slices("wq", 0, 2) + cols_slices("wo", 0, 2)
    kv_slices = cols_slices("wkv", 0, 4)
    halo_slices = ffn_slices("1") + mixer_slices[:4]

    tile_slices = ffn_slices("1")
    if stages in ("xpose", "norm"):
        tile_slices = []
    if stages in ("all", "mixer", "xattn"):
        tile_slices = tile_slices + mixer_slices
    if stages in ("all", "xattn"):
        tile_slices = tile_slices + xattn_slices
    if stages == "all":
        tile_slices = tile_slices + ffn_slices("2")

    plan = []
    if stages in ("all", "xattn"):
        plan += kv_slices
    for _ in range(NT):
        plan += tile_slices
    for d_ in plan:
        slice_id(d_)
    NSL = len(slice_list)

    scratch = nc.dram_tensor("wscratch", [max(NSL, 1), P, SLOT_ELEMS], BF16, kind="Internal").ap()
    scratch_buf = [Buf() for _ in range(NSL)]
    in_scratch = [False] * NSL
    n_uses = [0] * NSL
    for d_ in plan:
        n_uses[slice_ids[d_]] += 1
    slots_t = sb("wslots", [P, NSLOT, SLOT_ELEMS], BF16)
    slot_buf = [Buf() for _ in range(NSLOT)]
    slot_chan = [Chan(sem("wslot%d" % i)) for i in range(NSLOT)]
    store_chan = [Chan(sem("wstore%d" % i)) for i in range(NSLOT)]
    cast_chan = [Chan(sem("wcast%d" % i)) for i in range(NSLOT)]

    def slice_elems(desc):
        if desc[0] == "cols":
            return 8 * desc[3]
        return {"gu": 4096, "down": FC * P}[desc[0]]

    def emit_first_fetch(sid, s):
        desc = slice_list[sid]
        kind = desc[0]
        pieces = []
        if kind == "cols":
            _, key, c0, ncols = desc
            src_ = w_d[key][:, c0:c0 + ncols].rearrange("(kc p) n -> p kc n", p=P)
            dst = slots_t[:, s, 0:8 * ncols].rearrange("p (kc n) -> p kc n", n=ncols)
            pieces.append((dst, src_))
        elif kind == "gu":
            _, key, jj = desc
            dst_all = slots_t[:, s, 0:4096].rearrange("p (kc n) -> p kc n", n=512)
            for half in range(2):
                c0 = half * DFF + jj * 256
                src_ = w_d[key][:, c0:c0 + 256].rearrange("(kc p) n -> p kc n", p=P)
                pieces.append((dst_all[:, :, half * 256:(half + 1) * 256], src_))
        else:
            _, key, c = desc
            src_ = w_d[key][:, c * P:(c + 1) * P].rearrange("(kc p) n -> p kc n", p=P)
            dst = slots_t[:, s, 0:FC * P].rearrange("p (kc n) -> p kc n", n=P)
            pieces.append((dst, src_))
        for dst, src_ in pieces:
            S.add("pool", lambda e, dst=dst, src_=src_: e.dma_start(out=dst, in_=src_),
                  writes=[slot_buf[s]], chan=cast_chan[s])
        if n_uses[sid] > 1:
            ne = slice_elems(desc)
            S.add("sp", lambda e, ne=ne: e.dma_start(out=scratch[sid, :, 0:ne], in_=slots_t[:, s, 0:ne]),
                  reads=[slot_buf[s]], writes=[scratch_buf[sid]], chan=store_chan[s])
        in_scratch[sid] = True

    class WStream:
        def __init__(self):
            self.next_load = 0
            self.next_use = 0
            self.done_upto = 0
            self.done_flags = [False] * len(plan)

        def pump(self):
            while self.next_load < len(plan) and self.next_load - NSLOT < self.done_upto:
                n = self.next_load
                sid = slice_ids[plan[n]]
                s = n % NSLOT
                desc = plan[n]
                if not in_scratch[sid]:
                    emit_first_fetch(sid, s)
                else:
                    ne = slice_elems(desc)
                    dst = slots_t[:, s, 0:ne]
                    src = scratch[sid, :, 0:ne]
                    S.add("sp", lambda e, dst=dst, src=src: e.dma_start(out=dst, in_=src),
                          reads=[scratch_buf[sid]], writes=[slot_buf[s]], chan=slot_chan[s])
                self.next_load += 1

        def next(self, desc):
            n = self.next_use
            assert plan[n] == desc, (n, plan[n], desc)
            self.next_use += 1
            self.pump()
            assert self.next_load > n
            s = n % NSLOT
            return n, s

        def done(self, n):
            self.done_flags[n] = True
            while self.done_upto < len(plan) and self.done_flags[self.done_upto]:
                self.done_upto += 1
            self.pump()

    W = WStream()

    def slot_cols(s, ncols):
        return slots_t[:, s, 0:8 * ncols].rearrange("p (kc n) -> p kc n", n=ncols)

    def slot_down(s):
        return slots_t[:, s, 0:FC * P].rearrange("p (kc n) -> p kc n", n=P)

    vecs = sb("vecs", [P, NV], F32)
    vecs_b = Buf(const=True)
    ident = sb("ident", [P, P], F32)
    ident_b = Buf(const=True)
    ones_bf = sb("ones_bf", [P, P], BF16)
    ones_b = Buf(const=True)
    hmask = sb("hmask", [P, 1], F32)
    hmask_b = Buf(const=True)
    S.add("sp", lambda e: e.dma_start(out=vecs[:, :], in_=vecs_d[:, :]), writes=[vecs_b], chan=Chan(sem("c_vecs")))
    S.add("sp", lambda e: e.dma_start(out=ident[:, :], in_=ident_d[:, :]), writes=[ident_b], chan=Chan(sem("c_ident")))
    DBG = os.environ.get("KDBG", "")
    if "B" not in DBG:
        S.add("sp", lambda e: e.dma_start(out=hmask[:, :], in_=hmask_d[:, :]), writes=[hmask_b], chan=Chan(sem("c_hmask")))
    if "A" not in DBG:
        S.add("dve", lambda e: e.memset(ones_bf[:, :], 1.0), writes=[ones_b])

    def vcol(c):
        return vecs[:, c:c + 1]

    bhalf = sb("bhalf", [P, 16], F32)
    bhalf_b = Buf(const=True)
    S.add("dve", lambda e: e.tensor_scalar(bhalf[:, :], vecs[:, V_BIN + 32:V_BIN + 48], 0.5, None, ALU.mult),
          reads=[vecs_b], writes=[bhalf_b])

    hT = sb("hT", [P, DC, TT], F32)
    hT_b = [Buf() for _ in range(DC)]
    hT_main, hT_main_b = hT, hT_b
    hTh = sb("hTh", [P, DC, HALO], F32)
    hTh_b = [Buf() for _ in range(DC)]
    xnh = sb("xnh", [P, DC, HALO], BF16)
    xnh_b = [Buf() for _ in range(DC)]
    hidh = sb("hidh", [P, FC, HALO], BF16)
    hidh_b = [Buf() for _ in range(FC)]
    xn = sb("xn", [P, DC, TT], BF16)
    xn_b = [Buf() for _ in range(DC)]
    hid = sb("hid", [P, 24, TT], BF16)
    hid_b = [Buf() for _ in range(24)]
    sq = sb("sq", [P, 4, TT], BF16)
    sq_b = [Buf() for _ in range(4)]
    rstd = sb("rstd", [P, 2, TT], F32)
    rstd_b = [Buf(), Buf()]
    tmpf = sb("tmpf", [P, 3, TT], F32)
    tmpf_b = [Buf() for _ in range(3)]
    xs = sb("xs", [P, 3, D], F32)
    xs_b = [Buf() for _ in range(3)]
    xs_chan = [Chan(sem("xs%d" % i)) for i in range(3)]
    os_chan = [Chan(sem("os%d" % i)) for i in range(4)]
    ystore_b = [Buf() for _ in range(4)]

    psum = [es.enter_context(nc.psum_tensor("ps%d" % i, [P, TT], F32)) for i in range(8)]
    psum_b = [Buf() for _ in range(8)]
    ps_ctr = [0]

    def ps_next():
        i = ps_ctr[0] % 6
        ps_ctr[0] += 1
        return psum[i], psum_b[i]

    rr = {"sq": 0, "rstd": 0, "tmpf": 0, "xs": 0, "os": 0}

    def ring(name, n):
        i = rr[name] % n
        rr[name] += 1
        return i

    def mm(out, lhsT, rhs, start, stop, reads, writes):
        S.add("pe", lambda e: e.matmul(out, lhsT, rhs, start=start, stop=stop),
              reads=reads, writes=writes)

    def load_x_tile(src_d, r0, ntok, hT=None, hT_b=None):
        if hT is None:
            hT, hT_b = hT_main, hT_main_b
        S.label = "load_x"
        nch = (ntok + P - 1) // P
        for tc in range(nch):
            n = min(P, ntok - tc * P)
            i = ring("xs", 3)
            S.add("sp", lambda e, i=i, n=n, tc=tc: e.dma_start(
                out=xs[0:n, i, :], in_=src_d[r0 + tc * P:r0 + tc * P + n, :]),
                writes=[xs_b[i]], chan=xs_chan[i])
            for g in range(2):
                pt, pb = ps_next()
                for j in range(4):
                    c = 4 * g + j
                    S.add("pe", lambda e, pt=pt, j=j, i=i, n=n, c=c: e.transpose(
                        pt[:, j * P:j * P + n], xs[0:n, i, c * P:(c + 1) * P], ident[0:n, 0:n]),
                        reads=[xs_b[i], ident_b], writes=[pb])
                for j in range(4):
                    c = 4 * g + j
                    eng = "dve"
                    if eng == "act":
                        S.add("act", lambda e, pt=pt, j=j, n=n, c=c, tc=tc: e.activation(
                            hT[:, c, tc * P:tc * P + n], pt[:, j * P:j * P + n], AF.Copy),
                            reads=[pb], writes=[hT_b[c]])
                    else:
                        S.add("dve", lambda e, pt=pt, j=j, n=n, c=c, tc=tc: e.tensor_copy(
                            hT[:, c, tc * P:tc * P + n], pt[:, j * P:j * P + n]),
                            reads=[pb], writes=[hT_b[c]])

    stat_pending = []

    def stat_add(src, src_b, c, n, delay=0):
        i = ring("sq", 4)
        S.add("act", lambda e, i=i, c=c: e.activation(sq[:, i, 0:n], src[:, c, 0:n], AF.Square),
              reads=[src_b[c]], writes=[sq_b[i]])
        stat_pending.append((i, c, n))
        while len(stat_pending) > delay:
            stat_flush_one()

    def stat_flush_one():
        i, c, n = stat_pending.pop(0)
        mm(psum[6][:, 0:n], ones_bf[:, :], sq[:, i, 0:n], c == 0, c == DC - 1,
           [sq_b[i], ones_b], [psum_b[6]])

    def stat_flush():
        while stat_pending:
            stat_flush_one()

    def rmsnorm(src, src_b, gcol, n, dst, dst_b, nchunks=DC, pre=False):
        pt, pb = psum[6], psum_b[6]
        if not pre:
            for c in range(nchunks):
                stat_add(src, src_b, c, n)
        i = ring("rstd", 2)
        t = ring("tmpf", 3)
        S.add("act", lambda e: e.activation(tmpf[:, t, 0:n], pt[:, 0:n], AF.Sqrt,
                                            bias=eps_rms[:, 0:1], scale=1.0 / D),
              reads=[pb, eps_b], writes=[tmpf_b[t]])
        S.add("dve", lambda e: e.reciprocal(rstd[:, i, 0:n], tmpf[:, t, 0:n]),
              reads=[tmpf_b[t]], writes=[rstd_b[i]])
        for c in range(nchunks):
            S.add("dve", lambda e, c=c: e.scalar_tensor_tensor(
                out=dst[:, c, 0:n], in0=src[:, c, 0:n], scalar=vcol(gcol + c),
                in1=rstd[:, i, 0:n], op0=ALU.mult, op1=ALU.mult),
                reads=[src_b[c], rstd_b[i], vecs_b], writes=[dst_b[c]])

    eps_rms = sb("eps_rms", [P, 2], F32)
    eps_b = Buf(const=True)
    if "A" not in DBG:
        S.add("dve", lambda e: e.memset(eps_rms[:, 0:1], EPS_RMS), writes=[eps_b])
        S.add("dve", lambda e: e.memset(eps_rms[:, 1:2], EPS_LN), writes=[eps_b])

    def ffn(tag, gcol, n, pre=False, halo=False):
        S.label = "ffn_norm"
        if pre != "done":
            rmsnorm(hT, hT_b, gcol, n, xn, xn_b, pre=pre)
        if halo:
            rmsnorm(hTh, hTh_b, gcol, HALO, xnh, xnh_b)
        S.label = "ffn_up"
        for jj in range(11):
            sn, s = W.next(("gu", "gu" + tag, jj))
            wv = slot_cols(s, 512)
            for sub in range(2):
                j = 2 * jj + sub
                pg, pgb = ps_next()
                pu, pub = ps_next()
                for k in range(DC):
                    mm(pg[:, 0:n], wv[:, k, sub * P:(sub + 1) * P], xn[:, k, 0:n],
                       k == 0, k == DC - 1, [slot_buf[s], xn_b[k]], [pgb])
                for k in range(DC):
                    mm(pu[:, 0:n], wv[:, k, 256 + sub * P:256 + (sub + 1) * P], xn[:, k, 0:n],
                       k == 0, k == DC - 1, [slot_buf[s], xn_b[k]], [pub])
                t = ring("tmpf", 3)
                S.add("act", lambda e, t=t, pg=pg: e.activation(tmpf[:, t, 0:n], pg[:, 0:n], AF.Silu),
                      reads=[pgb], writes=[tmpf_b[t]])
                S.add("dve", lambda e, t=t, pu=pu, j=j: e.tensor_tensor(
                    out=hid[:, j, 0:n], in0=pu[:, 0:n], in1=tmpf[:, t, 0:n], op=ALU.mult),
                    reads=[pub, tmpf_b[t]], writes=[hid_b[j]])
                if halo:
                    ph, phb = ps_next()
                    for k in range(DC):
                        mm(ph[:, 0:HALO], wv[:, k, sub * P:(sub + 1) * P], xnh[:, k, :],
                           k == 0, k == DC - 1, [slot_buf[s], xnh_b[k]], [phb])
                    for k in range(DC):
                        mm(ph[:, HALO:2 * HALO], wv[:, k, 256 + sub * P:256 + (sub + 1) * P], xnh[:, k, :],
                           k == 0, k == DC - 1, [slot_buf[s], xnh_b[k]], [phb])
                    t = ring("tmpf", 3)
                    S.add("act", lambda e, t=t, ph=ph: e.activation(tmpf[:, t, 0:HALO], ph[:, 0:HALO], AF.Silu),
                          reads=[phb], writes=[tmpf_b[t]])
                    S.add("dve", lambda e, t=t, ph=ph, j=j: e.tensor_tensor(
                        out=hidh[:, j, :], in0=ph[:, HALO:2 * HALO], in1=tmpf[:, t, 0:HALO], op=ALU.mult),
                        reads=[phb, tmpf_b[t]], writes=[hidh_b[j]])
            W.done(sn)
        S.label = "ffn_down"
        for c in range(DC):
            sn, s = W.next(("down", "dn" + tag, c))
            wv = slot_down(s)
            po, pob = ps_next()
            for j in range(FC):
                mm(po[:, 0:n], wv[:, j, :], hid[:, j, 0:n], j == 0, j == FC - 1,
                   [slot_buf[s], hid_b[j]], [pob])
            if halo:
                ph, phb = ps_next()
                for j in range(FC):
                    mm(ph[:, 0:HALO], wv[:, j, :], hidh[:, j, :], j == 0, j == FC - 1,
                       [slot_buf[s], hidh_b[j]], [phb])
            W.done(sn)
            S.add("dve", lambda e, c=c, po=po: e.scalar_tensor_tensor(
                out=hT[:, c, 0:n], in0=po[:, 0:n], scalar=0.5, in1=hT[:, c, 0:n],
                op0=ALU.mult, op1=ALU.add),
                reads=[pob, hT_b[c]], writes=[hT_b[c]])
            stat_add(hT, hT_b, c, n, delay=1)
            if halo:
                S.add("dve", lambda e, c=c, ph=ph: e.scalar_tensor_tensor(
                    out=hTh[:, c, :], in0=ph[:, 0:HALO], scalar=0.5, in1=hTh[:, c, :],
                    op0=ALU.mult, op1=ALU.add),
                    reads=[phb, hTh_b[c]], writes=[hTh_b[c]])
        stat_flush()

    def store_tile(src, src_b, r0):
        S.label = "store"
        for tc in range(TT // P):
            i = ring("os", 4)
            for g in range(2):
                pt, pb = ps_next()
                for j in range(4):
                    c = 4 * g + j
                    S.add("pe", lambda e, pt=pt, j=j, c=c, tc=tc: e.transpose(
                        pt[:, j * P:(j + 1) * P], src[:, c, tc * P:(tc + 1) * P], ident[:, :]),
                        reads=[src_b[c], ident_b], writes=[pb])
                if g == 0:
                    S.add("act", lambda e, pt=pt, i=i: e.activation(osb[:, i, 0:512], pt[:, :], AF.Copy),
                          reads=[pb], writes=[os_b[i]])
                else:
                    S.add("act", lambda e, pt=pt, i=i: e.activation(osb[:, i, 512:1024], pt[:, :], AF.Copy),
                          reads=[pb], writes=[os_b[i]])
            S.add("sp", lambda e, i=i, tc=tc: e.dma_start(
                out=y_d[r0 + tc * P:r0 + (tc + 1) * P, :], in_=osb[:, i, :]),
                reads=[os_b[i]], writes=[ystore_b[i]], chan=os_chan[i])


    aT = sb("aT", [P, DC, HALO + TT], BF16)
    aT_b = [Buf() for _ in range(DC)]
    acc = sb("acc", [P, DC, TT], F32)
    acc_b = [Buf() for _ in range(DC)]
    gv = sb("gv", [P, 4, D], F32)
    gv_b = [Buf() for _ in range(4)]
    t2v = gv[:, :, :].rearrange("p a (b n) -> p (a b) n", n=TT)
    osb = gv
    os_b = gv_b
    vtok = sb("vtok", [P, 4, D], BF16)
    vtok_b = [Buf() for _ in range(4)]
    bct = sb("bct", [P, 512], F32)
    bct_b = Buf(const=True)
    brow = sb("brow", [P, 2, 1024], BF16)
    brow_b = Buf(const=True)
    rowsf = sb("rowsf", [1, 1536], F32)
    rowsf_b = Buf()
    WsT = sb("WsT", [P, 4, P], BF16)
    WsT_b = Buf(const=True)
    KT = sb("KT", [P, DC, NMEM], BF16)
    KT_b = Buf(const=True)
    Vt = sb("Vt", [P, 2, D], BF16)
    Vt_b = Buf(const=True)
    lnst = sb("lnst", [P, 2, TT], F32)
    lnst_b = [Buf() for _ in range(2)]
    bst = sb("bst", [P, 4, 12], F32)
    bst_b = Buf()
    mv = sb("mv", [P, 4, 2], F32)
    mv_b = Buf()
    sdv = sb("sdv", [P, 8], F32)
    sdv_b = Buf()
    identb = sb("identb", [P, P], BF16)
    identb_b = Buf(const=True)
    dg = sb("dg", [P, 8, P], BF16)
    dg_b = [Buf() for _ in range(8)]
    rr["dg"] = 0
    B2 = sb("B2", [P, DC, P], F32)
    B2_b = Buf(const=True)

    def setup_consts():
        S.add("sp", lambda e: e.dma_start(out=bct[:, :], in_=bc_d[:, :]), writes=[bct_b], chan=Chan(sem("c_bc")))
        S.add("sp", lambda e: e.dma_start(out=rowsf[:, :], in_=rows_d[:, :]), writes=[rowsf_b], chan=Chan(sem("c_rows")))
        S.add("pool", lambda e: e.memset(brow[:, :, :], 0.0), writes=[brow_b])
        S.add("dve", lambda e: e.tensor_copy(brow[0:1, 0, :], rowsf[0:1, 0:1024]), reads=[rowsf_b, brow_b], writes=[brow_b])
        S.add("dve", lambda e: e.tensor_tensor(out=brow[0:1, 1, :], in0=rowsf[0:1, 0:1024], in1=brow[0:1, 0, :],
                                               op=ALU.subtract), reads=[rowsf_b, brow_b], writes=[brow_b])
        wv = tmpf[:, 0, :].rearrange("p (g s) -> p g s", s=P)
        S.add("sp", lambda e: e.dma_start(out=wv, in_=sguw_d.rearrange("g t s -> t g s")),
              writes=[tmpf_b[0]], chan=Chan(sem("c_sguw")))
        S.add("sp", lambda e: e.dma_start(out=tmpf[:, 1, 0:P], in_=tril_d[:, :]),
              writes=[tmpf_b[1]], chan=Chan(sem("c_tril")))
        pt, pb = ps_next()
        for g in range(4):
            S.add("dve", lambda e, g=g: e.tensor_tensor(out=wv[:, g, :], in0=wv[:, g, :], in1=tmpf[:, 1, 0:P],
                                                        op=ALU.mult),
                  reads=[tmpf_b[0], tmpf_b[1]], writes=[tmpf_b[0]])
        for g in range(4):
            S.add("pe", lambda e, g=g: e.transpose(pt[:, g * P:(g + 1) * P], wv[:, g, :], ident[:, :]),
                  reads=[tmpf_b[0], ident_b], writes=[pb])
        S.add("dve", lambda e: e.tensor_copy(WsT[:, :, :].rearrange("p g t -> p (g t)"), pt[:, :]),
              reads=[pb], writes=[WsT_b])
        S.add("dve", lambda e: e.tensor_copy(identb[:, :], ident[:, :]), reads=[ident_b], writes=[identb_b])
        p2, p2b = ps_next()
        for g in range(4):
            mm(p2[:, g * P:(g + 1) * P], ones_bf[:, :], WsT[:, g, :], True, True, [ones_b, WsT_b], [p2b])
        for c in range(DC):
            g = c // 2
            S.add("dve", lambda e, c=c, g=g: e.scalar_tensor_tensor(
                out=B2[:, c, :], in0=p2[:, g * P:(g + 1) * P], scalar=vcol(V_SLNB + c),
                in1=bct[:, g * P:(g + 1) * P], op0=ALU.mult, op1=ALU.add),
                reads=[p2b, vecs_b, bct_b], writes=[B2_b])

    def kv_prep():
        load_x_tile(mem_d, 0, NMEM)
        rmsnorm(hT, hT_b, V_G_MEM, NMEM, xn, xn_b)
        for p in range(2):
            sn, s = W.next(("cols", "wkv", p * 512, 512))
            wv = slot_cols(s, 512)
            for cc in range(4):
                c = 4 * p + cc
                pk, pkb = ps_next()
                for k in range(DC):
                    mm(pk[:, 0:NMEM], wv[:, k, cc * P:(cc + 1) * P], xn[:, k, 0:NMEM],
                       k == 0, k == DC - 1, [slot_buf[s], xn_b[k]], [pkb])
                S.add("act", lambda e, c=c, pk=pk: e.activation(KT[:, c, :], pk[:, 0:NMEM], AF.Copy),
                      reads=[pkb], writes=[KT_b])
            W.done(sn)
        for p in range(2):
            sn, s = W.next(("cols", "wkv", 1024 + p * 512, 512))
            wv = slot_cols(s, 512)
            for mc in range(2):
                pv, pvb = ps_next()
                for k in range(DC):
                    mm(pv[:, :], xn[:, k, mc * P:(mc + 1) * P], wv[:, k, :],
                       k == 0, k == DC - 1, [slot_buf[s], xn_b[k]], [pvb])
                S.add("act", lambda e, mc=mc, p=p, pv=pv: e.activation(
                    Vt[:, mc, p * 512:(p + 1) * 512], pv[:, :], AF.Copy), reads=[pvb], writes=[Vt_b])
            W.done(sn)

    def mixer_a_proj(n, col0, halo=False):
        streams = [(xn, xn_b, n, col0)]
        if halo:
            streams.append((xnh, xnh_b, HALO, 0))
        for p in range(2):
            snv, sv = W.next(("cols", "win", p * 512, 512))
            sng, sg = W.next(("cols", "win", 1024 + p * 512, 512))
            wvv, wvg = slot_cols(sv, 512), slot_cols(sg, 512)
            for cc in range(4):
                c = 4 * p + cc
                for (sx, sx_b, sn_, sc0) in streams:
                    pv, pvb = ps_next()
                    pg, pgb = ps_next()
                    for k in range(DC):
                        mm(pv[:, 0:sn_], wvv[:, k, cc * P:(cc + 1) * P], sx[:, k, 0:sn_],
                           k == 0, k == DC - 1, [slot_buf[sv], sx_b[k]], [pvb])
                    for k in range(DC):
                        mm(pg[:, 0:sn_], wvg[:, k, cc * P:(cc + 1) * P], sx[:, k, 0:sn_],
                           k == 0, k == DC - 1, [slot_buf[sg], sx_b[k]], [pgb])
                    t = ring("tmpf", 3)
                    S.add("act", lambda e, t=t, pg=pg, c=c, sn_=sn_: e.activation(
                        tmpf[:, t, 0:sn_], pg[:, 0:sn_], AF.Sigmoid, bias=vcol(V_BIN + 8 + c)),
                        reads=[pgb, vecs_b], writes=[tmpf_b[t]])
                    S.add("dve", lambda e, t=t, pv=pv, c=c, sn_=sn_, sc0=sc0: e.scalar_tensor_tensor(
                        out=aT[:, c, sc0:sc0 + sn_], in0=pv[:, 0:sn_], scalar=vcol(V_BIN + c),
                        in1=tmpf[:, t, 0:sn_], op0=ALU.add, op1=ALU.mult),
                        reads=[pvb, tmpf_b[t], vecs_b], writes=[aT_b[c]])
                if halo:
                    S.add("dve", lambda e, c=c: e.tensor_scalar(aT[:, c, 0:HALO], aT[:, c, 0:HALO],
                                                                hmask[:, 0:1], None, ALU.mult),
                          reads=[aT_b[c], hmask_b], writes=[aT_b[c]])
            W.done(snv)
            W.done(sng)

    def mixer(halo=False):
        n = TT
        S.label = "mix_norm"
        rmsnorm(hT, hT_b, V_G_MIX, n, xn, xn_b, pre=True)
        if halo:
            rmsnorm(hTh, hTh_b, V_G_MIX, HALO, xnh, xnh_b)
        S.label = "mix_aproj"
        mixer_a_proj(n, HALO, halo=halo)
        S.label = "mix_v"
        for p in range(2):
            sn, s = W.next(("cols", "win", 3072 + p * 512, 512))
            wv = slot_cols(s, 512)
            for tc in range(4):
                pv, pvb = ps_next()
                for k in range(DC):
                    mm(pv[:, :], xn[:, k, tc * P:(tc + 1) * P], wv[:, k, :], k == 0, False,
                       [slot_buf[s], xn_b[k]], [pvb])
                mm(pv[:, :], ones_bf[:, :], brow[:, 0, p * 512:(p + 1) * 512], False, False,
                   [ones_b, brow_b], [pvb])
                mm(pv[:, :], ones_bf[:, :], brow[:, 1, p * 512:(p + 1) * 512], False, True,
                   [ones_b, brow_b], [pvb])
                S.add("act", lambda e, pv=pv, tc=tc, p=p: e.activation(
                    gv[:, tc, p * 512:(p + 1) * 512], pv[:, :], AF.Gelu_apprx_tanh),
                    reads=[pvb], writes=[gv_b[tc]])
            W.done(sn)
        for tc in range(4):
            for hh in range(2):
                S.add("dve", lambda e, tc=tc, hh=hh: e.bn_stats(
                    bst[:, tc, hh * 6:(hh + 1) * 6], gv[:, tc, hh * 512:(hh + 1) * 512]),
                    reads=[gv_b[tc]], writes=[bst_b])
            S.add("dve", lambda e, tc=tc: e.bn_aggr(
                mv[:, tc, :], bst[:, tc, :].rearrange("p (a b) -> p a b", b=6)),
                reads=[bst_b], writes=[mv_b])
        S.add("act", lambda e: e.activation(sdv[:, 0:4], mv[:, :, 1], AF.Sqrt, bias=eps_rms[:, 1:2]),
              reads=[mv_b, eps_b], writes=[sdv_b])
        S.add("dve", lambda e: e.reciprocal(sdv[:, 4:8], sdv[:, 0:4]), reads=[sdv_b], writes=[sdv_b])
        for tc in range(4):
            S.add("dve", lambda e, tc=tc: e.tensor_scalar(
                vtok[:, tc, :], gv[:, tc, :], mv[:, tc, 0:1], sdv[:, 4 + tc:5 + tc], ALU.subtract, ALU.mult),
                reads=[gv_b[tc], mv_b, sdv_b], writes=[vtok_b[tc]])
        S.label = "mix_u"
        for p in range(2):
            sn, s = W.next(("cols", "win", 2048 + p * 512, 512))
            wv = slot_cols(s, 512)
            for cc in range(4):
                c = 4 * p + cc
                pu, pub = ps_next()
                for k in range(DC):
                    mm(pu[:, :], wv[:, k, cc * P:(cc + 1) * P], xn[:, k, :],
                       k == 0, k == DC - 1, [slot_buf[s], xn_b[k]], [pub])
                S.add("act", lambda e, pu=pu, c=c: e.activation(
                    hid[:, c, :], pu[:, :], AF.Gelu_apprx_tanh, bias=vcol(V_BIN + 16 + c)),
                    reads=[pub, vecs_b], writes=[hid_b[c]])
            W.done(sn)
        S.label = "mix_conv"
        ps_s, ps_sb = psum[6], psum_b[6]
        ps_q, ps_qb = psum[7], psum_b[7]
        cstat_pending = []

        def cstat_flush_one():
            j1, j2, cj = cstat_pending.pop(0)
            mm(ps_s[:, :], ones_bf[:, :], sq[:, j1, :], cj == 0, cj == DC - 1, [sq_b[j1], ones_b], [ps_sb])
            mm(ps_q[:, :], ones_bf[:, :], sq[:, j2, :], cj == 0, cj == DC - 1, [sq_b[j2], ones_b], [ps_qb])
        def conv_builds(c_lo, c_hi, act_chunk):
            for c in range(c_lo, c_hi):
                for k in range(CW):
                    r = ring("dg", 8)
                    if k % 3 == 0 or c == act_chunk:
                        S.add("act", lambda e, r=r, k=k, c=c: e.activation(
                            dg[:, r, :], identb[:, :], AF.Copy, scale=vcol(V_CONVW + k * 8 + c)),
                            reads=[identb_b, vecs_b], writes=[dg_b[r]])
                    else:
                        S.add("dve", lambda e, r=r, k=k, c=c: e.tensor_scalar(
                            dg[:, r, :], identb[:, :], vcol(V_CONVW + k * 8 + c), None, ALU.mult),
                            reads=[identb_b, vecs_b], writes=[dg_b[r]])
                    yield r

        builds = conv_builds(0, 4, -1)
        ready = []
        for _ in range(7):
            ready.append(next(builds))
        for c in range(0, 4):
            pc, pcb = ps_next()
            for k in range(CW):
                nb = next(builds, None)
                if nb is not None:
                    ready.append(nb)
                r = ready.pop(0)
                mm(pc[:, :], dg[:, r, :], aT[:, c, 2 + k:2 + k + TT], k == 0, k == CW - 1,
                   [dg_b[r], aT_b[c]], [pcb])
            S.add("act", lambda e, pc=pc, c=c: e.activation(acc[:, c, :], pc[:, :], AF.Identity,
                                                            bias=vcol(V_CONVB + c)),
                  reads=[pcb, vecs_b], writes=[acc_b[c]])
            i1 = ring("sq", 4)
            S.add("act", lambda e, i=i1, pc=pc, c=c: e.activation(sq[:, i, :], pc[:, :], AF.Identity,
                                                                  bias=vcol(V_CONVB + c)),
                  reads=[pcb, vecs_b], writes=[sq_b[i1]])
            i2 = ring("sq", 4)
            S.add("act", lambda e, i=i2, pc=pc, c=c: e.activation(sq[:, i, :], pc[:, :], AF.Square,
                                                                  bias=vcol(V_CONVB + c)),
                  reads=[pcb, vecs_b], writes=[sq_b[i2]])
            cstat_pending.append((i1, i2, c))
            if len(cstat_pending) > 1:
                cstat_flush_one()
        S.label = "mix_sgu"
        for c in range(DC):
            pm, pmb = ps_next()
            g = c // 2
            for tc in range(4):
                mm(pm[:, tc * P:(tc + 1) * P], vtok[:, tc, c * P:(c + 1) * P], WsT[:, g, :], True, True,
                   [vtok_b[tc], WsT_b], [pmb])
            t = ring("tmpf", 3)
            S.add("dve", lambda e, t=t, pm=pm, c=c: e.scalar_tensor_tensor(
                out=tmpf[:, t, :].rearrange("p (a b) -> p a b", b=P),
                in0=pm[:, :].rearrange("p (a b) -> p a b", b=P), scalar=vcol(V_SLNG + c),
                in1=B2[:, c, :].unsqueeze(1).broadcast_to([P, 4, P]), op0=ALU.mult, op1=ALU.add),
                reads=[pmb, vecs_b, B2_b], writes=[tmpf_b[t]])
            S.add("dve", lambda e, t=t, c=c: e.tensor_tensor(
                out=hid[:, 16 + c, :], in0=tmpf[:, t, :], in1=hid[:, c, :], op=ALU.mult),
                reads=[tmpf_b[t], hid_b[c]], writes=[hid_b[16 + c]])
        S.label = "mix_conv"
        builds = conv_builds(4, DC, 4)
        ready = []
        for _ in range(7):
            ready.append(next(builds))
        for c in range(4, DC):
            pc, pcb = ps_next()
            for k in range(CW):
                nb = next(builds, None)
                if nb is not None:
                    ready.append(nb)
                r = ready.pop(0)
                mm(pc[:, :], dg[:, r, :], aT[:, c, 2 + k:2 + k + TT], k == 0, k == CW - 1,
                   [dg_b[r], aT_b[c]], [pcb])
            S.add("act", lambda e, pc=pc, c=c: e.activation(acc[:, c, :], pc[:, :], AF.Identity,
                                                            bias=vcol(V_CONVB + c)),
                  reads=[pcb, vecs_b], writes=[acc_b[c]])
            i1 = ring("sq", 4)
            S.add("act", lambda e, i=i1, pc=pc, c=c: e.activation(sq[:, i, :], pc[:, :], AF.Identity,
                                                                  bias=vcol(V_CONVB + c)),
                  reads=[pcb, vecs_b], writes=[sq_b[i1]])
            i2 = ring("sq", 4)
            S.add("act", lambda e, i=i2, pc=pc, c=c: e.activation(sq[:, i, :], pc[:, :], AF.Square,
                                                                  bias=vcol(V_CONVB + c)),
                  reads=[pcb, vecs_b], writes=[sq_b[i2]])
            cstat_pending.append((i1, i2, c))
            if len(cstat_pending) > 1:
                cstat_flush_one()
        while cstat_pending:
            cstat_flush_one()
        S.label = "mix_convln"
        S.add("dve", lambda e: e.tensor_scalar(lnst[:, 0, :], ps_s[:, :], 1.0 / D, None, ALU.mult),
              reads=[ps_sb], writes=[lnst_b[0]])
        S.add("dve", lambda e: e.tensor_tensor(out=lnst[:, 1, :], in0=lnst[:, 0, :], in1=lnst[:, 0, :],
                                               op=ALU.mult), reads=[lnst_b[0]], writes=[lnst_b[1]])
        S.add("dve", lambda e: e.scalar_tensor_tensor(
            out=lnst[:, 1, :], in0=ps_q[:, :], scalar=1.0 / D, in1=lnst[:, 1, :],
            op0=ALU.mult, op1=ALU.subtract), reads=[ps_qb, lnst_b[1]], writes=[lnst_b[1]])
        tq = ring("tmpf", 3)
        S.add("act", lambda e: e.activation(tmpf[:, tq, :], lnst[:, 1, :], AF.Sqrt, bias=eps_rms[:, 1:2]),
              reads=[lnst_b[1], eps_b], writes=[tmpf_b[tq]])
        S.add("dve", lambda e: e.reciprocal(lnst[:, 1, :], tmpf[:, tq, :]), reads=[tmpf_b[tq]], writes=[lnst_b[1]])
        S.add("dve", lambda e: e.tensor_tensor(out=lnst[:, 0, :], in0=lnst[:, 0, :], in1=lnst[:, 1, :],
                                               op=ALU.mult), reads=[lnst_b[0], lnst_b[1]], writes=[lnst_b[0]])
        for c in range(DC):
            S.add("pool", lambda e, c=c: e.tensor_tensor(out=acc[:, c, :], in0=acc[:, c, :], in1=lnst[:, 1, :],
                                                         op=ALU.mult),
                  reads=[acc_b[c], lnst_b[1]], writes=[acc_b[c]])

        def convln_chunk(c):
            S.add("dve", lambda e, c=c: e.tensor_tensor(out=acc[:, c, :], in0=acc[:, c, :], in1=lnst[:, 0, :],
                                                        op=ALU.subtract),
                  reads=[acc_b[c], lnst_b[0]], writes=[acc_b[c]])
            S.add("act", lambda e, c=c: e.activation(hid[:, 8 + c, :], acc[:, c, :], AF.Silu,
                                                     bias=vcol(V_CLNB + c), scale=vcol(V_CLNG + c)),
                  reads=[acc_b[c], vecs_b], writes=[hid_b[8 + c]])
        S.label = "mix_wb"
        for p in range(2):
            snb, sbb = W.next(("cols", "wb", p * 512, 512))
            sng, sg = W.next(("cols", "win", 5120 + p * 512, 512))
            wvb, wvg = slot_cols(sbb, 512), slot_cols(sg, 512)
            pgs = []
            for cc in range(4):
                pg, pgb = ps_next()
                for k in range(DC):
                    mm(pg[:, :], wvg[:, k, cc * P:(cc + 1) * P], xn[:, k, :],
                       k == 0, k == DC - 1, [slot_buf[sg], xn_b[k]], [pgb])
                pgs.append((pg, pgb))
            for cc in range(4):
                c = 4 * p + cc
                pg, pgb = pgs[cc]
                py, pyb = ps_next()
                for k in range(DC):
                    mm(py[:, :], wvb[:, k, cc * P:(cc + 1) * P], hid[:, 16 + k, :],
                       k == 0, k == DC - 1, [slot_buf[sbb], hid_b[16 + k]], [pyb])
                S.label = "mix_convln"
                convln_chunk(c)
                S.label = "mix_wb"
                t = ring("tmpf", 3)
                S.add("act", lambda e, t=t, pg=pg, c=c: e.activation(
                    tmpf[:, t, :], pg[:, :], AF.Tanh, bias=bhalf[:, 8 + c:9 + c], scale=0.5),
                    reads=[pgb, bhalf_b], writes=[tmpf_b[t]])
                S.add("dve", lambda e, t=t, py=py, c=c: e.scalar_tensor_tensor(
                    out=t2v[:, c, :], in0=tmpf[:, t, :], scalar=1.0, in1=py[:, :],
                    op0=ALU.add, op1=ALU.mult),
                    reads=[pyb, tmpf_b[t]], writes=[gv_b[c // 2]])
            W.done(snb)
            W.done(sng)
        S.label = "mix_wa"
        for p in range(2):
            sna, sa = W.next(("cols", "wa", p * 512, 512))
            sng, sg = W.next(("cols", "win", 4096 + p * 512, 512))
            wva, wvg = slot_cols(sa, 512), slot_cols(sg, 512)
            for cc in range(4):
                c = 4 * p + cc
                py, pyb = ps_next()
                pg, pgb = ps_next()
                for k in range(DC):
                    mm(py[:, :], wva[:, k, cc * P:(cc + 1) * P], hid[:, 8 + k, :],
                       k == 0, k == DC - 1, [slot_buf[sa], hid_b[8 + k]], [pyb])
                for k in range(DC):
                    mm(pg[:, :], wvg[:, k, cc * P:(cc + 1) * P], xn[:, k, :],
                       k == 0, k == DC - 1, [slot_buf[sg], xn_b[k]], [pgb])
                t = ring("tmpf", 3)
                S.add("act", lambda e, t=t, pg=pg, c=c: e.activation(
                    tmpf[:, t, :], pg[:, :], AF.Tanh, bias=bhalf[:, c:c + 1], scale=0.5),
                    reads=[pgb, bhalf_b], writes=[tmpf_b[t]])
                S.add("dve", lambda e, t=t, py=py: e.scalar_tensor_tensor(
                    out=tmpf[:, t, :], in0=tmpf[:, t, :], scalar=1.0, in1=py[:, :],
                    op0=ALU.add, op1=ALU.mult),
                    reads=[pyb, tmpf_b[t]], writes=[tmpf_b[t]])
                S.add("pool", lambda e, t=t, c=c: e.tensor_tensor(
                    out=hid[:, c, :], in0=tmpf[:, t, :], in1=t2v[:, c, :], op=ALU.add),
                    reads=[tmpf_b[t], gv_b[c // 2]], writes=[hid_b[c]])
            W.done(sna)
            W.done(sng)
        S.label = "mix_wout"
        proj_residual("wout", 0, scale=0.5)
        for c in range(DC):
            S.add("pool", lambda e, c=c: e.tensor_copy(aT[:, c, 0:HALO], aT[:, c, TT:TT + HALO]),
                  reads=[aT_b[c]], writes=[aT_b[c]])

    def proj_residual(key, src0, scale=1.0):
        for p in range(2):
            sn, s = W.next(("cols", key, p * 512, 512))
            wv = slot_cols(s, 512)
            for cc in range(4):
                c = 4 * p + cc
                po, pob = ps_next()
                for k in range(DC):
                    mm(po[:, :], wv[:, k, cc * P:(cc + 1) * P], hid[:, src0 + k, :],
                       k == 0, k == DC - 1, [slot_buf[s], hid_b[src0 + k]], [pob])
                S.add("dve", lambda e, c=c, po=po: e.scalar_tensor_tensor(
                    out=hT[:, c, :], in0=po[:, :], scalar=scale, in1=hT[:, c, :],
                    op0=ALU.mult, op1=ALU.add),
                    reads=[pob, hT_b[c]], writes=[hT_b[c]])
                stat_add(hT, hT_b, c, TT, delay=2)
            W.done(sn)
        stat_flush()

    def xattn():
        n = TT
        S.label = "xat_norm"
        rmsnorm(hT, hT_b, V_G_XAT, n, xn, xn_b, pre=True)
        S.label = "xat_q"
        for p in range(2):
            sn, s = W.next(("cols", "wq", p * 512, 512))
            wv = slot_cols(s, 512)
            for cc in range(4):
                c = 4 * p + cc
                pq, pqb = ps_next()
                for k in range(DC):
                    mm(pq[:, :], wv[:, k, cc * P:(cc + 1) * P], xn[:, k, :],
                       k == 0, k == DC - 1, [slot_buf[s], xn_b[k]], [pqb])
                S.add("act", lambda e, c=c, pq=pq: e.activation(hid[:, 8 + c, :], pq[:, :], AF.Copy,
                                                                scale=1.0 / 16.0),
                      reads=[pqb], writes=[hid_b[8 + c]])
            W.done(sn)
        S.label = "xat_attn"

        def head_scores(h):
            for mc in range(2):
                psc, pscb = ps_next()
                for dc in range(2):
                    mm(psc[:, :], KT[:, 2 * h + dc, mc * P:(mc + 1) * P], hid[:, 8 + 2 * h + dc, :],
                       dc == 0, dc == 1, [KT_b, hid_b[8 + 2 * h + dc]], [pscb])
                S.add("act", lambda e, psc=psc, h=h, mc=mc: e.activation(
                    hid[:, 16 + 2 * h + mc, :], psc[:, :], AF.Exp),
                    reads=[pscb], writes=[hid_b[16 + 2 * h + mc]])

        def head_rest(h):
            pss, pssb = ps_next()
            for mc in range(2):
                mm(pss[:, :], ones_bf[:, :], hid[:, 16 + 2 * h + mc, :], mc == 0, mc == 1,
                   [ones_b, hid_b[16 + 2 * h + mc]], [pssb])
            i = ring("rstd", 2)
            S.add("dve", lambda e, i=i, pss=pss: e.reciprocal(rstd[:, i, :], pss[:, :]),
                  reads=[pssb], writes=[rstd_b[i]])
            for dc in range(2):
                po, pob = ps_next()
                for mc in range(2):
                    mm(po[:, :], Vt[:, mc, (2 * h + dc) * P:(2 * h + dc + 1) * P], hid[:, 16 + 2 * h + mc, :],
                       mc == 0, mc == 1, [Vt_b, hid_b[16 + 2 * h + mc]], [pob])
                S.add("dve", lambda e, i=i, po=po, h=h, dc=dc: e.tensor_tensor(
                    out=hid[:, 2 * h + dc, :], in0=po[:, :], in1=rstd[:, i, :], op=ALU.mult),
                    reads=[pob, rstd_b[i]], writes=[hid_b[2 * h + dc]])

        head_scores(0)
        for h in range(4):
            if h + 1 < 4:
                head_scores(h + 1)
            head_rest(h)
        S.label = "xat_wo"
        proj_residual("wo", 0)

    full = stages in ("all", "mixer", "xattn")
    if full:
        setup_consts()
    if stages in ("all", "xattn"):
        kv_prep()
    fuse_halo = do_halo and full
    if fuse_halo:
        load_x_tile(xh_d, 0, HALO, hTh, hTh_b)
    def tile_head(it):
        load_x_tile(x_d, it * TT, TT)
        S.label = "ffn_norm"
        rmsnorm(hT, hT_b, V_G_FFN1, TT, xn, xn_b)

    for it in range(NT):
        if stages != "all":
            load_x_tile(x_d, it * TT, TT)
        if stages == "xpose":
            store_tile(hT, hT_b, it * TT)
            continue
        if stages == "norm":
            rmsnorm(hT, hT_b, V_G_FFN1, TT, hT, hT_b)
            store_tile(hT, hT_b, it * TT)
            continue
        if stages != "all":
            ffn("1", V_G_FFN1, TT, halo=(fuse_halo and it == 0))
        else:
            if it == 0:
                tile_head(0)
            ffn("1", V_G_FFN1, TT, pre="done", halo=(fuse_halo and it == 0))
        if stages == "ffn1":
            store_tile(hT, hT_b, it * TT)
            continue
        mixer(halo=(fuse_halo and it == 0))
        if stages == "mixer":
            store_tile(hT, hT_b, it * TT)
            continue
        xattn()
        if stages == "xattn":
            store_tile(hT, hT_b, it * TT)
            continue
        ffn("2", V_G_FFN2, TT, pre=True)
        S.label = "final_norm"
        rmsnorm(hT, hT_b, V_G_FIN, TT, acc, acc_b, pre=True)
        if it + 1 < NT:
            tile_head(it + 1)
        store_tile(acc, acc_b, it * TT)

    S.add("sp", lambda e: e.nop(), reads=ystore_b)

    with nc.Block() as block:
        S.emit(nc, block, esem)
    es.close()
    nc._pe_labels = S.labels["pe"]
    return nc


def _pack_inputs(inputs, NT):
    f = lambda a: np.ascontiguousarray(np.asarray(a, dtype=np.float32))
    x = f(inputs["x"])
    mem = f(inputs["mem"])

    def fm(v):
        return f(v).reshape(-1, P).T

    vecs = np.zeros((P, NV), np.float32)
    vecs[:, V_G_FFN1:V_G_FFN1 + 8] = fm(inputs["ffn1_norm"][0])
    vecs[:, V_G_MIX:V_G_MIX + 8] = fm(inputs["mix_norm"][0])
    vecs[:, V_G_XAT:V_G_XAT + 8] = fm(inputs["xattn_norm"][0])
    vecs[:, V_G_MEM:V_G_MEM + 8] = fm(inputs["mem_norm"][0])
    vecs[:, V_G_FFN2:V_G_FFN2 + 8] = fm(inputs["ffn2_norm"][0])
    vecs[:, V_G_FIN:V_G_FIN + 8] = fm(inputs["final_norm"])
    vecs[:, V_BIN:V_BIN + 48] = fm(inputs["b_in"][0])
    vecs[:, V_CONVB:V_CONVB + 8] = fm(inputs["conv_b"][0])
    vecs[:, V_CLNG:V_CLNG + 8] = fm(inputs["conv_ln_g"][0])
    vecs[:, V_CLNB:V_CLNB + 8] = fm(inputs["conv_ln_b"][0])
    cw = f(inputs["conv_w"][0])
    vecs[:, V_CONVW:V_CONVW + CW * 8] = cw.reshape(CW, 8, P).transpose(2, 0, 1).reshape(P, CW * 8)
    vecs[:, V_SLNG:V_SLNG + 8] = fm(inputs["sgu_ln_g"][0])
    vecs[:, V_SLNB:V_SLNB + 8] = fm(inputs["sgu_ln_b"][0])
    rows = np.concatenate([f(inputs["b_in"][0])[3072:4096], f(inputs["sgu_b"][0]).reshape(-1)])[None, :]
    bc = np.broadcast_to(f(inputs["sgu_b"][0]).reshape(1, -1), (P, 512))
    shared = {
        "vecs": vecs, "rows": np.ascontiguousarray(rows), "bc": np.ascontiguousarray(bc),
        "ident": np.eye(P, dtype=np.float32),
        "tril": np.tril(np.ones((P, P), np.float32)),
        "sgu_w": f(inputs["sgu_w"][0]),
        "ffn1_w_gu": f(inputs["ffn1_w_gu"][0]), "ffn1_w_down": f(inputs["ffn1_w_down"][0]),
        "w_in": f(inputs["w_in"][0]), "w_a_out": f(inputs["w_a_out"][0]),
        "w_b_out": f(inputs["w_b_out"][0]), "w_out": f(inputs["w_out"][0]),
        "w_q": f(inputs["w_q"][0]), "w_kv": f(inputs["w_kv"][0]), "w_o": f(inputs["w_o"][0]),
        "ffn2_w_gu": f(inputs["ffn2_w_gu"][0]), "ffn2_w_down": f(inputs["ffn2_w_down"][0]),
    }
    per = NT * TT
    in_maps = []
    for i in range(NCORES):
        b, q = i // 4, i % 4
        t0 = q * per
        m = dict(shared)
        m["x"] = np.ascontiguousarray(x[b, t0:t0 + per])
        if q == 0:
            m["xh"] = np.zeros((HALO, D), np.float32)
            m["hmask"] = np.zeros((P, 1), np.float32)
        else:
            m["xh"] = np.ascontiguousarray(x[b, t0 - HALO:t0])
            m["hmask"] = np.ones((P, 1), np.float32)
        m["mem"] = np.ascontiguousarray(mem[b])
        in_maps.append(m)
    return in_maps


_PROG_CACHE = {}


def _run(inputs, NT=8, stages="all", trace=False):
    key = (NT, stages)
    if key not in _PROG_CACHE:
        _PROG_CACHE[key] = build_program(NT, stages)
    nc = _PROG_CACHE[key]
    in_maps = _pack_inputs(inputs, NT)
    res = run_bass_kernel_spmd(nc, in_maps, core_ids=list(range(NCORES)), trace=trace)
    per = NT * TT
    out = np.zeros((2, 4 * per, D), np.float32)
    for i in range(NCORES):
        b, q = i // 4, i % 4
        out[b, q * per:(q + 1) * per] = res.results[i]["y"]
    return out, res


def kernel(**inputs):
    out, _ = _run(inputs, NT=SEQ // 4 // TT, stages="all")
    return out
```

```python
import contextlib
import os
import numpy as np
import concourse.bass as bass
import concourse.mybir as mybir
from concourse.bass_utils import run_bass_kernel_spmd

F32 = mybir.dt.float32
BF16 = mybir.dt.bfloat16
AF = mybir.ActivationFunctionType
ALU = mybir.AluOpType

P = 128
D = 1024
DC = 8
DFF = 2816
FC = 22
TT = 512
NMEM = 256
HALO = 32
CW = 31
NCORES = 8
SEQ = 16384
NSLOT = 6
SLOT_ELEMS = 4096
EPS_RMS = 1e-6
EPS_LN = 1e-5

V_G_FFN1, V_G_MIX, V_G_XAT, V_G_MEM, V_G_FFN2, V_G_FIN = 0, 8, 16, 24, 32, 40
V_BIN = 48
V_CONVB, V_CLNG, V_CLNB = 96, 104, 112
V_CONVW = 120
V_SLNG = 120 + CW * 8
V_SLNB = V_SLNG + 8
NV = V_SLNB + 8


class Buf:
    __slots__ = ("w", "r", "const")

    def __init__(self, const=False):
        self.w = None
        self.r = []
        self.const = const


class Chan:
    def __init__(self, sem):
        self.sem = sem
        self.count = 0


class Sched:
    ENG = ("pe", "act", "dve", "pool", "sp")

    def __init__(self):
        self.q = {e: [] for e in self.ENG}
        self.label = ""
        self.labels = {e: [] for e in self.ENG}

    def add(self, eng, fn, reads=(), writes=(), chan=None):
        raw, other = [], []
        for b in reads:
            if b.w is not None:
                raw.append(b.w)
        for b in writes:
            if b.w is not None:
                other.append(b.w)
            other.extend(b.r)
        idx = len(self.q[eng])
        if chan is not None:
            chan.count += 16
            ev = ("d", chan, chan.count)
        else:
            ev = ("e", eng, idx)
        self.q[eng].append((fn, raw, other, chan))
        self.labels[eng].append(self.label)
        for b in reads:
            if not b.const:
                b.r.append(ev)
        for b in writes:
            b.w = ev
            b.r = []
        return ev

    def emit(self, nc, block, esem):
        waits = {e: [] for e in self.ENG}
        flagged = {e: set() for e in self.ENG}
        for E in self.ENG:
            seen_e, seen_d = {}, {}
            for idx, (fn, raw, other, chan) in enumerate(self.q[E]):
                need_e, need_d = {}, {}
                for deps, is_raw in ((raw, True), (other, False)):
                    for d in deps:
                        if d[0] == "e":
                            Pn, j = d[1], d[2]
                            if Pn == E and E == "pe":
                                continue
                            if j > need_e.get(Pn, -1):
                                need_e[Pn] = j
                        else:
                            ch, v = d[1], d[2]
                            if v > need_d.get(ch, 0):
                                need_d[ch] = v
                w = []
                for Pn, j in need_e.items():
                    if seen_e.get(Pn, -1) >= j:
                        continue
                    seen_e[Pn] = j
                    flagged[Pn].add(j)
                    w.append(("e", Pn, j))
                for ch, v in need_d.items():
                    if seen_d.get(ch, 0) >= v:
                        continue
                    seen_d[ch] = v
                    w.append(("d", ch, v))
                waits[E].append(w)
        count_at = {}
        for E in self.ENG:
            c = 0
            m = {}
            for idx in range(len(self.q[E])):
                if idx in flagged[E]:
                    c += 1
                    m[idx] = c
            count_at[E] = m

        def run(E, eng):
            for idx, (fn, raw, other, chan) in enumerate(self.q[E]):
                for w in waits[E][idx]:
                    if w[0] == "e":
                        eng.wait_ge(esem[w[1]], count_at[w[1]][w[2]])
                    else:
                        eng.wait_ge(w[1].sem, w[2])
                ins = fn(eng)
                if chan is not None:
                    ins.then_inc(chan.sem, 16)
                elif idx in flagged[E]:
                    ins.then_inc(esem[E], 1)

        @block.sync
        def _(e):
            run("sp", e)

        @block.tensor
        def _(e):
            run("pe", e)

        @block.scalar
        def _(e):
            run("act", e)

        @block.vector
        def _(e):
            run("dve", e)

        @block.gpsimd
        def _(e):
            run("pool", e)


def build_program(NT, stages="all", do_halo=True):
    nc = bass.Bass("TRN2", target_bir_lowering=False)
    S = Sched()
    es = contextlib.ExitStack()

    def dram_in(name, shape, dt=F32):
        return nc.dram_tensor(name, list(shape), dt, kind="ExternalInput").ap()

    NTOK = NT * TT
    x_d = dram_in("x", [NTOK, D])
    xh_d = dram_in("xh", [HALO, D])
    mem_d = dram_in("mem", [NMEM, D])
    vecs_d = dram_in("vecs", [P, NV])
    rows_d = dram_in("rows", [1, 1536])
    bc_d = dram_in("bc", [P, 512])
    hmask_d = dram_in("hmask", [P, 1])
    ident_d = dram_in("ident", [P, P])
    tril_d = dram_in("tril", [P, P])
    sguw_d = dram_in("sgu_w", [4, P, P])
    w_d = {
        "gu1": dram_in("ffn1_w_gu", [D, 2 * DFF]),
        "dn1": dram_in("ffn1_w_down", [DFF, D]),
        "win": dram_in("w_in", [D, 6 * D]),
        "wa": dram_in("w_a_out", [D, D]),
        "wb": dram_in("w_b_out", [D, D]),
        "wout": dram_in("w_out", [D, D]),
        "wq": dram_in("w_q", [D, D]),
        "wkv": dram_in("w_kv", [D, 2 * D]),
        "wo": dram_in("w_o", [D, D]),
        "gu2": dram_in("ffn2_w_gu", [D, 2 * DFF]),
        "dn2": dram_in("ffn2_w_down", [DFF, D]),
    }
    y_d = nc.dram_tensor("y", [NTOK, D], F32, kind="ExternalOutput").ap()

    def sb(name, shape, dt):
        return es.enter_context(nc.sbuf_tensor("sb_" + name, list(shape), dt))

    def sem(name):
        return es.enter_context(nc.semaphore(name))

    esem = {e: sem("prog_" + e) for e in ("pe", "act", "dve", "pool")}

    slice_ids = {}
    slice_list = []

    def slice_id(desc):
        if desc not in slice_ids:
            slice_ids[desc] = len(slice_list)
            slice_list.append(desc)
        return slice_ids[desc]

    def ffn_slices(tag):
        out = []
        for jj in range(11):
            out.append(("gu", "gu" + tag, jj))
        for c in range(DC):
            out.append(("down", "dn" + tag, c))
        return out

    def cols_slices(key, lo, n):
        return [("cols", key, lo + i * 512, 512) for i in range(n)]

    mixer_slices = (
        [("cols", "win", 0, 512), ("cols", "win", 1024, 512),
         ("cols", "win", 512, 512), ("cols", "win", 1536, 512)]
        + cols_slices("win", 3072, 2)
        + cols_slices("win", 2048, 2)
        + [("cols", "wb", 0, 512), ("cols", "win", 5120, 512),
           ("cols", "wb", 512, 512), ("cols", "win", 5632, 512)]
        + [("cols", "wa", 0, 512), ("cols", "win", 4096, 512),
           ("cols", "wa", 512, 512), ("cols", "win", 4608, 512)]
        + cols_slices("wout", 0, 2)
    )
    xattn_slices = cols_slices("wq", 0, 2) + cols_slices("wo", 0, 2)
    kv_slices = cols_slices("wkv", 0, 4)
    halo_slices = ffn_slices("1") + mixer_slices[:4]

    tile_slices = ffn_slices("1")
    if stages in ("xpose", "norm"):
        tile_slices = []
    if stages in ("all", "mixer", "xattn"):
        tile_slices = tile_slices + mixer_slices
    if stages in ("all", "xattn"):
        tile_slices = tile_slices + xattn_slices
    if stages == "all":
        tile_slices = tile_slices + ffn_slices("2")

    plan = []
    if stages in ("all", "xattn"):
        plan += kv_slices
    for _ in range(NT):
        plan += tile_slices
    for d_ in plan:
        slice_id(d_)
    NSL = len(slice_list)

    scratch = nc.dram_tensor("wscratch", [max(NSL, 1), P, SLOT_ELEMS], BF16, kind="Internal").ap()
    scratch_buf = [Buf() for _ in range(NSL)]
    in_scratch = [False] * NSL
    n_uses = [0] * NSL
    for d_ in plan:
        n_uses[slice_ids[d_]] += 1
    slots_t = sb("wslots", [P, NSLOT, SLOT_ELEMS], BF16)
    slot_buf = [Buf() for _ in range(NSLOT)]
    slot_chan = [Chan(sem("wslot%d" % i)) for i in range(NSLOT)]
    store_chan = [Chan(sem("wstore%d" % i)) for i in range(NSLOT)]
    cast_chan = [Chan(sem("wcast%d" % i)) for i in range(NSLOT)]

    def slice_elems(desc):
        if desc[0] == "cols":
            return 8 * desc[3]
        return {"gu": 4096, "down": FC * P}[desc[0]]

    def emit_first_fetch(sid, s):
        desc = slice_list[sid]
        kind = desc[0]
        pieces = []
        if kind == "cols":
            _, key, c0, ncols = desc
            src_ = w_d[key][:, c0:c0 + ncols].rearrange("(kc p) n -> p kc n", p=P)
            dst = slots_t[:, s, 0:8 * ncols].rearrange("p (kc n) -> p kc n", n=ncols)
            pieces.append((dst, src_))
        elif kind == "gu":
            _, key, jj = desc
            dst_all = slots_t[:, s, 0:4096].rearrange("p (kc n) -> p kc n", n=512)
            for half in range(2):
                c0 = half * DFF + jj * 256
                src_ = w_d[key][:, c0:c0 + 256].rearrange("(kc p) n -> p kc n", p=P)
                pieces.append((dst_all[:, :, half * 256:(half + 1) * 256], src_))
        else:
            _, key, c = desc
            src_ = w_d[key][:, c * P:(c + 1) * P].rearrange("(kc p) n -> p kc n", p=P)
            dst = slots_t[:, s, 0:FC * P].rearrange("p (kc n) -> p kc n", n=P)
            pieces.append((dst, src_))
        for dst, src_ in pieces:
            S.add("pool", lambda e, dst=dst, src_=src_: e.dma_start(out=dst, in_=src_),
                  writes=[slot_buf[s]], chan=cast_chan[s])
        if n_uses[sid] > 1:
            ne = slice_elems(desc)
            S.add("sp", lambda e, ne=ne: e.dma_start(out=scratch[sid, :, 0:ne], in_=slots_t[:, s, 0:ne]),
                  reads=[slot_buf[s]], writes=[scratch_buf[sid]], chan=store_chan[s])
        in_scratch[sid] = True

    class WStream:
        def __init__(self):
            self.next_load = 0
            self.next_use = 0
            self.done_upto = 0
            self.done_flags = [False] * len(plan)

        def pump(self):
            while self.next_load < len(plan) and self.next_load - NSLOT < self.done_upto:
                n = self.next_load
                sid = slice_ids[plan[n]]
                s = n % NSLOT
                desc = plan[n]
                if not in_scratch[sid]:
                    emit_first_fetch(sid, s)
                else:
                    ne = slice_elems(desc)
                    dst = slots_t[:, s, 0:ne]
                    src = scratch[sid, :, 0:ne]
                    S.add("sp", lambda e, dst=dst, src=src: e.dma_start(out=dst, in_=src),
                          reads=[scratch_buf[sid]], writes=[slot_buf[s]], chan=slot_chan[s])
                self.next_load += 1

        def next(self, desc):
            n = self.next_use
            assert plan[n] == desc, (n, plan[n], desc)
            self.next_use += 1
            self.pump()
            assert self.next_load > n
            s = n % NSLOT
            return n, s

        def done(self, n):
            self.done_flags[n] = True
            while self.done_upto < len(plan) and self.done_flags[self.done_upto]:
                self.done_upto += 1
            self.pump()

    W = WStream()

    def slot_cols(s, ncols):
        return slots_t[:, s, 0:8 * ncols].rearrange("p (kc n) -> p kc n", n=ncols)

    def slot_down(s):
        return slots_t[:, s, 0:FC * P].rearrange("p (kc n) -> p kc n", n=P)

    vecs = sb("vecs", [P, NV], F32)
    vecs_b = Buf(const=True)
    ident = sb("ident", [P, P], F32)
    ident_b = Buf(const=True)
    ones_bf = sb("ones_bf", [P, P], BF16)
    ones_b = Buf(const=True)
    hmask = sb("hmask", [P, 1], F32)
    hmask_b = Buf(const=True)
    S.add("sp", lambda e: e.dma_start(out=vecs[:, :], in_=vecs_d[:, :]), writes=[vecs_b], chan=Chan(sem("c_vecs")))
    S.add("sp", lambda e: e.dma_start(out=ident[:, :], in_=ident_d[:, :]), writes=[ident_b], chan=Chan(sem("c_ident")))
    DBG = os.environ.get("KDBG", "")
    if "B" not in DBG:
        S.add("sp", lambda e: e.dma_start(out=hmask[:, :], in_=hmask_d[:, :]), writes=[hmask_b], chan=Chan(sem("c_hmask")))
    if "A" not in DBG:
        S.add("dve", lambda e: e.memset(ones_bf[:, :], 1.0), writes=[ones_b])

    def vcol(c):
        return vecs[:, c:c + 1]

    bhalf = sb("bhalf", [P, 16], F32)
    bhalf_b = Buf(const=True)
    S.add("dve", lambda e: e.tensor_scalar(bhalf[:, :], vecs[:, V_BIN + 32:V_BIN + 48], 0.5, None, ALU.mult),
          reads=[vecs_b], writes=[bhalf_b])

    hT = sb("hT", [P, DC, TT], F32)
    hT_b = [Buf() for _ in range(DC)]
    hT_main, hT_main_b = hT, hT_b
    hTh = sb("hTh", [P, DC, HALO], F32)
    hTh_b = [Buf() for _ in range(DC)]
    xnh = sb("xnh", [P, DC, HALO], BF16)
    xnh_b = [Buf() for _ in range(DC)]
    hidh = sb("hidh", [P, FC, HALO], BF16)
    hidh_b = [Buf() for _ in range(FC)]
    xn = sb("xn", [P, DC, TT], BF16)
    xn_b = [Buf() for _ in range(DC)]
    hid = sb("hid", [P, 24, TT], BF16)
    hid_b = [Buf() for _ in range(24)]
    sq = sb("sq", [P, 4, TT], BF16)
    sq_b = [Buf() for _ in range(4)]
    rstd = sb("rstd", [P, 2, TT], F32)
    rstd_b = [Buf(), Buf()]
    tmpf = sb("tmpf", [P, 3, TT], F32)
    tmpf_b = [Buf() for _ in range(3)]
    xs = sb("xs", [P, 3, D], F32)
    xs_b = [Buf() for _ in range(3)]
    xs_chan = [Chan(sem("xs%d" % i)) for i in range(3)]
    os_chan = [Chan(sem("os%d" % i)) for i in range(4)]
    ystore_b = [Buf() for _ in range(4)]

    psum = [es.enter_context(nc.psum_tensor("ps%d" % i, [P, TT], F32)) for i in range(8)]
    psum_b = [Buf() for _ in range(8)]
    ps_ctr = [0]

    def ps_next():
        i = ps_ctr[0] % 6
        ps_ctr[0] += 1
        return psum[i], psum_b[i]

    rr = {"sq": 0, "rstd": 0, "tmpf": 0, "xs": 0, "os": 0}

    def ring(name, n):
        i = rr[name] % n
        rr[name] += 1
        return i

    def mm(out, lhsT, rhs, start, stop, reads, writes):
        S.add("pe", lambda e: e.matmul(out, lhsT, rhs, start=start, stop=stop),
              reads=reads, writes=writes)

    def load_x_tile(src_d, r0, ntok, hT=None, hT_b=None):
        if hT is None:
            hT, hT_b = hT_main, hT_main_b
        S.label = "load_x"
        nch = (ntok + P - 1) // P
        for tc in range(nch):
            n = min(P, ntok - tc * P)
            i = ring("xs", 3)
            S.add("sp", lambda e, i=i, n=n, tc=tc: e.dma_start(
                out=xs[0:n, i, :], in_=src_d[r0 + tc * P:r0 + tc * P + n, :]),
                writes=[xs_b[i]], chan=xs_chan[i])
            for g in range(2):
                pt, pb = ps_next()
                for j in range(4):
                    c = 4 * g + j
                    S.add("pe", lambda e, pt=pt, j=j, i=i, n=n, c=c: e.transpose(
                        pt[:, j * P:j * P + n], xs[0:n, i, c * P:(c + 1) * P], ident[0:n, 0:n]),
                        reads=[xs_b[i], ident_b], writes=[pb])
                for j in range(4):
                    c = 4 * g + j
                    eng = "dve"
                    if eng == "act":
                        S.add("act", lambda e, pt=pt, j=j, n=n, c=c, tc=tc: e.activation(
                            hT[:, c, tc * P:tc * P + n], pt[:, j * P:j * P + n], AF.Copy),
                            reads=[pb], writes=[hT_b[c]])
                    else:
                        S.add("dve", lambda e, pt=pt, j=j, n=n, c=c, tc=tc: e.tensor_copy(
                            hT[:, c, tc * P:tc * P + n], pt[:, j * P:j * P + n]),
                            reads=[pb], writes=[hT_b[c]])

    stat_pending = []
    actdummy = sb("actdummy", [P, 2], F32)
    actdummy_b = Buf()

    def stat_add(src, src_b, c, n, delay=0):
        if c == 0:
            S.add("act", lambda e: e.activation(actdummy[:, 0:1], eps_rms[:, 0:1], AF.Sqrt),
                  reads=[eps_b], writes=[actdummy_b])
        i = ring("sq", 4)
        S.add("act", lambda e, i=i, c=c: e.activation(sq[:, i, 0:n], src[:, c, 0:n], AF.Square),
              reads=[src_b[c]], writes=[sq_b[i]])
        stat_pending.append((i, c, n))
        while len(stat_pending) > delay:
            stat_flush_one()

    def stat_flush_one():
        i, c, n = stat_pending.pop(0)
        mm(psum[6][:, 0:n], ones_bf[:, :], sq[:, i, 0:n], c == 0, c == DC - 1,
           [sq_b[i], ones_b], [psum_b[6]])

    def stat_flush():
        while stat_pending:
            stat_flush_one()

    def rmsnorm(src, src_b, gcol, n, dst, dst_b, nchunks=DC, pre=False):
        pt, pb = psum[6], psum_b[6]
        if not pre:
            for c in range(nchunks):
                stat_add(src, src_b, c, n)
        i = ring("rstd", 2)
        t = ring("tmpf", 3)
        S.add("act", lambda e: e.activation(tmpf[:, t, 0:n], pt[:, 0:n], AF.Sqrt,
                                            bias=eps_rms[:, 0:1], scale=1.0 / D),
              reads=[pb, eps_b], writes=[tmpf_b[t]])
        S.add("dve", lambda e: e.reciprocal(rstd[:, i, 0:n], tmpf[:, t, 0:n]),
              reads=[tmpf_b[t]], writes=[rstd_b[i]])
        for c in range(nchunks):
            S.add("dve", lambda e, c=c: e.scalar_tensor_tensor(
                out=dst[:, c, 0:n], in0=src[:, c, 0:n], scalar=vcol(gcol + c),
                in1=rstd[:, i, 0:n], op0=ALU.mult, op1=ALU.mult),
                reads=[src_b[c], rstd_b[i], vecs_b], writes=[dst_b[c]])

    eps_rms = sb("eps_rms", [P, 2], F32)
    eps_b = Buf(const=True)
    if "A" not in DBG:
        S.add("dve", lambda e: e.memset(eps_rms[:, 0:1], EPS_RMS), writes=[eps_b])
        S.add("dve", lambda e: e.memset(eps_rms[:, 1:2], EPS_LN), writes=[eps_b])

    def ffn(tag, gcol, n, pre=False, halo=False):
        S.label = "ffn_norm"
        if pre != "done":
            rmsnorm(hT, hT_b, gcol, n, xn, xn_b, pre=pre)
        if halo:
            rmsnorm(hTh, hTh_b, gcol, HALO, xnh, xnh_b)
        S.label = "ffn_up"
        for jj in range(11):
            sn, s = W.next(("gu", "gu" + tag, jj))
            wv = slot_cols(s, 512)
            for sub in range(2):
                j = 2 * jj + sub
                pg, pgb = ps_next()
                pu, pub = ps_next()
                for k in range(DC):
                    mm(pg[:, 0:n], wv[:, k, sub * P:(sub + 1) * P], xn[:, k, 0:n],
                       k == 0, k == DC - 1, [slot_buf[s], xn_b[k]], [pgb])
                for k in range(DC):
                    mm(pu[:, 0:n], wv[:, k, 256 + sub * P:256 + (sub + 1) * P], xn[:, k, 0:n],
                       k == 0, k == DC - 1, [slot_buf[s], xn_b[k]], [pub])
                t = ring("tmpf", 3)
                S.add("act", lambda e, t=t, pg=pg: e.activation(tmpf[:, t, 0:n], pg[:, 0:n], AF.Silu),
                      reads=[pgb], writes=[tmpf_b[t]])
                S.add("dve", lambda e, t=t, pu=pu, j=j: e.tensor_tensor(
                    out=hid[:, j, 0:n], in0=pu[:, 0:n], in1=tmpf[:, t, 0:n], op=ALU.mult),
                    reads=[pub, tmpf_b[t]], writes=[hid_b[j]])
                if halo:
                    ph, phb = ps_next()
                    for k in range(DC):
                        mm(ph[:, 0:HALO], wv[:, k, sub * P:(sub + 1) * P], xnh[:, k, :],
                           k == 0, k == DC - 1, [slot_buf[s], xnh_b[k]], [phb])
                    for k in range(DC):
                        mm(ph[:, HALO:2 * HALO], wv[:, k, 256 + sub * P:256 + (sub + 1) * P], xnh[:, k, :],
                           k == 0, k == DC - 1, [slot_buf[s], xnh_b[k]], [phb])
                    t = ring("tmpf", 3)
                    S.add("act", lambda e, t=t, ph=ph: e.activation(tmpf[:, t, 0:HALO], ph[:, 0:HALO], AF.Silu),
                          reads=[phb], writes=[tmpf_b[t]])
                    S.add("dve", lambda e, t=t, ph=ph, j=j: e.tensor_tensor(
                        out=hidh[:, j, :], in0=ph[:, HALO:2 * HALO], in1=tmpf[:, t, 0:HALO], op=ALU.mult),
                        reads=[phb, tmpf_b[t]], writes=[hidh_b[j]])
            W.done(sn)
        S.label = "ffn_down"
        for c in range(DC):
            sn, s = W.next(("down", "dn" + tag, c))
            wv = slot_down(s)
            po, pob = ps_next()
            for j in range(FC):
                mm(po[:, 0:n], wv[:, j, :], hid[:, j, 0:n], j == 0, j == FC - 1,
                   [slot_buf[s], hid_b[j]], [pob])
            if halo:
                ph, phb = ps_next()
                for j in range(FC):
                    mm(ph[:, 0:HALO], wv[:, j, :], hidh[:, j, :], j == 0, j == FC - 1,
                       [slot_buf[s], hidh_b[j]], [phb])
            W.done(sn)
            S.add("dve", lambda e, c=c, po=po: e.scalar_tensor_tensor(
                out=hT[:, c, 0:n], in0=po[:, 0:n], scalar=0.5, in1=hT[:, c, 0:n],
                op0=ALU.mult, op1=ALU.add),
                reads=[pob, hT_b[c]], writes=[hT_b[c]])
            stat_add(hT, hT_b, c, n, delay=1)
            if halo:
                S.add("dve", lambda e, c=c, ph=ph: e.scalar_tensor_tensor(
                    out=hTh[:, c, :], in0=ph[:, 0:HALO], scalar=0.5, in1=hTh[:, c, :],
                    op0=ALU.mult, op1=ALU.add),
                    reads=[phb, hTh_b[c]], writes=[hTh_b[c]])
        stat_flush()

    def store_tile(src, src_b, r0):
        S.label = "store"
        for tc in range(TT // P):
            i = ring("os", 4)
            for g in range(2):
                pt, pb = ps_next()
                for j in range(4):
                    c = 4 * g + j
                    S.add("pe", lambda e, pt=pt, j=j, c=c, tc=tc: e.transpose(
                        pt[:, j * P:(j + 1) * P], src[:, c, tc * P:(tc + 1) * P], ident[:, :]),
                        reads=[src_b[c], ident_b], writes=[pb])
                if g == 0:
                    S.add("act", lambda e, pt=pt, i=i: e.activation(osb[:, i, 0:512], pt[:, :], AF.Copy),
                          reads=[pb], writes=[os_b[i]])
                else:
                    S.add("act", lambda e, pt=pt, i=i: e.activation(osb[:, i, 512:1024], pt[:, :], AF.Copy),
                          reads=[pb], writes=[os_b[i]])
            S.add("sp", lambda e, i=i, tc=tc: e.dma_start(
                out=y_d[r0 + tc * P:r0 + (tc + 1) * P, :], in_=osb[:, i, :]),
                reads=[os_b[i]], writes=[ystore_b[i]], chan=os_chan[i])


    aT = sb("aT", [P, DC, HALO + TT], BF16)
    aT_b = [Buf() for _ in range(DC)]
    acc = sb("acc", [P, DC, TT], F32)
    acc_b = [Buf() for _ in range(DC)]
    gv = sb("gv", [P, 4, D], F32)
    gv_b = [Buf() for _ in range(4)]
    t2v = gv[:, :, :].rearrange("p a (b n) -> p (a b) n", n=TT)
    osb = gv
    os_b = gv_b
    vtok = sb("vtok", [P, 4, D], BF16)
    vtok_b = [Buf() for _ in range(4)]
    bct = sb("bct", [P, 512], F32)
    bct_b = Buf(const=True)
    brow = sb("brow", [P, 2, 1024], BF16)
    brow_b = Buf(const=True)
    rowsf = sb("rowsf", [1, 1536], F32)
    rowsf_b = Buf()
    WsT = sb("WsT", [P, 4, P], BF16)
    WsT_b = Buf(const=True)
    KT = sb("KT", [P, DC, NMEM], BF16)
    KT_b = Buf(const=True)
    Vt = sb("Vt", [P, 2, D], BF16)
    Vt_b = Buf(const=True)
    lnst = sb("lnst", [P, 2, TT], F32)
    lnst_b = [Buf() for _ in range(2)]
    bst = sb("bst", [P, 4, 12], F32)
    bst_b = Buf()
    mv = sb("mv", [P, 4, 2], F32)
    mv_b = Buf()
    sdv = sb("sdv", [P, 8], F32)
    sdv_b = Buf()
    identb = sb("identb", [P, P], BF16)
    identb_b = Buf(const=True)
    dg = sb("dg", [P, 8, P], BF16)
    dg_b = [Buf() for _ in range(8)]
    rr["dg"] = 0
    B2 = sb("B2", [P, DC, P], F32)
    B2_b = Buf(const=True)

    def setup_consts():
        S.add("sp", lambda e: e.dma_start(out=bct[:, :], in_=bc_d[:, :]), writes=[bct_b], chan=Chan(sem("c_bc")))
        S.add("sp", lambda e: e.dma_start(out=rowsf[:, :], in_=rows_d[:, :]), writes=[rowsf_b], chan=Chan(sem("c_rows")))
        S.add("pool", lambda e: e.memset(brow[:, :, :], 0.0), writes=[brow_b])
        S.add("dve", lambda e: e.tensor_copy(brow[0:1, 0, :], rowsf[0:1, 0:1024]), reads=[rowsf_b, brow_b], writes=[brow_b])
        S.add("dve", lambda e: e.tensor_tensor(out=brow[0:1, 1, :], in0=rowsf[0:1, 0:1024], in1=brow[0:1, 0, :],
                                               op=ALU.subtract), reads=[rowsf_b, brow_b], writes=[brow_b])
        wv = tmpf[:, 0, :].rearrange("p (g s) -> p g s", s=P)
        S.add("sp", lambda e: e.dma_start(out=wv, in_=sguw_d.rearrange("g t s -> t g s")),
              writes=[tmpf_b[0]], chan=Chan(sem("c_sguw")))
        S.add("sp", lambda e: e.dma_start(out=tmpf[:, 1, 0:P], in_=tril_d[:, :]),
              writes=[tmpf_b[1]], chan=Chan(sem("c_tril")))
        pt, pb = ps_next()
        for g in range(4):
            S.add("dve", lambda e, g=g: e.tensor_tensor(out=wv[:, g, :], in0=wv[:, g, :], in1=tmpf[:, 1, 0:P],
                                                        op=ALU.mult),
                  reads=[tmpf_b[0], tmpf_b[1]], writes=[tmpf_b[0]])
        for g in range(4):
            S.add("pe", lambda e, g=g: e.transpose(pt[:, g * P:(g + 1) * P], wv[:, g, :], ident[:, :]),
                  reads=[tmpf_b[0], ident_b], writes=[pb])
        S.add("dve", lambda e: e.tensor_copy(WsT[:, :, :].rearrange("p g t -> p (g t)"), pt[:, :]),
              reads=[pb], writes=[WsT_b])
        S.add("dve", lambda e: e.tensor_copy(identb[:, :], ident[:, :]), reads=[ident_b], writes=[identb_b])
        p2, p2b = ps_next()
        for g in range(4):
            mm(p2[:, g * P:(g + 1) * P], ones_bf[:, :], WsT[:, g, :], True, True, [ones_b, WsT_b], [p2b])
        for c in range(DC):
            g = c // 2
            S.add("dve", lambda e, c=c, g=g: e.scalar_tensor_tensor(
                out=B2[:, c, :], in0=p2[:, g * P:(g + 1) * P], scalar=vcol(V_SLNB + c),
                in1=bct[:, g * P:(g + 1) * P], op0=ALU.mult, op1=ALU.add),
                reads=[p2b, vecs_b, bct_b], writes=[B2_b])

    def kv_prep():
        load_x_tile(mem_d, 0, NMEM)
        rmsnorm(hT, hT_b, V_G_MEM, NMEM, xn, xn_b)
        for p in range(2):
            sn, s = W.next(("cols", "wkv", p * 512, 512))
            wv = slot_cols(s, 512)
            for cc in range(4):
                c = 4 * p + cc
                pk, pkb = ps_next()
                for k in range(DC):
                    mm(pk[:, 0:NMEM], wv[:, k, cc * P:(cc + 1) * P], xn[:, k, 0:NMEM],
                       k == 0, k == DC - 1, [slot_buf[s], xn_b[k]], [pkb])
                S.add("act", lambda e, c=c, pk=pk: e.activation(KT[:, c, :], pk[:, 0:NMEM], AF.Copy),
                      reads=[pkb], writes=[KT_b])
            W.done(sn)
        for p in range(2):
            sn, s = W.next(("cols", "wkv", 1024 + p * 512, 512))
            wv = slot_cols(s, 512)
            for mc in range(2):
                pv, pvb = ps_next()
                for k in range(DC):
                    mm(pv[:, :], xn[:, k, mc * P:(mc + 1) * P], wv[:, k, :],
                       k == 0, k == DC - 1, [slot_buf[s], xn_b[k]], [pvb])
                S.add("act", lambda e, mc=mc, p=p, pv=pv: e.activation(
                    Vt[:, mc, p * 512:(p + 1) * 512], pv[:, :], AF.Copy), reads=[pvb], writes=[Vt_b])
            W.done(sn)

    def mixer_a_proj(n, col0, halo=False):
        streams = [(xn, xn_b, n, col0)]
        if halo:
            streams.append((xnh, xnh_b, HALO, 0))
        for p in range(2):
            snv, sv = W.next(("cols", "win", p * 512, 512))
            sng, sg = W.next(("cols", "win", 1024 + p * 512, 512))
            wvv, wvg = slot_cols(sv, 512), slot_cols(sg, 512)
            for cc in range(4):
                c = 4 * p + cc
                for (sx, sx_b, sn_, sc0) in streams:
                    pv, pvb = ps_next()
                    pg, pgb = ps_next()
                    for k in range(DC):
                        mm(pv[:, 0:sn_], wvv[:, k, cc * P:(cc + 1) * P], sx[:, k, 0:sn_],
                           k == 0, k == DC - 1, [slot_buf[sv], sx_b[k]], [pvb])
                    for k in range(DC):
                        mm(pg[:, 0:sn_], wvg[:, k, cc * P:(cc + 1) * P], sx[:, k, 0:sn_],
                           k == 0, k == DC - 1, [slot_buf[sg], sx_b[k]], [pgb])
                    t = ring("tmpf", 3)
                    S.add("act", lambda e, t=t, pg=pg, c=c, sn_=sn_: e.activation(
                        tmpf[:, t, 0:sn_], pg[:, 0:sn_], AF.Sigmoid, bias=vcol(V_BIN + 8 + c)),
                        reads=[pgb, vecs_b], writes=[tmpf_b[t]])
                    S.add("dve", lambda e, t=t, pv=pv, c=c, sn_=sn_, sc0=sc0: e.scalar_tensor_tensor(
                        out=aT[:, c, sc0:sc0 + sn_], in0=pv[:, 0:sn_], scalar=vcol(V_BIN + c),
                        in1=tmpf[:, t, 0:sn_], op0=ALU.add, op1=ALU.mult),
                        reads=[pvb, tmpf_b[t], vecs_b], writes=[aT_b[c]])
                if halo:
                    S.add("dve", lambda e, c=c: e.tensor_scalar(aT[:, c, 0:HALO], aT[:, c, 0:HALO],
                                                                hmask[:, 0:1], None, ALU.mult),
                          reads=[aT_b[c], hmask_b], writes=[aT_b[c]])
            W.done(snv)
            W.done(sng)

    def mixer(halo=False):
        n = TT
        S.label = "mix_norm"
        rmsnorm(hT, hT_b, V_G_MIX, n, xn, xn_b, pre=True)
        if halo:
            rmsnorm(hTh, hTh_b, V_G_MIX, HALO, xnh, xnh_b)
        S.label = "mix_aproj"
        mixer_a_proj(n, HALO, halo=halo)
        S.label = "mix_v"
        for p in range(2):
            sn, s = W.next(("cols", "win", 3072 + p * 512, 512))
            wv = slot_cols(s, 512)
            for tc in range(4):
                pv, pvb = ps_next()
                for k in range(DC):
                    mm(pv[:, :], xn[:, k, tc * P:(tc + 1) * P], wv[:, k, :], k == 0, False,
                       [slot_buf[s], xn_b[k]], [pvb])
                mm(pv[:, :], ones_bf[:, :], brow[:, 0, p * 512:(p + 1) * 512], False, False,
                   [ones_b, brow_b], [pvb])
                mm(pv[:, :], ones_bf[:, :], brow[:, 1, p * 512:(p + 1) * 512], False, True,
                   [ones_b, brow_b], [pvb])
                S.add("act", lambda e, pv=pv, tc=tc, p=p: e.activation(
                    gv[:, tc, p * 512:(p + 1) * 512], pv[:, :], AF.Gelu_apprx_tanh),
                    reads=[pvb], writes=[gv_b[tc]])
            W.done(sn)
        for tc in range(4):
            for hh in range(2):
                S.add("dve", lambda e, tc=tc, hh=hh: e.bn_stats(
                    bst[:, tc, hh * 6:(hh + 1) * 6], gv[:, tc, hh * 512:(hh + 1) * 512]),
                    reads=[gv_b[tc]], writes=[bst_b])
            S.add("dve", lambda e, tc=tc: e.bn_aggr(
                mv[:, tc, :], bst[:, tc, :].rearrange("p (a b) -> p a b", b=6)),
                reads=[bst_b], writes=[mv_b])
        S.add("act", lambda e: e.activation(sdv[:, 0:4], mv[:, :, 1], AF.Sqrt, bias=eps_rms[:, 1:2]),
              reads=[mv_b, eps_b], writes=[sdv_b])
        S.add("dve", lambda e: e.reciprocal(sdv[:, 4:8], sdv[:, 0:4]), reads=[sdv_b], writes=[sdv_b])
        for tc in range(4):
            S.add("dve", lambda e, tc=tc: e.tensor_scalar(
                vtok[:, tc, :], gv[:, tc, :], mv[:, tc, 0:1], sdv[:, 4 + tc:5 + tc], ALU.subtract, ALU.mult),
                reads=[gv_b[tc], mv_b, sdv_b], writes=[vtok_b[tc]])
        S.label = "mix_u"
        for p in range(2):
            sn, s = W.next(("cols", "win", 2048 + p * 512, 512))
            wv = slot_cols(s, 512)
            for cc in range(4):
                c = 4 * p + cc
                pu, pub = ps_next()
                for k in range(DC):
                    mm(pu[:, :], wv[:, k, cc * P:(cc + 1) * P], xn[:, k, :],
                       k == 0, k == DC - 1, [slot_buf[s], xn_b[k]], [pub])
                S.add("act", lambda e, pu=pu, c=c: e.activation(
                    hid[:, c, :], pu[:, :], AF.Gelu_apprx_tanh, bias=vcol(V_BIN + 16 + c)),
                    reads=[pub, vecs_b], writes=[hid_b[c]])
            W.done(sn)
        S.label = "mix_conv"
        ps_s, ps_sb = psum[6], psum_b[6]
        ps_q, ps_qb = psum[7], psum_b[7]
        cstat_pending = []

        def cstat_flush_one():
            j1, j2, cj = cstat_pending.pop(0)
            mm(ps_s[:, :], ones_bf[:, :], sq[:, j1, :], cj == 0, cj == DC - 1, [sq_b[j1], ones_b], [ps_sb])
            mm(ps_q[:, :], ones_bf[:, :], sq[:, j2, :], cj == 0, cj == DC - 1, [sq_b[j2], ones_b], [ps_qb])
        def conv_builds(c_lo, c_hi, act_chunk):
            for c in range(c_lo, c_hi):
                for k in range(CW):
                    r = ring("dg", 8)
                    if k % 3 == 0 or c == act_chunk:
                        S.add("act", lambda e, r=r, k=k, c=c: e.activation(
                            dg[:, r, :], identb[:, :], AF.Copy, scale=vcol(V_CONVW + k * 8 + c)),
                            reads=[identb_b, vecs_b], writes=[dg_b[r]])
                    else:
                        S.add("dve", lambda e, r=r, k=k, c=c: e.tensor_scalar(
                            dg[:, r, :], identb[:, :], vcol(V_CONVW + k * 8 + c), None, ALU.mult),
                            reads=[identb_b, vecs_b], writes=[dg_b[r]])
                    yield r

        builds = conv_builds(0, 4, -1)
        ready = []
        for _ in range(7):
            ready.append(next(builds))
        for c in range(0, 4):
            pc, pcb = ps_next()
            for k in range(CW):
                nb = next(builds, None)
                if nb is not None:
                    ready.append(nb)
                r = ready.pop(0)
                mm(pc[:, :], dg[:, r, :], aT[:, c, 2 + k:2 + k + TT], k == 0, k == CW - 1,
                   [dg_b[r], aT_b[c]], [pcb])
            S.add("act", lambda e, pc=pc, c=c: e.activation(acc[:, c, :], pc[:, :], AF.Identity,
                                                            bias=vcol(V_CONVB + c)),
                  reads=[pcb, vecs_b], writes=[acc_b[c]])
            i1 = ring("sq", 4)
            S.add("act", lambda e, i=i1, pc=pc, c=c: e.activation(sq[:, i, :], pc[:, :], AF.Identity,
                                                                  bias=vcol(V_CONVB + c)),
                  reads=[pcb, vecs_b], writes=[sq_b[i1]])
            i2 = ring("sq", 4)
            S.add("act", lambda e, i=i2, pc=pc, c=c: e.activation(sq[:, i, :], pc[:, :], AF.Square,
                                                                  bias=vcol(V_CONVB + c)),
                  reads=[pcb, vecs_b], writes=[sq_b[i2]])
            cstat_pending.append((i1, i2, c))
            if len(cstat_pending) > 1:
                cstat_flush_one()
        S.label = "mix_sgu"
        for c in range(DC):
            pm, pmb = ps_next()
            g = c // 2
            for tc in range(4):
                mm(pm[:, tc * P:(tc + 1) * P], vtok[:, tc, c * P:(c + 1) * P], WsT[:, g, :], True, True,
                   [vtok_b[tc], WsT_b], [pmb])
            t = ring("tmpf", 3)
            S.add("dve", lambda e, t=t, pm=pm, c=c: e.scalar_tensor_tensor(
                out=tmpf[:, t, :].rearrange("p (a b) -> p a b", b=P),
                in0=pm[:, :].rearrange("p (a b) -> p a b", b=P), scalar=vcol(V_SLNG + c),
                in1=B2[:, c, :].unsqueeze(1).broadcast_to([P, 4, P]), op0=ALU.mult, op1=ALU.add),
                reads=[pmb, vecs_b, B2_b], writes=[tmpf_b[t]])
            S.add("dve", lambda e, t=t, c=c: e.tensor_tensor(
                out=hid[:, 16 + c, :], in0=tmpf[:, t, :], in1=hid[:, c, :], op=ALU.mult),
                reads=[tmpf_b[t], hid_b[c]], writes=[hid_b[16 + c]])
        S.label = "mix_conv"
        builds = conv_builds(4, DC, 4)
        ready = []
        for _ in range(7):
            ready.append(next(builds))
        for c in range(4, DC):
            pc, pcb = ps_next()
            for k in range(CW):
                nb = next(builds, None)
                if nb is not None:
                    ready.append(nb)
                r = ready.pop(0)
                mm(pc[:, :], dg[:, r, :], aT[:, c, 2 + k:2 + k + TT], k == 0, k == CW - 1,
                   [dg_b[r], aT_b[c]], [pcb])
            S.add("act", lambda e, pc=pc, c=c: e.activation(acc[:, c, :], pc[:, :], AF.Identity,
                                                            bias=vcol(V_CONVB + c)),
                  reads=[pcb, vecs_b], writes=[acc_b[c]])
            i1 = ring("sq", 4)
            S.add("act", lambda e, i=i1, pc=pc, c=c: e.activation(sq[:, i, :], pc[:, :], AF.Identity,
                                                                  bias=vcol(V_CONVB + c)),
                  reads=[pcb, vecs_b], writes=[sq_b[i1]])
            i2 = ring("sq", 4)
            S.add("act", lambda e, i=i2, pc=pc, c=c: e.activation(sq[:, i, :], pc[:, :], AF.Square,
                                                                  bias=vcol(V_CONVB + c)),
                  reads=[pcb, vecs_b], writes=[sq_b[i2]])
            cstat_pending.append((i1, i2, c))
            if len(cstat_pending) > 1:
                cstat_flush_one()
        while cstat_pending:
            cstat_flush_one()
        S.label = "mix_convln"
        S.add("dve", lambda e: e.tensor_scalar(lnst[:, 0, :], ps_s[:, :], 1.0 / D, None, ALU.mult),
              reads=[ps_sb], writes=[lnst_b[0]])
        S.add("dve", lambda e: e.tensor_tensor(out=lnst[:, 1, :], in0=lnst[:, 0, :], in1=lnst[:, 0, :],
                                               op=ALU.mult), reads=[lnst_b[0]], writes=[lnst_b[1]])
        S.add("dve", lambda e: e.scalar_tensor_tensor(
            out=lnst[:, 1, :], in0=ps_q[:, :], scalar=1.0 / D, in1=lnst[:, 1, :],
            op0=ALU.mult, op1=ALU.subtract), reads=[ps_qb, lnst_b[1]], writes=[lnst_b[1]])
        tq = ring("tmpf", 3)
        S.add("act", lambda e: e.activation(tmpf[:, tq, :], lnst[:, 1, :], AF.Sqrt, bias=eps_rms[:, 1:2]),
              reads=[lnst_b[1], eps_b], writes=[tmpf_b[tq]])
        S.add("dve", lambda e: e.reciprocal(lnst[:, 1, :], tmpf[:, tq, :]), reads=[tmpf_b[tq]], writes=[lnst_b[1]])
        S.add("dve", lambda e: e.tensor_tensor(out=lnst[:, 0, :], in0=lnst[:, 0, :], in1=lnst[:, 1, :],
                                               op=ALU.mult), reads=[lnst_b[0], lnst_b[1]], writes=[lnst_b[0]])
        for c in range(DC):
            S.add("pool", lambda e, c=c: e.tensor_tensor(out=acc[:, c, :], in0=acc[:, c, :], in1=lnst[:, 1, :],
                                                         op=ALU.mult),
                  reads=[acc_b[c], lnst_b[1]], writes=[acc_b[c]])

        def convln_chunk(c):
            S.add("dve", lambda e, c=c: e.tensor_tensor(out=acc[:, c, :], in0=acc[:, c, :], in1=lnst[:, 0, :],
                                                        op=ALU.subtract),
                  reads=[acc_b[c], lnst_b[0]], writes=[acc_b[c]])
            S.add("act", lambda e, c=c: e.activation(hid[:, 8 + c, :], acc[:, c, :], AF.Silu,
                                                     bias=vcol(V_CLNB + c), scale=vcol(V_CLNG + c)),
                  reads=[acc_b[c], vecs_b], writes=[hid_b[8 + c]])
        S.label = "mix_wb"
        for p in range(2):
            snb, sbb = W.next(("cols", "wb", p * 512, 512))
            sng, sg = W.next(("cols", "win", 5120 + p * 512, 512))
            wvb, wvg = slot_cols(sbb, 512), slot_cols(sg, 512)
            pgs = []
            for cc in range(4):
                pg, pgb = ps_next()
                for k in range(DC):
                    mm(pg[:, :], wvg[:, k, cc * P:(cc + 1) * P], xn[:, k, :],
                       k == 0, k == DC - 1, [slot_buf[sg], xn_b[k]], [pgb])
                pgs.append((pg, pgb))
            for cc in range(4):
                c = 4 * p + cc
                pg, pgb = pgs[cc]
                py, pyb = ps_next()
                for k in range(DC):
                    mm(py[:, :], wvb[:, k, cc * P:(cc + 1) * P], hid[:, 16 + k, :],
                       k == 0, k == DC - 1, [slot_buf[sbb], hid_b[16 + k]], [pyb])
                S.label = "mix_convln"
                convln_chunk(c)
                S.label = "mix_wb"
                t = ring("tmpf", 3)
                S.add("act", lambda e, t=t, pg=pg, c=c: e.activation(
                    tmpf[:, t, :], pg[:, :], AF.Tanh, bias=bhalf[:, 8 + c:9 + c], scale=0.5),
                    reads=[pgb, bhalf_b], writes=[tmpf_b[t]])
                S.add("dve", lambda e, t=t, py=py, c=c: e.scalar_tensor_tensor(
                    out=t2v[:, c, :], in0=tmpf[:, t, :], scalar=1.0, in1=py[:, :],
                    op0=ALU.add, op1=ALU.mult),
                    reads=[pyb, tmpf_b[t]], writes=[gv_b[c // 2]])
            W.done(snb)
            W.done(sng)
        S.label = "mix_wa"
        for p in range(2):
            sna, sa = W.next(("cols", "wa", p * 512, 512))
            sng, sg = W.next(("cols", "win", 4096 + p * 512, 512))
            wva, wvg = slot_cols(sa, 512), slot_cols(sg, 512)
            for cc in range(4):
                c = 4 * p + cc
                py, pyb = ps_next()
                pg, pgb = ps_next()
                for k in range(DC):
                    mm(py[:, :], wva[:, k, cc * P:(cc + 1) * P], hid[:, 8 + k, :],
                       k == 0, k == DC - 1, [slot_buf[sa], hid_b[8 + k]], [pyb])
                for k in range(DC):
                    mm(pg[:, :], wvg[:, k, cc * P:(cc + 1) * P], xn[:, k, :],
                       k == 0, k == DC - 1, [slot_buf[sg], xn_b[k]], [pgb])
                t = ring("tmpf", 3)
                S.add("act", lambda e, t=t, pg=pg, c=c: e.activation(
                    tmpf[:, t, :], pg[:, :], AF.Tanh, bias=bhalf[:, c:c + 1], scale=0.5),
                    reads=[pgb, bhalf_b], writes=[tmpf_b[t]])
                S.add("dve", lambda e, t=t, py=py: e.scalar_tensor_tensor(
                    out=tmpf[:, t, :], in0=tmpf[:, t, :], scalar=1.0, in1=py[:, :],
                    op0=ALU.add, op1=ALU.mult),
                    reads=[pyb, tmpf_b[t]], writes=[tmpf_b[t]])
                S.add("pool", lambda e, t=t, c=c: e.tensor_tensor(
                    out=hid[:, c, :], in0=tmpf[:, t, :], in1=t2v[:, c, :], op=ALU.add),
                    reads=[tmpf_b[t], gv_b[c // 2]], writes=[hid_b[c]])
            W.done(sna)
            W.done(sng)
        S.label = "mix_wout"
        proj_residual("wout", 0, scale=0.5)
        for c in range(DC):
            S.add("pool", lambda e, c=c: e.tensor_copy(aT[:, c, 0:HALO], aT[:, c, TT:TT + HALO]),
                  reads=[aT_b[c]], writes=[aT_b[c]])

    def proj_residual(key, src0, scale=1.0):
        for p in range(2):
            sn, s = W.next(("cols", key, p * 512, 512))
            wv = slot_cols(s, 512)
            for cc in range(4):
                c = 4 * p + cc
                po, pob = ps_next()
                for k in range(DC):
                    mm(po[:, :], wv[:, k, cc * P:(cc + 1) * P], hid[:, src0 + k, :],
                       k == 0, k == DC - 1, [slot_buf[s], hid_b[src0 + k]], [pob])
                S.add("dve", lambda e, c=c, po=po: e.scalar_tensor_tensor(
                    out=hT[:, c, :], in0=po[:, :], scalar=scale, in1=hT[:, c, :],
                    op0=ALU.mult, op1=ALU.add),
                    reads=[pob, hT_b[c]], writes=[hT_b[c]])
                stat_add(hT, hT_b, c, TT, delay=2)
            W.done(sn)
        stat_flush()

    def xattn():
        n = TT
        S.label = "xat_norm"
        rmsnorm(hT, hT_b, V_G_XAT, n, xn, xn_b, pre=True)
        S.label = "xat_q"
        for p in range(2):
            sn, s = W.next(("cols", "wq", p * 512, 512))
            wv = slot_cols(s, 512)
            for cc in range(4):
                c = 4 * p + cc
                pq, pqb = ps_next()
                for k in range(DC):
                    mm(pq[:, :], wv[:, k, cc * P:(cc + 1) * P], xn[:, k, :],
                       k == 0, k == DC - 1, [slot_buf[s], xn_b[k]], [pqb])
                S.add("act", lambda e, c=c, pq=pq: e.activation(hid[:, 8 + c, :], pq[:, :], AF.Copy,
                                                                scale=1.0 / 16.0),
                      reads=[pqb], writes=[hid_b[8 + c]])
            W.done(sn)
        S.label = "xat_attn"

        def head_scores(h):
            for mc in range(2):
                psc, pscb = ps_next()
                for dc in range(2):
                    mm(psc[:, :], KT[:, 2 * h + dc, mc * P:(mc + 1) * P], hid[:, 8 + 2 * h + dc, :],
                       dc == 0, dc == 1, [KT_b, hid_b[8 + 2 * h + dc]], [pscb])
                S.add("act", lambda e, psc=psc, h=h, mc=mc: e.activation(
                    hid[:, 16 + 2 * h + mc, :], psc[:, :], AF.Exp),
                    reads=[pscb], writes=[hid_b[16 + 2 * h + mc]])

        def head_rest(h):
            pss, pssb = ps_next()
            for mc in range(2):
                mm(pss[:, :], ones_bf[:, :], hid[:, 16 + 2 * h + mc, :], mc == 0, mc == 1,
                   [ones_b, hid_b[16 + 2 * h + mc]], [pssb])
            i = ring("rstd", 2)
            S.add("dve", lambda e, i=i, pss=pss: e.reciprocal(rstd[:, i, :], pss[:, :]),
                  reads=[pssb], writes=[rstd_b[i]])
            for dc in range(2):
                po, pob = ps_next()
                for mc in range(2):
                    mm(po[:, :], Vt[:, mc, (2 * h + dc) * P:(2 * h + dc + 1) * P], hid[:, 16 + 2 * h + mc, :],
                       mc == 0, mc == 1, [Vt_b, hid_b[16 + 2 * h + mc]], [pob])
                S.add("dve", lambda e, i=i, po=po, h=h, dc=dc: e.tensor_tensor(
                    out=hid[:, 2 * h + dc, :], in0=po[:, :], in1=rstd[:, i, :], op=ALU.mult),
                    reads=[pob, rstd_b[i]], writes=[hid_b[2 * h + dc]])

        head_scores(0)
        for h in range(4):
            if h + 1 < 4:
                head_scores(h + 1)
            head_rest(h)
        S.label = "xat_wo"
        proj_residual("wo", 0)

    full = stages in ("all", "mixer", "xattn")
    if full:
        setup_consts()
    if stages in ("all", "xattn"):
        kv_prep()
    fuse_halo = do_halo and full
    if fuse_halo:
        load_x_tile(xh_d, 0, HALO, hTh, hTh_b)
    def tile_head(it):
        load_x_tile(x_d, it * TT, TT)
        S.label = "ffn_norm"
        rmsnorm(hT, hT_b, V_G_FFN1, TT, xn, xn_b)

    for it in range(NT):
        if stages != "all":
            load_x_tile(x_d, it * TT, TT)
        if stages == "xpose":
            store_tile(hT, hT_b, it * TT)
            continue
        if stages == "norm":
            rmsnorm(hT, hT_b, V_G_FFN1, TT, hT, hT_b)
            store_tile(hT, hT_b, it * TT)
            continue
        if stages != "all":
            ffn("1", V_G_FFN1, TT, halo=(fuse_halo and it == 0))
        else:
            if it == 0:
                tile_head(0)
            ffn("1", V_G_FFN1, TT, pre="done", halo=(fuse_halo and it == 0))
        if stages == "ffn1":
            store_tile(hT, hT_b, it * TT)
            continue
        mixer(halo=(fuse_halo and it == 0))
        if stages == "mixer":
            store_tile(hT, hT_b, it * TT)
            continue
        xattn()
        if stages == "xattn":
            store_tile(hT, hT_b, it * TT)
            continue
        ffn("2", V_G_FFN2, TT, pre=True)
        S.label = "final_norm"
        rmsnorm(hT, hT_b, V_G_FIN, TT, acc, acc_b, pre=True)
        if it + 1 < NT:
            tile_head(it + 1)
        store_tile(acc, acc_b, it * TT)

    S.add("sp", lambda e: e.nop(), reads=ystore_b)

    with nc.Block() as block:
        S.emit(nc, block, esem)
    es.close()
    nc._pe_labels = S.labels["pe"]
    return nc


def _pack_inputs(inputs, NT):
    f = lambda a: np.ascontiguousarray(np.asarray(a, dtype=np.float32))
    x = f(inputs["x"])
    mem = f(inputs["mem"])

    def fm(v):
        return f(v).reshape(-1, P).T

    vecs = np.zeros((P, NV), np.float32)
    vecs[:, V_G_FFN1:V_G_FFN1 + 8] = fm(inputs["ffn1_norm"][0])
    vecs[:, V_G_MIX:V_G_MIX + 8] = fm(inputs["mix_norm"][0])
    vecs[:, V_G_XAT:V_G_XAT + 8] = fm(inputs["xattn_norm"][0])
    vecs[:, V_G_MEM:V_G_MEM + 8] = fm(inputs["mem_norm"][0])
    vecs[:, V_G_FFN2:V_G_FFN2 + 8] = fm(inputs["ffn2_norm"][0])
    vecs[:, V_G_FIN:V_G_FIN + 8] = fm(inputs["final_norm"])
    vecs[:, V_BIN:V_BIN + 48] = fm(inputs["b_in"][0])
    vecs[:, V_CONVB:V_CONVB + 8] = fm(inputs["conv_b"][0])
    vecs[:, V_CLNG:V_CLNG + 8] = fm(inputs["conv_ln_g"][0])
    vecs[:, V_CLNB:V_CLNB + 8] = fm(inputs["conv_ln_b"][0])
    cw = f(inputs["conv_w"][0])
    vecs[:, V_CONVW:V_CONVW + CW * 8] = cw.reshape(CW, 8, P).transpose(2, 0, 1).reshape(P, CW * 8)
    vecs[:, V_SLNG:V_SLNG + 8] = fm(inputs["sgu_ln_g"][0])
    vecs[:, V_SLNB:V_SLNB + 8] = fm(inputs["sgu_ln_b"][0])
    rows = np.concatenate([f(inputs["b_in"][0])[3072:4096], f(inputs["sgu_b"][0]).reshape(-1)])[None, :]
    bc = np.broadcast_to(f(inputs["sgu_b"][0]).reshape(1, -1), (P, 512))
    shared = {
        "vecs": vecs, "rows": np.ascontiguousarray(rows), "bc": np.ascontiguousarray(bc),
        "ident": np.eye(P, dtype=np.float32),
        "tril": np.tril(np.ones((P, P), np.float32)),
        "sgu_w": f(inputs["sgu_w"][0]),
        "ffn1_w_gu": f(inputs["ffn1_w_gu"][0]), "ffn1_w_down": f(inputs["ffn1_w_down"][0]),
        "w_in": f(inputs["w_in"][0]), "w_a_out": f(inputs["w_a_out"][0]),
        "w_b_out": f(inputs["w_b_out"][0]), "w_out": f(inputs["w_out"][0]),
        "w_q": f(inputs["w_q"][0]), "w_kv": f(inputs["w_kv"][0]), "w_o": f(inputs["w_o"][0]),
        "ffn2_w_gu": f(inputs["ffn2_w_gu"][0]), "ffn2_w_down": f(inputs["ffn2_w_down"][0]),
    }
    per = NT * TT
    in_maps = []
    for i in range(NCORES):
        b, q = i // 4, i % 4
        t0 = q * per
        m = dict(shared)
        m["x"] = np.ascontiguousarray(x[b, t0:t0 + per])
        if q == 0:
            m["xh"] = np.zeros((HALO, D), np.float32)
            m["hmask"] = np.zeros((P, 1), np.float32)
        else:
            m["xh"] = np.ascontiguousarray(x[b, t0 - HALO:t0])
            m["hmask"] = np.ones((P, 1), np.float32)
        m["mem"] = np.ascontiguousarray(mem[b])
        in_maps.append(m)
    return in_maps


_PROG_CACHE = {}


def _run(inputs, NT=8, stages="all", trace=False):
    key = (NT, stages)
    if key not in _PROG_CACHE:
        _PROG_CACHE[key] = build_program(NT, stages)
    nc = _PROG_CACHE[key]
    in_maps = _pack_inputs(inputs, NT)
    res = run_bass_kernel_spmd(nc, in_maps, core_ids=list(range(NCORES)), trace=trace)
    per = NT * TT
    out = np.zeros((2, 4 * per, D), np.float32)
    for i in range(NCORES):
        b, q = i // 4, i % 4
        out[b, q * per:(q + 1) * per] = res.results[i]["y"]
    return out, res


def kernel(**inputs):
    out, _ = _run(inputs, NT=SEQ // 4 // TT, stages="all")
    return out
```
